# Optimizing a Trainium2 kernel written in Bass

```python
import jax, jax.numpy as jnp
from jax import lax
import numpy as np

D_MODEL = 1024
BATCH = 2
SEQ = 8192
DEPTH = 1

HEAD_DIM = 64
GRID_W = 64
NA_HEADS = 8
NA_KH_MAX = 8
NA_KW = 16
NA_QCB = NA_KW
NA_SPAN = 2 * NA_KW
SW_HEADS = 8
SW_KV_HEADS = 2
SW_WINDOW = 128
SW_BLOCK = 128
T5_BUCKETS = 32
T5_MAX_DIST = 128
FFN_HIDDEN = -(-8 * D_MODEL // (3 * 256)) * 256

A_W = NA_HEADS * HEAD_DIM
B_QW = SW_HEADS * HEAD_DIM
B_KVW = SW_KV_HEADS * HEAD_DIM
IN_WIDTHS = [A_W, A_W, A_W, B_QW, B_KVW, B_KVW, D_MODEL, D_MODEL]
IN_WIDTH = sum(IN_WIDTHS)
IN_SPLITS = list(np.cumsum(IN_WIDTHS)[:-1])
RMS_EPS = 1e-6
NEG_INF = -1e30

kernel_name = "hybrid_natten_swa_gated_encoder_block"


def rms_norm(x, g):
    xf = x.astype(jnp.float32)
    y = xf * lax.rsqrt(jnp.mean(xf * xf, axis=-1, keepdims=True) + RMS_EPS)
    return y.astype(x.dtype) * g


def t5_bucket(rel):
    half = T5_BUCKETS // 2
    max_exact = half // 2
    ret = (rel > 0).astype(np.int32) * half
    n = np.abs(rel)
    large = max_exact + (np.log(np.maximum(n, 1) / max_exact)
                         / np.log(T5_MAX_DIST / max_exact) * (half - max_exact)).astype(np.int32)
    large = np.minimum(large, half - 1)
    return ret + np.where(n < max_exact, n, large)


def neighbourhood_attention(q, k, v, rpb):
    B, T, H, D = q.shape
    rows = T // GRID_W
    kh = min(NA_KH_MAX, rows)
    ncb = GRID_W // NA_QCB
    r = np.arange(rows)
    row_start = np.clip(r - kh // 2, 0, rows - kh)
    key_rows = row_start[:, None] + np.arange(kh)[None, :]
    j = np.arange(ncb)
    span_start = np.clip(j * NA_QCB - NA_KW // 2, 0, GRID_W - NA_SPAN)
    key_cols = span_start[:, None] + np.arange(NA_SPAN)[None, :]
    q_cols = j[:, None] * NA_QCB + np.arange(NA_QCB)[None, :]
    win_start = np.clip(q_cols - NA_KW // 2, 0, GRID_W - NA_KW)
    kc = key_cols[:, None, :]
    in_win = (kc >= win_start[..., None]) & (kc < win_start[..., None] + NA_KW)

    idx = (key_rows[:, None, :, None] * GRID_W + key_cols[None, :, None, :]).reshape(rows, ncb, kh * NA_SPAN)
    kg = jnp.take(k, idx, axis=1)
    vg = jnp.take(v, idx, axis=1)
    qb = q.reshape(B, rows, ncb, NA_QCB, H, D)
    s = jnp.einsum('brjqhd,brjkhd->brjhqk', qb, kg,
                   preferred_element_type=jnp.float32) * (HEAD_DIM ** -0.5)

    dr = key_rows - r[:, None] + (NA_KH_MAX - 1)
    dc = np.clip(kc - q_cols[..., None], -(NA_KW - 1), NA_KW - 1) + (NA_KW - 1)
    bias = rpb[dr[:, None, None, :, None], dc[None, :, :, None, :]]
    bias = jnp.transpose(bias, (0, 1, 5, 2, 3, 4)).astype(jnp.float32)
    bias = jnp.where(in_win[None, :, None, :, None, :], bias, NEG_INF)
    bias = bias.reshape(rows, ncb, H, NA_QCB, kh * NA_SPAN)
    p = jax.nn.softmax(s + bias[None], axis=-1)
    o = jnp.einsum('brjhqk,brjkhd->brjqhd', p.astype(v.dtype), vg)
    return o.reshape(B, T, H, D)


def sliding_window_attention(q, k, v, sink, t5_table):
    B, T, H, D = q.shape
    kvh = k.shape[2]
    g = H // kvh
    nb = T // SW_BLOCK
    pad = ((0, 0), (SW_BLOCK, SW_BLOCK), (0, 0), (0, 0))
    kp = jnp.pad(k, pad).reshape(B, nb + 2, SW_BLOCK, kvh, D)
    vp = jnp.pad(v, pad).reshape(B, nb + 2, SW_BLOCK, kvh, D)
    kw = jnp.concatenate([kp[:, :-2], kp[:, 1:-1], kp[:, 2:]], axis=2)
    vw = jnp.concatenate([vp[:, :-2], vp[:, 1:-1], vp[:, 2:]], axis=2)
    qb = q.reshape(B, nb, SW_BLOCK, kvh, g, D)
    s = jnp.einsum('bnqkgd,bnskd->bnkgqs', qb, kw,
                   preferred_element_type=jnp.float32) * (HEAD_DIM ** -0.5)

    rel = np.arange(3 * SW_BLOCK)[None, :] - SW_BLOCK - np.arange(SW_BLOCK)[:, None]
    bias = t5_table[t5_bucket(rel)]
    bias = jnp.transpose(bias, (2, 0, 1)).reshape(kvh, g, SW_BLOCK, 3 * SW_BLOCK).astype(jnp.float32)
    key_pos = np.arange(nb)[:, None] * SW_BLOCK + np.arange(3 * SW_BLOCK)[None, :] - SW_BLOCK
    valid = (np.abs(rel) <= SW_WINDOW)[None] & ((key_pos >= 0) & (key_pos < T))[:, None, :]
    s = jnp.where(valid[None, :, None, None], s + bias[None, None], NEG_INF)

    sk = sink.astype(jnp.float32).reshape(1, 1, kvh, g, 1, 1)
    m = jnp.maximum(jnp.max(s, axis=-1, keepdims=True), sk)
    e = jnp.exp(s - m)
    p = e / (jnp.sum(e, axis=-1, keepdims=True) + jnp.exp(sk - m))
    o = jnp.einsum('bnkgqs,bnskd->bnqkgd', p.astype(v.dtype), vw)
    return o.reshape(B, T, H, D)


def setup_inputs(seed: int = 0) -> dict:
    key = jax.random.key(seed)
    ks = jax.random.split(key, 18)
    f32 = jnp.float32

    def w(k, shape, fan_in):
        return jax.random.normal(k, shape, f32) * fan_in ** -0.5

    def gain(k, shape):
        return 1.0 + 0.05 * jax.random.normal(k, shape, f32)

    return {
        "x": jax.random.normal(ks[0], (BATCH, SEQ, D_MODEL), f32),
        "norm_mix": gain(ks[1], (DEPTH, D_MODEL)),
        "w_in": w(ks[2], (DEPTH, D_MODEL, IN_WIDTH), D_MODEL),
        "q_norm_a": gain(ks[3], (DEPTH, HEAD_DIM)),
        "k_norm_a": gain(ks[4], (DEPTH, HEAD_DIM)),
        "rpb_a": 0.1 * jax.random.normal(ks[5], (DEPTH, 2 * NA_KH_MAX - 1, 2 * NA_KW - 1, NA_HEADS), f32),
        "q_norm_b": gain(ks[6], (DEPTH, HEAD_DIM)),
        "k_norm_b": gain(ks[7], (DEPTH, HEAD_DIM)),
        "sink_b": 1.0 + 0.5 * jax.random.normal(ks[8], (DEPTH, SW_HEADS), f32),
        "t5_table": 0.1 * jax.random.normal(ks[9], (T5_BUCKETS, SW_HEADS), f32),
        "w_branch_a": w(ks[10], (DEPTH, A_W, D_MODEL), A_W),
        "w_branch_b": w(ks[11], (DEPTH, B_QW, D_MODEL), B_QW),
        "w_out": w(ks[12], (DEPTH, D_MODEL, D_MODEL), D_MODEL),
        "norm_ffn": gain(ks[13], (DEPTH, D_MODEL)),
        "w_gate": w(ks[14], (DEPTH, D_MODEL, FFN_HIDDEN), D_MODEL),
        "w_up": w(ks[15], (DEPTH, D_MODEL, FFN_HIDDEN), D_MODEL),
        "w_down": w(ks[16], (DEPTH, FFN_HIDDEN, D_MODEL), FFN_HIDDEN),
    }


def reference(x, norm_mix, w_in, q_norm_a, k_norm_a, rpb_a, q_norm_b, k_norm_b, sink_b,
              t5_table, w_branch_a, w_branch_b, w_out, norm_ffn, w_gate, w_up, w_down):
    B, T, _ = x.shape
    for l in range(DEPTH):
        h = rms_norm(x, norm_mix[l])
        proj = jnp.einsum('btd,de->bte', h, w_in[l])
        q_a, k_a, v_a, q_b, k_b, v_b, g_a, g_b = jnp.split(proj, IN_SPLITS, axis=-1)

        q_a = rms_norm(q_a.reshape(B, T, NA_HEADS, HEAD_DIM), q_norm_a[l])
        k_a = rms_norm(k_a.reshape(B, T, NA_HEADS, HEAD_DIM), k_norm_a[l])
        v_a = v_a.reshape(B, T, NA_HEADS, HEAD_DIM)
        o_a = neighbourhood_attention(q_a, k_a, v_a, rpb_a[l]).reshape(B, T, A_W)

        q_b = rms_norm(q_b.reshape(B, T, SW_HEADS, HEAD_DIM), q_norm_b[l])
        k_b = rms_norm(k_b.reshape(B, T, SW_KV_HEADS, HEAD_DIM), k_norm_b[l])
        v_b = v_b.reshape(B, T, SW_KV_HEADS, HEAD_DIM)
        o_b = sliding_window_attention(q_b, k_b, v_b, sink_b[l], t5_table).reshape(B, T, B_QW)

        y = (jax.nn.sigmoid(g_a) * jnp.einsum('bte,ed->btd', o_a, w_branch_a[l])
             + jax.nn.sigmoid(g_b) * jnp.einsum('bte,ed->btd', o_b, w_branch_b[l]))
        x = x + jnp.einsum('btd,de->bte', y, w_out[l])

        h = rms_norm(x, norm_ffn[l])
        u = jax.nn.silu(jnp.einsum('btd,df->btf', h, w_gate[l])) * jnp.einsum('btd,df->btf', h, w_up[l])
        x = x + jnp.einsum('btf,fd->btd', u, w_down[l])
    return x
```

```python
import numpy as np
import concourse.bass as bass
import concourse.mybir as mybir
from concourse.bass_utils import run_bass_kernel_spmd

F32 = mybir.dt.float32
BF16 = mybir.dt.bfloat16
AF = mybir.ActivationFunctionType
ALU = mybir.AluOpType
AX = mybir.AxisListType

NCORES = 8
D = 1024
T = 8192
TOK = 2048
NT = 16
NE = 20
FF = 2816
NFC = 22
NEG = -30000.0
EPS = 1e-6

MYBASE = 17920
SB_TOP = 229376
TOTAL = SB_TOP - MYBASE


def _t5_bucket(rel):
    half = 16
    max_exact = 8
    ret = (rel > 0).astype(np.int32) * half
    n = np.abs(rel)
    large = max_exact + (np.log(np.maximum(n, 1) / max_exact)
                         / np.log(128 / max_exact) * (half - max_exact)).astype(np.int32)
    large = np.minimum(large, half - 1)
    return ret + np.where(n < max_exact, n, large)


_HPERM_A = [0, 2, 4, 6, 1, 3, 5, 7]


def _cbA(h):
    return (h % 2) * 4 + h // 2


def tabA_index(t, j):
    if t == 0:
        return {0: 5, 1: 6, 4: 7}.get(j, j)
    if t == 1:
        return {0: 8, 4: 9}.get(j, j)
    if t == 14:
        return {0: 10, 4: 11}.get(j, j)
    if t == 15:
        return {0: 12, 3: 13, 4: 14}.get(j, j)
    return j


def tabB_index(t, j):
    if t == 0 and j == 0:
        return 3
    if t == 15 and j == 2:
        return 4
    return j


def _ext_rows(s, e):
    if s == 0 and e == 0:
        return 6
    if s == 0 and e == 1:
        return None
    if s == 3 and e == 18:
        return None
    if s == 3 and e == 19:
        return 120
    return 32 * s - 4 + 2 * e


def _build_tabA(rpb, s):
    out = np.full((15, 128, 8, 128), NEG, dtype=np.float32)
    kl = np.arange(128)
    krl, kc = kl // 64, kl % 64
    ql = np.arange(128)
    qrl, qc = ql // 64, ql % 64
    ws = np.clip(qc - 8, 0, 48)
    colok = (kc[:, None] >= ws[None, :]) & (kc[:, None] < ws[None, :] + 16)
    dc = np.clip(kc[:, None] - qc[None, :], -15, 15) + 15
    reps = {}
    for t in range(16):
        for j in range(5):
            idx = tabA_index(t, j)
            if idx >= 5 or (t == 5):
                reps[idx] = (t, j)
    for idx, (t, j) in reps.items():
        k0 = _ext_rows(s, t + j)
        if k0 is None:
            continue
        q0 = 32 * s + 2 * t
        kr = k0 + krl
        qr = q0 + qrl
        rs = np.clip(qr - 4, 0, 120)
        rowok = (kr[:, None] >= rs[None, :]) & (kr[:, None] < rs[None, :] + 8)
        dr = kr[:, None] - qr[None, :] + 7
        ok = rowok & colok
        drc = np.clip(dr, 0, 14)
        vals = rpb[drc, dc, :]
        vals = np.where(ok[:, :, None], vals, np.float32(NEG))
        out[idx] = np.transpose(vals, (0, 2, 1))[:, _HPERM_A, :]
    return np.ascontiguousarray(np.transpose(out, (1, 0, 2, 3)).reshape(128, 15 * 1024))


def _build_tabB(t5, s):
    out = np.full((5, 128, 8, 128), NEG, dtype=np.float32)
    k = np.arange(128)[:, None]
    q = np.arange(128)[None, :]
    for jb in range(3):
        rel = (jb - 1) * 128 + k - q
        ok = np.abs(rel) <= 128
        vals = t5[_t5_bucket(rel), :]
        vals = np.where(ok[:, :, None], vals, np.float32(NEG))
        out[jb] = np.transpose(vals, (0, 2, 1))
    if s != 0:
        out[3] = out[0]
    if s != 3:
        out[4] = out[2]
    return np.ascontiguousarray(np.transpose(out, (1, 0, 2, 3)).reshape(128, 5 * 1024))


class _Ev:
    def __init__(self, nc, name):
        self.sem = nc.alloc_semaphore(name)
        self.n = 0

    def inc(self, ins, dma=False):
        k = 16 if dma else 1
        ins.then_inc(self.sem, k)
        self.n += k
        return (self, self.n)


def build_program():
    nc = bass.Bass("TRN2", target_bir_lowering=False)
    PE, DVE, ACT, POOL, SP = nc.tensor, nc.vector, nc.scalar, nc.gpsimd, nc.sync
    eng = {"pe": PE, "dve": DVE, "act": ACT, "pool": POOL, "sp": SP}
    waited = {}

    def W(e, tok):
        if tok is None:
            return
        if isinstance(tok, list):
            for x in tok:
                W(e, x)
            return
        ev, val = tok
        key = (e, id(ev))
        if waited.get(key, 0) >= val:
            return
        waited[key] = val
        eng[e].wait_ge(ev.sem, val)

    evs = {}

    def EV(name):
        if name not in evs:
            evs[name] = _Ev(nc, name)
        return evs[name]

    def din(name, shape, dt=F32):
        return nc.dram_tensor(name, list(shape), dt, kind="ExternalInput").ap()

    xe = din("xe", [NE * 128, D])
    xeT = din("xeT", [NE * 128, D])
    gmixT = din("gmixT", [128, 8])
    w_in = din("w_in", [D, 4352])
    gmix = din("gmix", [1, D])
    gffn = din("gffn", [1, D])
    qna = din("qna", [64, 1])
    kna = din("kna", [64, 1])
    qnb = din("qnb", [64, 1])
    knb = din("knb", [64, 1])
    sink = din("sink", [1, 8])
    tabA = din("tabA", [128, 15 * 1024])
    tabB = din("tabB", [128, 5 * 1024])
    w_ba = din("w_ba", [512, D])
    w_bb = din("w_bb", [512, D])
    w_out = din("w_out", [D, D])
    w_gate = din("w_gate", [D, FF])
    w_up = din("w_up", [D, FF])
    w_down = din("w_down", [FF, D])
    out = nc.dram_tensor("out", [TOK, D], F32, kind="ExternalOutput").ap()
    gts = nc.dram_tensor("gts", [NT, 128, 2048], BF16).ap()

    def at(name, shape, dt, rel):
        nbytes = int(np.prod(shape[1:])) * (4 if dt == F32 else 2)
        assert rel % 32 == 0, (name, rel)
        assert rel + nbytes <= TOTAL, (name, rel, nbytes, TOTAL)
        return nc.alloc_sbuf_tensor_at(name, list(shape), dt, offset=MYBASE + rel)

    QAT = at("QAT", [128, 4, TOK], BF16, 0)
    KAT = at("KAT", [128, 4, NE * 128], BF16, 16384)
    VA = at("VA", [128, NE, 8, 65], BF16, 36864)
    QBT = at("QBT", [128, 4, TOK], BF16, 57696)
    KBT = at("KBT", [128, 18 * 128], BF16, 74080)
    VB = at("VB", [128, 18, 2, 65], BF16, 78688)
    QKV_END = 83392
    Win = at("Win", [128, 8, 4352], BF16, QKV_END)
    W1 = QKV_END + 69632
    CB = TOTAL - 5152
    gvec = at("gvec", [128, D], F32, CB)
    ident = at("ident", [128, 128], BF16, CB + 4096)
    SM = CB + 4096 + 256
    qsA = at("qsA", [128, 1], F32, SM)
    qsB = at("qsB", [128, 1], F32, SM + 32)
    gtmp = at("gtmp", [128, 4], F32, SM + 64)
    sinkexp = at("sinkexp", [128, 8], F32, SM + 96)
    mhalf = at("mhalf", [128, 32], F32, SM + 128)
    ssqx = at("ssqx", [128, 20], F32, SM + 256)
    rsx = at("rsx", [128, 20], F32, SM + 352)
    ssq2 = at("ssq2", [128, 16], F32, SM + 448)
    rs2 = at("rs2", [128, 16], F32, SM + 512)
    rdenA = at("rdenA", [128, 8], F32, SM + 576)
    rdenB = at("rdenB", [128, 8], F32, SM + 608)
    epsq = at("epsq", [128, 20], F32, SM + 640)
    gcolT = at("gcolT", [128, 8], F32, SM + 736)
    assert SM + 768 <= TOTAL

    o = W1
    xt = [at("xt%d" % i, [128, D], F32, o + 4096 * i) for i in range(2)]; o += 8192
    xT = [at("xT%d" % i, [128, 8, 128], F32, o + 4096 * i) for i in range(2)]
    hb = [at("hb%d" % i, [128, D], BF16, o + 4096 + 2048 * i) for i in range(2)]; o += 8192
    raw = [at("raw%d" % i, [128, 1664], F32, o + 6656 * i) for i in range(2)]; o += 13312
    junk = at("junk", [128, D], BF16, o); o += 2048
    hT = [at("hT%d" % i, [128, D], BF16, o + 2048 * i) for i in range(2)]; o += 4096
    sq = [at("sq%d" % i, [128, 512], F32, o + 2048 * i) for i in range(2)]; o += 4096
    qn = [at("qn%d" % i, [128, 1664], BF16, o + 3328 * i) for i in range(2)]; o += 6656
    gsb = [at("gsb%d" % i, [128, 512], BF16, o + 1024 * i) for i in range(2)]; o += 2048
    ssq = [at("ssq%d" % i, [128, 32], F32, o + 128 * i) for i in range(2)]; o += 256
    rstd = [at("rstd%d" % i, [128, 32], F32, o + 128 * i) for i in range(2)]; o += 256
    identf = at("identf", [128, 128], F32, o); o += 512
    assert o <= CB

    tabAsb = at("tabAsb", [128, 15, 1024], BF16, QKV_END)
    tabM = [at("tabM%d" % k, [128, 1024], BF16, QKV_END + (5 + k // 2) * 8704 + (k % 2) * 2048) for k in range(5)]
    tabBsb = at("tabBsb", [128, 5, 1024], BF16, QKV_END + 30720)
    OTOK = QKV_END + 40960
    otok = at("otok", [128, NT, 1024], BF16, OTOK)
    WB = OTOK + 32768
    Wba = at("Wba", [128, 4, D], BF16, WB)
    Wbb = at("Wbb", [128, 4, D], BF16, WB + 8192)
    Wout = at("Wout", [128, 8, D], BF16, WB + 16384)
    W2 = WB + 32768
    PR = [at("PR%d" % i, [128, 1024], BF16, W2 + 2048 * i) for i in range(3)]
    PT = [at("PT%d" % i, [128, 1024], BF16, W2 + 8192 + 2048 * i) for i in range(3)]
    assert W2 + 8192 + 6144 <= CB

    h2T = at("h2T", [128, 8, TOK], BF16, 0)
    Wg = at("Wg", [128, 8, FF], BF16, 32768)
    Wu1 = at("Wu1", [128, 8, 1408], BF16, 77824)
    Wu2 = at("Wu2", [128, 8, 1408], BF16, 100352)
    WU_SPLIT = 1408
    X2B = 77824 + 8 * WU_SPLIT * 2
    o = X2B
    xt2 = [at("xt2_%d" % i, [128, D], F32, o + 4096 * i) for i in range(2)]; o += 8192
    ytok = [at("ytok%d" % i, [128, D], BF16, o + 2048 * i) for i in range(2)]; o += 4096
    h2b = [at("h2b%d" % i, [128, D], BF16, o + 2048 * i) for i in range(2)]; o += 4096
    oTb = [at("oTb%d" % i, [128, D], BF16, o + 2048 * i) for i in range(2)]; o += 4096
    junk2 = at("junk2", [128, D], BF16, o); o += 2048
    assert o <= 122880
    o = W2
    yTb = [at("yTb%d" % i, [128, D], BF16, o + 2048 * i) for i in range(2)]; o += 4096
    usb = [at("usb%d" % i, [128, 512], F32, o + 2048 * i) for i in range(2)]; o += 4096
    gsh = [at("gsh%d" % i, [128, 2, 512], BF16, o + 2048 * i) for i in range(2)]; o += 4096
    assert o <= CB
    Wd = at("Wd", [128, NFC, D], BF16, 122880)
    uT = at("uT", [128, NFC, 512], BF16, 167936)
    o = 190464
    x2t = [at("x2t%d" % i, [128, D], F32, o + 4096 * i) for i in range(2)]; o += 8192
    sgb = [at("sgb%d" % i, [128, 512], F32, o + 2048 * i) for i in range(2)]; o += 4096
    assert o <= CB + 4096

    ps = nc.alloc_psum_tensor("ps", [128, 4096], F32)

    def bank(k, n=1):
        return ps[:, 512 * k:512 * (k + n)]

    def bankbf(k, n=1):
        return ps[:, 512 * k:512 * (k + n)].bitcast(BF16)

    ev_setp = EV("setp")
    ev_setv = EV("setv")
    ev_setd = EV("setd")
    t = ev_setp.inc(POOL.memset(identf[:], 0.0))
    W("pool", t)
    t = ev_setp.inc(POOL.affine_select(out=identf[:], in_=identf[:], pattern=[[-1, 128]],
                                      compare_op=ALU.not_equal, fill=1.0, base=0, channel_multiplier=1))
    W("dve", t)
    tok_ident = ev_setv.inc(DVE.tensor_copy(out=ident[:], in_=identf[:]))
    tok_mhalf = ev_setp.inc(POOL.memset(mhalf[:], -0.5))
    ev_gv = EV("gv")
    tok_gcol = EV("gcol").inc(SP.dma_start(out=gcolT[:], in_=gmixT), dma=True)
    tok_gvec = ev_gv.inc(SP.dma_start(out=gvec[:], in_=gmix.partition_broadcast(128)), dma=True)
    DVE.memset(ssq[0][:], 1.0)
    DVE.memset(ssq[1][:], 1.0)
    late = {}

    def late_setup():
        for k_, src in enumerate([qna, kna, qnb, knb]):
            ev_setd.inc(SP.dma_start(out=gtmp[0:64, k_:k_ + 1], in_=src), dma=True)
            ev_setd.inc(SP.dma_start(out=gtmp[64:128, k_:k_ + 1], in_=src), dma=True)
        late["sinkld"] = ev_setd.inc(SP.dma_start(out=sinkexp[:], in_=sink.partition_broadcast(128)), dma=True)
        W("dve", late["sinkld"])
        DVE.scalar_tensor_tensor(out=qsA[:], in0=gtmp[:, 0:1], scalar=0.125, in1=gtmp[:, 1:2],
                                 op0=ALU.mult, op1=ALU.mult)
        late["qs"] = ev_setv.inc(DVE.scalar_tensor_tensor(out=qsB[:], in0=gtmp[:, 2:3], scalar=0.125,
                                                          in1=gtmp[:, 3:4], op0=ALU.mult, op1=ALU.mult))

    col_groups = {
        "qA": (0, 512), "kA": (512, 1024), "vA": (1024, 1536), "qB": (1536, 2048), "kvB": (2048, 2304),
        "g0": (2304, 2816), "g1": (2816, 3328), "g2": (3328, 3840), "g3": (3840, 4352),
    }
    w_in_v = w_in.rearrange("(c p) n -> p c n", p=128)
    tokW = {}

    def issue_w(names):
        for gname in names:
            a, b = col_groups[gname]
            tokW[gname] = EV("w_" + gname).inc(
                POOL.dma_start(out=Win[:, :, a:b], in_=w_in_v[:, :, a:b]), dma=True)

    issue_w(["kA", "vA", "kvB"])

    DVE.memset(VA[:, :, :, 64:65], 1.0)
    tok_vones = ev_setv.inc(DVE.memset(VB[:, :, :, 64:65], 1.0))
    ev_x = EV("xld")
    ev_a2 = EV("a2")
    ev_p = EV("pool")
    ev_a4 = EV("a4")
    ev_tx = EV("tx")
    ev_a6 = EV("a6")
    ev_g = EV("grp")
    ev_sq = EV("sq")
    ev_red = EV("red")
    ev_qn = EV("qn")
    ev_vc = EV("vcopy")
    ev_sg = EV("sig")
    ev_gst = EV("gst")
    ev_ttr = EV("ttr")
    ev_evd = EV("evd")
    ev_eva = EV("eva")

    tok_x = {}
    tok_a4 = {}
    tok_tx = {}
    tok_a6 = {}
    xt_free = [None, None]
    hb_free = [None, None]
    hT_free = [None, None]
    qn_free = [None, None]
    sq_free = [None, None]
    gsb_free = [None, None]
    bank_free = [None] * 5
    st = {"psT1_free": None, "psT2_free": None, "n": 0, "m": 0, "gm": 0}
    tok_qn_last = {}
    tok_lastgrp = {}
    PS_T1 = 0
    PS_G = 1
    PS_T2 = 6

    def is_own(e):
        return 2 <= e <= 17

    ORDER = [0, 1, 18, 19] + list(range(2, 18))
    POS = {e_: i_ for i_, e_ in enumerate(ORDER)}
    raw_free = [None, None]
    tok_rstd = {}
    tok_stat = {}
    NGB = 5

    xT_free = [None, None]
    tok_xT = {}
    OLD = set(ORDER[0:8])

    def A_load(e):
        b = POS[e] % 2
        W("sp", xt_free[b])
        tok_x[e] = EV('xld%d' % b).inc(SP.dma_start(out=xt[b][:], in_=xe[e * 128:(e + 1) * 128, :]), dma=True)
        if e in OLD:
            return
        W("sp", xT_free[b])
        tok_xT[e] = EV('xTld%d' % b).inc(SP.dma_start(
            out=xT[b][:], in_=xeT[e * 128:(e + 1) * 128, :].rearrange("p (c t) -> p c t", c=8)), dma=True)

    def A_stat(e):
        b = POS[e] % 2
        W("act", tok_x[e])
        W("act", st.get("junk_tok"))
        t2 = ev_a2.inc(ACT.activation(out=junk[:], in_=xt[b][:], func=AF.Square, accum_out=ssqx[:, e:e + 1]))
        st["junk_tok"] = t2
        xt_free[b] = t2
        W("pool", t2)
        W("pool", tok_mhalf)
        t3 = ev_p.inc(POOL.tensor_scalar(out=rsx[:, e:e + 1], in0=ssqx[:, e:e + 1], scalar1=1.0 / D, scalar2=EPS,
                                         op0=ALU.mult, op1=ALU.add))
        ev_p.inc(POOL.tensor_scalar(out=epsq[:, e:e + 1], in0=ssqx[:, e:e + 1], scalar1=EPS / D, scalar2=EPS * EPS,
                                    op0=ALU.mult, op1=ALU.add))
        W("pool", t3)
        tok_stat[e] = ev_p.inc(POOL.tensor_tensor(out=rsx[:, e:e + 1], in0=rsx[:, e:e + 1], in1=mhalf[:, 0:1],
                                                  op=ALU.pow))

    def A_scale(e):
        b = POS[e] % 2
        if e in OLD:
            W("dve", tok_x[e])
            W("dve", hb_free[b])
            W("dve", tok_gvec)
            tok_a4[e] = ev_a4.inc(DVE.tensor_tensor(out=hb[b][:], in0=xt[b][:], in1=gvec[:], op=ALU.mult))
            xt_free[b] = [xt_free[b], tok_a4[e]]
            return
        W("dve", tok_xT[e])
        W("dve", hT_free[b])
        W("dve", tok_gcol)
        tok_a4[e] = ev_a4.inc(DVE.tensor_tensor(out=hT[b][:].rearrange("p (c t) -> p c t", c=8), in0=xT[b][:],
                                                in1=gcolT[:].unsqueeze(2).to_broadcast([128, 8, 128]),
                                                op=ALU.mult))
        xT_free[b] = tok_a4[e]
        tok_a6[e] = tok_a4[e]

    def A_tx_old(e):
        b = POS[e] % 2
        if True:
            W("pe", tok_a4[e])
            W("pe", st["psT1_free"])
            W("pe", tok_ident)
            pT = bankbf(PS_T1)
            for c in range(8):
                ins = PE.transpose(out=pT[:, c * 128:(c + 1) * 128], in_=hb[b][:, c * 128:(c + 1) * 128],
                                   identity=ident[:])
            tok_tx[e] = ev_tx.inc(ins)
            hb_free[b] = tok_tx[e]
            xT_free[1] = [xT_free[1], tok_tx[e]]
            W("act", tok_tx[e])
            W("act", hT_free[b])
            tok_a6[e] = ev_a6.inc(ACT.activation(out=hT[b][:], in_=pT, func=AF.Copy))
            st["psT1_free"] = tok_a6[e]

    NORM_OFF = {"qA": 0, "kA": 512, "qB": 1024, "kvB": 1536}
    NORM_C0 = {"qA": 0, "kA": 8, "qB": 16, "kvB": 24}

    early = {"tok": None}

    def emit_rstd(e, tred_last):
        par = POS[e] % 2
        W("pool", tred_last)
        W("pool", tok_stat[e])
        tp = ev_p.inc(POOL.tensor_scalar(out=rstd[par][:, 0:26], in0=ssq[par][:, 0:26],
                                         scalar1=1.0 / 64, scalar2=epsq[:, e:e + 1], op0=ALU.mult, op1=ALU.add))
        W("pool", tp)
        tok_rstd[e] = ev_p.inc(POOL.tensor_tensor(out=rstd[par][:, 0:26], in0=rstd[par][:, 0:26],
                                                  in1=mhalf[:, 0:26], op=ALU.pow))

    def A_groups(e, mid_hook=None):
        b = POS[e] % 2
        par = POS[e] % 2
        own = is_own(e)
        last = (e == ORDER[NE - 1])
        if own and last:
            glist = ["qA", "kA", "qB", "kvB", "vA", "g0", "g1", "g2", "g3"]
        elif own:
            glist = ["qA", "g0", "kA", "g1", "vA", "g2", "qB", "g3", "kvB"]
        else:
            glist = ["kA", "vA"] + (["kvB"] if 1 <= e <= 18 else [])
        tt = e - 2
        tg_last = None
        tred_last = None
        W("act", tok_stat[e])
        hook_at = min(2, len(glist) - 1)
        for gi, gname in enumerate(glist):
            a, bb = col_groups[gname]
            w = bb - a
            n = st["n"]; st["n"] += 1
            bk = n % NGB
            pb = bank(PS_G + bk)
            W("pe", bank_free[bk])
            W("pe", tok_a6[e])
            W("pe", tokW[gname])
            for c in range(8):
                ins = PE.matmul(pb[:, 0:w], lhsT=hT[b][:, c * 128:(c + 1) * 128], rhs=Win[:, c, a:bb],
                                start=(c == 0), stop=(c == 7))
            tg_ = ev_g.inc(ins)
            tg_last = tg_
            if gname in ("qA", "kA", "qB", "kvB"):
                m = st["m"]; st["m"] += 1
                s = m % 2
                nh, wn = (2, 128) if gname == "kvB" else (8, 512)
                c0 = NORM_C0[gname]
                q0 = NORM_OFF[gname]
                W("act", tg_)
                W("act", raw_free[par])
                ACT.activation(out=raw[par][:, q0:q0 + wn], in_=pb[:, 0:wn], func=AF.Copy)
                if gname == "kvB":
                    W("act", tok_vones)
                    ACT.activation(out=VB[:, e - 1, :, 0:64],
                                   in_=pb[:, 128:256].rearrange("p (h d) -> p h d", d=64), func=AF.Copy,
                                   scale=rsx[:, e:e + 1])
                W("act", sq_free[s])
                tsq = ev_sq.inc(ACT.activation(out=sq[s][:, 0:wn], in_=pb[:, 0:wn], func=AF.Square))
                bank_free[bk] = tsq
                W("dve", tsq)
                tred = ev_red.inc(DVE.tensor_reduce(out=ssq[par][:, c0:c0 + nh],
                                                    in_=sq[s][:, 0:wn].rearrange("p (h d) -> p h d", d=64),
                                                    axis=AX.X, op=ALU.add))
                sq_free[s] = tred
                tred_last = tred
            elif gname == "vA":
                W("act", tg_)
                W("act", tok_vones)
                tv = ev_sq.inc(ACT.activation(out=VA[:, e, :, 0:64],
                                              in_=pb[:, 0:512].rearrange("p (h d) -> p h d", d=64), func=AF.Copy,
                                              scale=rsx[:, e:e + 1]))
                bank_free[bk] = tv
            else:
                k = int(gname[1])
                gm = st["gm"]; st["gm"] += 1
                s = gm % 2
                W("act", tg_)
                W("act", gsb_free[s])
                tsg = ev_sq.inc(ACT.activation(out=gsb[s][:], in_=pb[:, 0:512], func=AF.Sigmoid,
                                               scale=rsx[:, e:e + 1]))
                bank_free[bk] = tsg
                W("sp", tsg)
                gsb_free[s] = EV('gst%d' % s).inc(SP.dma_start(out=gts[tt, :, k * 512:(k + 1) * 512], in_=gsb[s][:]), dma=True)
            if gi == hook_at and mid_hook is not None:
                mid_hook()
            if last and gname == "vA":
                W("pool", tg_)
                for k_ in range(5):
                    early["tok"] = EV("tabM").inc(POOL.dma_start(out=tabM[k_][:], in_=tabA[:, k_ * 1024:(k_ + 1) * 1024]),
                                                  dma=True)
            if last and gi == 3:
                emit_rstd(e, tred_last)
            if last and gi == 6:
                A_normalize(e)
        hT_free[b] = tg_last
        tok_lastgrp[e] = tg_last
        if not last:
            emit_rstd(e, tred_last)

    def A_normalize(e):
        par = POS[e] % 2
        own = is_own(e)
        W("dve", tok_rstd[e])
        W("dve", qn_free[par])
        names = (["qA"] if own else []) + ["kA"] + (["qB"] if own else []) + (["kvB"] if 1 <= e <= 18 else [])
        tq = None
        for gname in names:
            q0 = NORM_OFF[gname]
            c0 = NORM_C0[gname]
            if gname == "qB":
                o_v = qn[par][:, 1024:1536].rearrange("p (r g d) -> p g r d", r=4, g=2, d=64)
                i_v = raw[par][:, 1024:1536].rearrange("p (g r d) -> p g r d", g=2, r=4, d=64)
                r_v = rstd[par][:, 16:24].rearrange("p (g r) -> p g r", g=2).unsqueeze(3).to_broadcast([128, 2, 4, 64])
            else:
                nh, wn = (2, 128) if gname == "kvB" else (8, 512)
                o_v = qn[par][:, q0:q0 + wn].rearrange("p (h d) -> p h d", d=64)
                i_v = raw[par][:, q0:q0 + wn].rearrange("p (h d) -> p h d", d=64)
                r_v = rstd[par][:, c0:c0 + nh].unsqueeze(2).to_broadcast([128, nh, 64])
            tq = ev_qn.inc(DVE.tensor_tensor(out=o_v, in0=i_v, in1=r_v, op=ALU.mult))
        tok_qn_last[e] = tq
        raw_free[par] = tq

    def A_ttr(e):
        par = POS[e] % 2
        own = is_own(e)
        tt = e - 2
        pT2 = bankbf(PS_T2, 2)
        W("pe", tok_qn_last[e])
        W("pe", st["psT2_free"])
        srcs = []
        if own:
            srcs += [(p, p * 128) for p in range(4)]
            srcs += [(4 + r, 1024 + r * 128) for r in range(4)]
        srcs += [(8 + p, 512 + p * 128) for p in range(4)]
        if 1 <= e <= 18:
            srcs += [(12, 1536)]
        for slot, c0 in srcs:
            ins = PE.transpose(out=pT2[:, slot * 128:(slot + 1) * 128], in_=qn[par][:, c0:c0 + 128], identity=ident[:])
        tt_ = ev_ttr.inc(ins)
        qn_free[par] = tt_
        frees = []
        if own:
            W("dve", tt_)
            W("dve", late["qs"])
            DVE.tensor_scalar(out=QAT[:, :, tt * 128:(tt + 1) * 128],
                              in0=pT2[:, 0:512].rearrange("p (a t) -> p a t", a=4),
                              scalar1=qsA[:, 0:1], scalar2=None, op0=ALU.mult)
            td = ev_evd.inc(DVE.tensor_scalar(out=QBT[:, :, tt * 128:(tt + 1) * 128],
                                              in0=pT2[:, 512:1024].rearrange("p (a t) -> p a t", a=4),
                                              scalar1=qsB[:, 0:1], scalar2=None, op0=ALU.mult))
            frees.append(td)
        W("act", tt_)
        ta = ev_eva.inc(ACT.activation(out=KAT[:, :, e * 128:(e + 1) * 128],
                                       in_=pT2[:, 1024:1536].rearrange("p (a t) -> p a t", a=4), func=AF.Copy))
        if 1 <= e <= 18:
            ta = ev_eva.inc(ACT.activation(out=KBT[:, (e - 1) * 128:e * 128], in_=pT2[:, 1536:1664], func=AF.Copy))
        frees.append(ta)
        st["psT2_free"] = frees

    A_load(ORDER[0])
    A_load(ORDER[1])
    A_stat(ORDER[0])
    issue_w(["qA", "qB", "g0", "g1", "g2", "g3"])
    A_scale(ORDER[0])
    A_stat(ORDER[1])
    A_scale(ORDER[1])
    A_load(ORDER[2])
    A_tx_old(ORDER[0])

    def mid_hook(i):
        if i >= 1:
            A_normalize(ORDER[i - 1])
        if i + 2 < NE and ORDER[i + 2] in OLD:
            A_scale(ORDER[i + 2])
        elif i + 1 < NE and ORDER[i + 1] not in OLD:
            A_scale(ORDER[i + 1])
    for i in range(NE + 1):
        if i == 1:
            late_setup()
        if i + 1 < NE and ORDER[i + 1] in OLD:
            A_tx_old(ORDER[i + 1])
        if i + 2 < NE:
            A_stat(ORDER[i + 2])
        if i < NE:
            A_groups(ORDER[i], (lambda j=i: mid_hook(j)))
        if i >= 1:
            A_ttr(ORDER[i - 1])
        if i + 3 < NE:
            A_load(ORDER[i + 3])

    p1_done = [st["psT2_free"], gsb_free[0], gsb_free[1], tok_lastgrp[ORDER[NE - 1]], tok_lastgrp[ORDER[NE - 2]]]

    W("pool", [tok_lastgrp[ORDER[NE - 1]], tok_lastgrp[ORDER[NE - 2]]])
    tok_tabA = EV("tabA").inc(POOL.dma_start(out=tabAsb[:, 0:5, :].rearrange("p a b -> p (a b)"),
                                             in_=tabA[:, 0:5 * 1024]), dma=True)
    W("pool", tok_tabA)
    tok_tabB = EV("tabB").inc(POOL.dma_start(out=tabBsb[:].rearrange("p a b -> p (a b)"), in_=tabB), dma=True)
    W("pool", tok_tabB)
    tok_tabE = EV("tabE").inc(POOL.dma_start(out=tabAsb[:, 5:15, :].rearrange("p a b -> p (a b)"),
                                             in_=tabA[:, 5 * 1024:15 * 1024]), dma=True)
    ev_wb = EV("wb")
    W("pool", tok_tabE)
    W("pool", p1_done)
    ev_wb.inc(POOL.dma_start(out=Wba[:], in_=w_ba.rearrange("(c p) n -> p c n", p=128)), dma=True)
    ev_wb.inc(POOL.dma_start(out=Wbb[:], in_=w_bb.rearrange("(c p) n -> p c n", p=128)), dma=True)
    tok_wb = ev_wb.inc(POOL.dma_start(out=Wout[:], in_=w_out.rearrange("(c p) n -> p c n", p=128)), dma=True)
    W("act", late["sinkld"])
    tok_sinkexp = EV("sinkexp").inc(ACT.activation(out=sinkexp[:], in_=sinkexp[:], func=AF.Exp))
    ev_te = EV("tabexp")
    tab_ready = {}

    def table_tok(kind, idx):
        key = (kind, idx)
        if key not in tab_ready:
            if kind == "M":
                W("act", early["tok"])
                v = tabM[idx][:]
            elif kind == "A":
                W("act", tok_tabA if idx < 5 else tok_tabE)
                v = tabAsb[:, idx, :]
            else:
                W("act", tok_tabB)
                v = tabBsb[:, idx, :]
            tab_ready[key] = ev_te.inc(ACT.activation(out=v, in_=v, func=AF.Exp))
        return tab_ready[key]
    tabE = {"tok": None, "tokB": None}

    def exp_B_table():
        W("act", tok_tabB)
        tabE["tokB"] = ev_te.inc(ACT.activation(out=tabBsb[:], in_=tabBsb[:], func=AF.Exp))

    def exp_edge_tables():
        W("act", tok_tabE)
        for k in range(1, 3):
            v = tabAsb[:, 5 * k:5 * (k + 1), :]
            tabE["tok"] = ev_te.inc(ACT.activation(out=v, in_=v, func=AF.Exp))

    ev_S = EV("S")
    ev_add = EV("add")
    ev_exp = EV("exp")
    ev_pv = EV("pv")
    ev_na = EV("na")
    PS_S = [0, 2]
    PS_OA = 4
    PS_OB = 6
    psS_free = [None, None]
    PR_free = [None, None, None]
    PT_free = [None, None, None]
    O_free = {"A": None, "B": None}
    tok_norm = {}

    slots = []
    for t_ in list(range(2, 14)) + [0, 1, 14, 15]:
        for j in range(5):
            slots.append(("A", t_, j))
        for j in range(3):
            slots.append(("B", t_, j))

    def Oview(bk):
        return ps[:, 512 * bk:512 * (bk + 2)].rearrange("p (b c) -> p b c", b=2)[:, :, 0:260].rearrange(
            "p b (h d) -> p b h d", d=65)

    def emit_S(n):
        kind, t_, j = slots[n]
        pS = bank(PS_S[n % 2], 2)
        W("pe", psS_free[n % 2])
        if n == 0:
            W("pe", p1_done)
        for h in range(8):
            if kind == "A":
                e_ = t_ + j
                p_, hp = h // 2, (h % 2) * 64
                lhsT = KAT[hp:hp + 64, p_, e_ * 128:(e_ + 1) * 128]
                rhs = QAT[hp:hp + 64, p_, t_ * 128:(t_ + 1) * 128]
            else:
                e_ = t_ + 1 + j
                g_, r_ = h // 4, h % 4
                lhsT = KBT[g_ * 64:(g_ + 1) * 64, (e_ - 1) * 128:e_ * 128]
                rhs = QBT[g_ * 64:(g_ + 1) * 64, r_, t_ * 128:(t_ + 1) * 128]
            cb = _cbA(h) if kind == "A" else h
            ins = PE.matmul(pS[:, cb * 128:(cb + 1) * 128], lhsT=lhsT, rhs=rhs, start=True, stop=True)
        tS = ev_S.inc(ins)
        use_early = (kind == "A" and n < 5)
        if use_early:
            ttab = table_tok("M", tabA_index(t_, j))
        else:
            ttab = table_tok(kind, tabA_index(t_, j) if kind == "A" else tabB_index(t_, j))
        W("act", tS)
        W("act", PR_free[n % 3])
        tE = ev_exp.inc(ACT.activation(out=PR[n % 3][:], in_=pS, func=AF.Exp))
        psS_free[n % 2] = tE
        W("dve", tE)
        W("dve", PT_free[n % 3])
        if use_early:
            tb = tabM[tabA_index(t_, j)][:]
            W("dve", ttab)
        elif kind == "A":
            tb = tabAsb[:, tabA_index(t_, j), :]
            W("dve", ttab)
        else:
            tb = tabBsb[:, tabB_index(t_, j), :]
            W("dve", ttab)
        tA = ev_add.inc(DVE.tensor_tensor(out=PT[n % 3][:], in0=PR[n % 3][:], in1=tb, op=ALU.mult))
        PR_free[n % 3] = tA
        return tA

    def emit_PV(n, tE):
        kind, t_, j = slots[n]
        nslot = 5 if kind == "A" else 3
        bk = PS_OA if kind == "A" else PS_OB
        W("pe", tE)
        if j == 0:
            W("pe", O_free[kind])
        for h in range(8):
            if kind == "A":
                rhs = VA[:, t_ + j, h, :]
            else:
                rhs = VB[:, t_ + j, h // 4, :]
            o_ap = ps[:, 512 * (bk + h // 4) + (h % 4) * 65: 512 * (bk + h // 4) + (h % 4) * 65 + 65]
            cb = _cbA(h) if kind == "A" else h
            ins = PE.matmul(o_ap, lhsT=PT[n % 3][:, cb * 128:(cb + 1) * 128], rhs=rhs,
                            start=(j == 0 and h % 4 == 0), stop=(j == nslot - 1 and h % 4 == 3))
        tP = ev_pv.inc(ins)
        PT_free[n % 3] = tP
        if j == nslot - 1:
            ov = Oview(bk)
            W("dve", tP)
            if kind == "A":
                rd = rdenA
                t1 = ev_na.inc(DVE.reciprocal(out=rd[:].rearrange("p (b h o) -> p b h o", b=2, h=4, o=1),
                                              in_=ov[:, :, :, 64:65]))
            else:
                rd = rdenB
                W("dve", tok_sinkexp)
                t0 = ev_na.inc(DVE.tensor_tensor(out=rd[:].rearrange("p (b h o) -> p b h o", b=2, h=4, o=1),
                                                 in0=ov[:, :, :, 64:65],
                                                 in1=sinkexp[:].rearrange("p (b h o) -> p b h o", b=2, h=4, o=1),
                                                 op=ALU.add))
                W("dve", t0)
                t1 = ev_na.inc(DVE.reciprocal(out=rd[:], in_=rd[:]))
            W("dve", t1)
            c0 = 0 if kind == "A" else 512
            t2 = ev_na.inc(DVE.tensor_tensor(
                out=otok[:, t_, c0:c0 + 512].rearrange("p (b h d) -> p b h d", b=2, h=4),
                in0=ov[:, :, :, 0:64],
                in1=rd[:].rearrange("p (b h) -> p b h", b=2).unsqueeze(3).to_broadcast([128, 2, 4, 64]),
                op=ALU.mult))
            O_free[kind] = t2
            tok_norm[(kind, t_)] = t2

    toks = {}
    for n in range(len(slots)):
        toks[n] = emit_S(n)
        if n >= 2:
            emit_PV(n - 2, toks[n - 2])
    emit_PV(len(slots) - 2, toks[len(slots) - 2])
    emit_PV(len(slots) - 1, toks[len(slots) - 1])
    p2a_done = [tok_norm[("A", NT - 1)], tok_norm[("B", NT - 1)], PT_free[0], PT_free[1], PT_free[2]]

    ev_wg = EV("wg")
    w_gate_v = w_gate.rearrange("(c p) n -> p c n", p=128)
    w_up_v = w_up.rearrange("(c p) n -> p c n", p=128)
    wtok = {}

    def issue_wg_prefetch(k):
        W("pool", p2a_done)
        if k < 8:
            wtok["wg"] = ev_wg.inc(POOL.dma_start(out=Wg[:, k, :], in_=w_gate_v[:, k, :]), dma=True)
        elif k < 12:
            c0 = 2 * (k - 8)
            wtok["wu1"] = EV('wu1').inc(POOL.dma_start(out=Wu1[:, c0:c0 + 2, :],
                                                       in_=w_up_v[:, c0:c0 + 2, 0:WU_SPLIT]), dma=True)
    W("sp", p1_done)
    W("sp", tok_a4[ORDER[NE - 1]])
    tok_gvec2 = ev_gv.inc(SP.dma_start(out=gvec[:], in_=gffn.partition_broadcast(128)), dma=True)

    ev_gl = EV("gl")
    ev_xl = EV("xl2")
    ev_otr = EV("otr")
    ev_ev2 = EV("ev2")
    ev_y = EV("ymm")
    ev_cmb = EV("cmb")
    ev_ytr = EV("ytr")
    ev_x2mm = EV("x2mm")
    ev_res = EV("res")
    ev_st2 = EV("st2")
    ev_sq2 = EV("sq2")
    ev_h2 = EV("h2")
    ev_h2tr = EV("h2tr")
    PS_TB = [0, 1]
    PS_Y = [[2, 3], [4, 5]]
    PS_X = 6
    psT_free = [None, None]
    st2 = {"k": 0}
    gsh_free = [None, None]
    xt2_free = [None, None]
    ytok_free = [None, None]
    h2b_free = [None, None]
    oT_free = [None, None]
    yT_free = [None, None]
    usb_free = [None, None]
    psY_free = [None, None]
    psX_free = [None]
    tok_oT = {}
    tok_cmb = {}
    tok_yT = {}
    tok_h2 = {}
    tok_h2T = {}
    tok_x2st = {}
    tok_res = {}
    tok_gl = {}
    tok_xl = {}

    def B_loads(t_):
        b = t_ % 2
        W("sp", xt2_free[b])
        if t_ < 2:
            W("sp", p2a_done)
        tok_xl[t_] = EV('xl2_%d' % b).inc(SP.dma_start(out=xt2[b][:], in_=xe[(t_ + 2) * 128:(t_ + 3) * 128, :]), dma=True)

    def B_gload(t_, c):
        W("sp", gsh_free[c])
        if t_ == 0:
            W("sp", p2a_done)
        tok_gl[(t_, c)] = EV('gl%d' % c).inc(SP.dma_start(
            out=gsh[c][:], in_=gts[t_].rearrange("p (g n) -> p g n", g=2)[:, :, c * 512:(c + 1) * 512]), dma=True)

    def tr8(src, dst_free_tok, extra_wait):
        k = st2["k"]; st2["k"] += 1
        pT = bankbf(PS_TB[k % 2])
        W("pe", psT_free[k % 2])
        W("pe", extra_wait)
        for c in range(8):
            ins = PE.transpose(out=pT[:, c * 128:(c + 1) * 128], in_=src[:, c * 128:(c + 1) * 128], identity=ident[:])
        return k % 2, pT, ins

    def B_otr(t_):
        b = t_ % 2
        kk, pT, ins = tr8(otok[:, t_, :], None, p2a_done if t_ == 0 else None)
        tt_ = ev_otr.inc(ins)
        W("act", tt_)
        W("act", oT_free[b])
        ta_ = ev_ev2.inc(ACT.activation(out=oTb[b][:, 0:512], in_=pT[:, 0:512], func=AF.Copy))
        tb_ = ev_ev2.inc(ACT.activation(out=oTb[b][:, 512:1024], in_=pT[:, 512:1024], func=AF.Copy))
        tok_oT[t_] = (ta_, tb_)
        psT_free[kk] = tb_

    def B_ymm(t_):
        b = t_ % 2
        W("pe", tok_oT[t_][0])
        W("pe", tok_wb)
        for c in range(2):
            W("pe", psY_free[c])
            pa = bank(PS_Y[c][0])
            pbk = bank(PS_Y[c][1])
            for k in range(4):
                PE.matmul(pa, lhsT=oTb[b][:, k * 128:(k + 1) * 128], rhs=Wba[:, k, c * 512:(c + 1) * 512],
                          start=(k == 0), stop=(k == 3))
            W("pe", tok_oT[t_][1])
            for k in range(4):
                ins = PE.matmul(pbk, lhsT=oTb[b][:, (4 + k) * 128:(5 + k) * 128], rhs=Wbb[:, k, c * 512:(c + 1) * 512],
                                start=(k == 0), stop=(k == 3))
            ty = ev_y.inc(ins)
            if c == 1:
                oT_free[b] = ty
            W("dve", ty)
            W("dve", tok_gl[(t_, c)])
            W("dve", usb_free[c])
            t1 = ev_cmb.inc(DVE.tensor_tensor(out=usb[c][:], in0=pa, in1=gsh[c][:, 0, :], op=ALU.mult))
            t2 = ev_cmb.inc(DVE.tensor_tensor(out=pbk, in0=pbk, in1=gsh[c][:, 1, :], op=ALU.mult))
            gsh_free[c] = t2
            W("dve", t2)
            W("dve", ytok_free[b])
            t3 = ev_cmb.inc(DVE.tensor_tensor(out=ytok[b][:, c * 512:(c + 1) * 512], in0=pbk, in1=usb[c][:], op=ALU.add))
            usb_free[c] = t3
            psY_free[c] = t3
            tok_cmb[t_] = t3
            if t_ + 1 < NT:
                B_gload(t_ + 1, c)

    def B_ytr(t_):
        b = t_ % 2
        kk, pT, ins = tr8(ytok[b], None, tok_cmb[t_])
        tt_ = ev_ytr.inc(ins)
        ytok_free[b] = tt_
        W("act", tt_)
        W("act", yT_free[b])
        ta_ = ev_ev2.inc(ACT.activation(out=yTb[b][:, 0:512], in_=pT[:, 0:512], func=AF.Copy))
        tb_ = ev_ev2.inc(ACT.activation(out=yTb[b][:, 512:1024], in_=pT[:, 512:1024], func=AF.Copy))
        tok_yT[t_] = (ta_, tb_)
        psT_free[kk] = tb_

    def B_x2mm(t_):
        b = t_ % 2
        W("pe", tok_yT[t_][0])
        W("pe", psX_free[0])
        pX = bank(PS_X, 2)
        for c in range(2):
            for k in range(8):
                if k == 4:
                    W("pe", tok_yT[t_][1])
                ins = PE.matmul(pX[:, c * 512:(c + 1) * 512], lhsT=yTb[b][:, k * 128:(k + 1) * 128],
                                rhs=Wout[:, k, c * 512:(c + 1) * 512], start=(k == 0), stop=(k == 7))
        tx_ = ev_x2mm.inc(ins)
        yT_free[b] = tx_
        W("dve", tx_)
        W("dve", tok_xl[t_])
        tr_ = ev_res.inc(DVE.tensor_tensor(out=xt2[b][:], in0=pX, in1=xt2[b][:], op=ALU.add))
        psX_free[0] = tr_
        W("sp", tr_)
        tok_x2st[t_] = EV('st2_%d' % b).inc(SP.dma_start(out=out[t_ * 128:(t_ + 1) * 128, :], in_=xt2[b][:]), dma=True)
        tok_res[t_] = tr_

    def B_norm(t_):
        b = t_ % 2
        tr_ = tok_res[t_]
        W("act", tr_)
        ts_ = ev_sq2.inc(ACT.activation(out=junk2[:], in_=xt2[b][:], func=AF.Square, accum_out=ssq2[:, t_:t_ + 1]))
        W("pool", ts_)
        tp = ev_p.inc(POOL.tensor_scalar(out=rs2[:, t_:t_ + 1], in0=ssq2[:, t_:t_ + 1], scalar1=1.0 / D, scalar2=EPS,
                                         op0=ALU.mult, op1=ALU.add))
        W("pool", tp)
        tp = ev_p.inc(POOL.tensor_tensor(out=rs2[:, t_:t_ + 1], in0=rs2[:, t_:t_ + 1], in1=mhalf[:, 0:1], op=ALU.pow))
        issue_wg_prefetch(t_)
        W("dve", tp)
        W("dve", h2b_free[b])
        W("dve", tok_gvec2)
        tok_h2[t_] = ev_h2.inc(DVE.scalar_tensor_tensor(out=h2b[b][:], in0=xt2[b][:], scalar=rs2[:, t_:t_ + 1],
                                                        in1=gvec[:], op0=ALU.mult, op1=ALU.mult))
        xt2_free[b] = [tok_x2st[t_], tok_h2[t_]]
        if t_ + 2 < NT:
            B_loads(t_ + 2)

    def B_h2tr(t_):
        b = t_ % 2
        kk, pT, ins = tr8(h2b[b], None, tok_h2[t_])
        tt_ = ev_h2tr.inc(ins)
        h2b_free[b] = tt_
        W("act", tt_)
        tok_h2T[t_] = ev_ev2.inc(ACT.activation(out=h2T[:, :, t_ * 128:(t_ + 1) * 128],
                                                in_=pT.rearrange("p (a t) -> p a t", a=8), func=AF.Copy))
        psT_free[kk] = tok_h2T[t_]

    B_loads(0)
    B_loads(1)
    B_gload(0, 0)
    B_gload(0, 1)
    B_otr(0)
    B_ymm(0)
    for i in range(NT):
        if i + 1 < NT:
            B_otr(i + 1)
        if i >= 1:
            B_norm(i - 1)
        B_ytr(i)
        if i + 1 < NT:
            B_ymm(i + 1)
        if i == NT - 1:
            B_x2mm(i)
            B_h2tr(i - 1)
        else:
            if i >= 1:
                B_h2tr(i - 1)
            B_x2mm(i)
    tok_wg = wtok["wg"]
    tok_wu1 = wtok["wu1"]
    ev_gu = EV("gu")
    ev_silu = EV("silu")
    ev_u = EV("umul")
    PS_GU = [[2, 3], [4, 5]]
    psGU_free = [psY_free[0], psY_free[1]]
    sgb_free = [[psY_free[0], psY_free[1]], [psY_free[0], psY_free[1]]]
    p3 = {"uT_free": None, "tok_wu2": None}
    EARLY_GU = 4

    def emit_GU(u, j, tok_uT):
        s_ = j % 2
        pG = bank(PS_GU[s_][0])
        pU = bank(PS_GU[s_][1])
        W("pe", psGU_free[s_])
        W("pe", tok_wg)
        for c in range(8):
            PE.matmul(pG, lhsT=Wg[:, c, j * 128:(j + 1) * 128], rhs=h2T[:, c, u * 512:(u + 1) * 512],
                      start=(c == 0), stop=(c == 7))
        W("pe", tok_wu1 if j < 11 else p3["tok_wu2"])
        Wuj, jj = (Wu1, j) if j < 11 else (Wu2, j - 11)
        for c in range(8):
            ins = PE.matmul(pU, lhsT=Wuj[:, c, jj * 128:(jj + 1) * 128], rhs=h2T[:, c, u * 512:(u + 1) * 512],
                            start=(c == 0), stop=(c == 7))
        tgu = ev_gu.inc(ins)
        W("act", tgu)
        W("act", sgb_free[s_])
        tsl = ev_silu.inc(ACT.activation(out=sgb[s_][:], in_=pG, func=AF.Silu))
        W("dve", tsl)
        if j == 0:
            W("dve", p3["uT_free"])
        tu = ev_u.inc(DVE.tensor_tensor(out=uT[:, j, :], in0=pU, in1=sgb[s_][:], op=ALU.mult))
        sgb_free[s_] = tu
        psGU_free[s_] = tu
        tok_uT[j] = tu

    tok_uT0 = {}
    B_norm(NT - 1)
    for j in range(EARLY_GU):
        emit_GU(0, j, tok_uT0)
    B_h2tr(NT - 1)
    p2b_done = [tok_h2T[NT - 1], tok_h2T[NT - 2], tok_x2st[NT - 1], tok_x2st[NT - 2], yT_free[0], yT_free[1]]

    W("pool", p2b_done)
    tok_wu2 = EV('wu2').inc(POOL.dma_start(out=Wu2[:], in_=w_up_v[:, :, WU_SPLIT:FF]), dma=True)
    ev_wd = EV("wd")
    w_down_v = w_down.rearrange("(c p) n -> p c n", p=128)
    tok_wd = {}
    for k0 in range(0, NFC, 6):
        k1 = min(NFC, k0 + 6)
        tkn = EV('wd%d' % k0).inc(POOL.dma_start(out=Wd[:, k0:k1, :], in_=w_down_v[:, k0:k1, :]), dma=True)
        for k in range(k0, k1):
            tok_wd[k] = tkn

    ev_d = EV("dmm")
    ev_fin = EV("fin")
    PS_D = [0, 6]
    psD_free = [p2b_done, p2b_done]
    x2t_free = [None, None]
    tok_x2l = {}
    tok_ost = {}
    p3["tok_wu2"] = tok_wu2

    def C_x2load(tile):
        b = tile % 2
        W("sp", x2t_free[b])
        W("sp", p2b_done)
        W("sp", tok_x2st[tile])
        tok_x2l[tile] = EV('x2l%d' % b).inc(SP.dma_start(out=x2t[b][:], in_=out[tile * 128:(tile + 1) * 128, :]), dma=True)

    C_x2load(0)
    C_x2load(1)
    for u in range(4):
        tok_uT = tok_uT0 if u == 0 else {}
        for j in range(EARLY_GU if u == 0 else 0, NFC):
            emit_GU(u, j, tok_uT)
        for i in range(4):
            tile = 4 * u + i
            b = tile % 2
            pD = bank(PS_D[i % 2], 2)
            W("pe", psD_free[i % 2])
            for c in range(2):
                for j in range(NFC):
                    W("pe", tok_uT[j])
                    W("pe", tok_wd[j])
                    ins = PE.matmul(pD[:, c * 512:(c + 1) * 512], lhsT=uT[:, j, i * 128:(i + 1) * 128],
                                    rhs=Wd[:, j, c * 512:(c + 1) * 512], start=(j == 0), stop=(j == NFC - 1))
            td = ev_d.inc(ins)
            if i == 3:
                p3["uT_free"] = td
            W("dve", td)
            W("dve", tok_x2l[tile])
            tf = ev_fin.inc(DVE.tensor_tensor(out=x2t[b][:], in0=pD, in1=x2t[b][:], op=ALU.add))
            psD_free[i % 2] = tf
            W("sp", tf)
            tok_ost[tile] = EV('ost%d' % b).inc(SP.dma_start(out=out[tile * 128:(tile + 1) * 128, :], in_=x2t[b][:]), dma=True)
            x2t_free[b] = tok_ost[tile]
            if tile + 2 < NT:
                C_x2load(tile + 2)
    W("sp", tok_ost[NT - 1])
    W("sp", tok_ost[NT - 2])
    return nc


_CACHE = {}


def _host_inputs(x, norm_mix, w_in, q_norm_a, k_norm_a, rpb_a, q_norm_b, k_norm_b, sink_b, t5_table,
                 w_branch_a, w_branch_b, w_out, norm_ffn, w_gate, w_up, w_down):
    f = lambda a: np.ascontiguousarray(np.asarray(a, dtype=np.float32))
    x = f(x)
    shared = {
        "w_in": f(w_in[0]), "gmix": f(norm_mix[0]).reshape(1, D),
        "gmixT": np.ascontiguousarray(f(norm_mix[0]).reshape(8, 128).T), "gffn": f(norm_ffn[0]).reshape(1, D),
        "qna": f(q_norm_a[0]).reshape(64, 1), "kna": f(k_norm_a[0]).reshape(64, 1),
        "qnb": f(q_norm_b[0]).reshape(64, 1), "knb": f(k_norm_b[0]).reshape(64, 1),
        "sink": f(sink_b[0]).reshape(1, 8),
        "w_ba": f(w_branch_a[0]), "w_bb": f(w_branch_b[0]), "w_out": f(w_out[0]),
        "w_gate": f(w_gate[0]), "w_up": f(w_up[0]), "w_down": f(w_down[0]),
    }
    rpb = f(rpb_a[0])
    t5 = f(t5_table)
    tabAs = [_build_tabA(rpb, s) for s in range(4)]
    tabBs = [_build_tabB(t5, s) for s in range(4)]
    in_maps = []
    for c in range(NCORES):
        b, s = c // 4, c % 4
        xe = np.zeros((NE * 128, D), dtype=np.float32)
        for e in range(NE):
            r0 = _ext_rows(s, e)
            if r0 is None:
                continue
            xe[e * 128:(e + 1) * 128] = x[b, r0 * 64:r0 * 64 + 128]
        m = dict(shared)
        m["xe"] = xe
        m["xeT"] = np.ascontiguousarray(xe.reshape(NE, 128, 8, 128).transpose(0, 3, 2, 1).reshape(NE * 128, D))
        m["tabA"] = tabAs[s]
        m["tabB"] = tabBs[s]
        in_maps.append(m)
    return in_maps


def kernel(**inputs):
    if "nc" not in _CACHE:
        _CACHE["nc"] = build_program()
    nc = _CACHE["nc"]
    in_maps = _host_inputs(**inputs)
    res = run_bass_kernel_spmd(nc, in_maps, core_ids=list(range(NCORES)))
    outp = np.empty((2, T, D), dtype=np.float32)
    for c in range(NCORES):
        b, s = c // 4, c % 4
        outp[b, s * TOK:(s + 1) * TOK] = res.results[c]["out"]
    return outp
```

```python
import numpy as np
import concourse.bass as bass
import concourse.mybir as mybir
from concourse.bass_utils import run_bass_kernel_spmd

F32 = mybir.dt.float32
BF16 = mybir.dt.bfloat16
AF = mybir.ActivationFunctionType
ALU = mybir.AluOpType
AX = mybir.AxisListType

NCORES = 8
D = 1024
T = 8192
TOK = 2048
NT = 16
NE = 20
FF = 2816
NFC = 22
NEG = -30000.0
EPS = 1e-6

MYBASE = 17920
SB_TOP = 229376
TOTAL = SB_TOP - MYBASE


def _t5_bucket(rel):
    half = 16
    max_exact = 8
    ret = (rel > 0).astype(np.int32) * half
    n = np.abs(rel)
    large = max_exact + (np.log(np.maximum(n, 1) / max_exact)
                         / np.log(128 / max_exact) * (half - max_exact)).astype(np.int32)
    large = np.minimum(large, half - 1)
    return ret + np.where(n < max_exact, n, large)


_HPERM_A = [0, 2, 4, 6, 1, 3, 5, 7]


def _cbA(h):
    return (h % 2) * 4 + h // 2


def tabA_index(t, j):
    if t == 0:
        return {0: 5, 1: 6, 4: 7}.get(j, j)
    if t == 1:
        return {0: 8, 4: 9}.get(j, j)
    if t == 14:
        return {0: 10, 4: 11}.get(j, j)
    if t == 15:
        return {0: 12, 3: 13, 4: 14}.get(j, j)
    return j


def tabB_index(t, j):
    if t == 0 and j == 0:
        return 3
    if t == 15 and j == 2:
        return 4
    return j


def _ext_rows(s, e):
    if s == 0 and e == 0:
        return 6
    if s == 0 and e == 1:
        return None
    if s == 3 and e == 18:
        return None
    if s == 3 and e == 19:
        return 120
    return 32 * s - 4 + 2 * e


def _build_tabA(rpb, s):
    out = np.full((15, 128, 8, 128), NEG, dtype=np.float32)
    kl = np.arange(128)
    krl, kc = kl // 64, kl % 64
    ql = np.arange(128)
    qrl, qc = ql // 64, ql % 64
    ws = np.clip(qc - 8, 0, 48)
    colok = (kc[:, None] >= ws[None, :]) & (kc[:, None] < ws[None, :] + 16)
    dc = np.clip(kc[:, None] - qc[None, :], -15, 15) + 15
    reps = {}
    for t in range(16):
        for j in range(5):
            idx = tabA_index(t, j)
            if idx >= 5 or (t == 5):
                reps[idx] = (t, j)
    for idx, (t, j) in reps.items():
        k0 = _ext_rows(s, t + j)
        if k0 is None:
            continue
        q0 = 32 * s + 2 * t
        kr = k0 + krl
        qr = q0 + qrl
        rs = np.clip(qr - 4, 0, 120)
        rowok = (kr[:, None] >= rs[None, :]) & (kr[:, None] < rs[None, :] + 8)
        dr = kr[:, None] - qr[None, :] + 7
        ok = rowok & colok
        drc = np.clip(dr, 0, 14)
        vals = rpb[drc, dc, :]
        vals = np.where(ok[:, :, None], vals, np.float32(NEG))
        out[idx] = np.transpose(vals, (0, 2, 1))[:, _HPERM_A, :]
    return np.ascontiguousarray(np.transpose(out, (1, 0, 2, 3)).reshape(128, 15 * 1024))


def _build_tabB(t5, s):
    out = np.full((5, 128, 8, 128), NEG, dtype=np.float32)
    k = np.arange(128)[:, None]
    q = np.arange(128)[None, :]
    for jb in range(3):
        rel = (jb - 1) * 128 + k - q
        ok = np.abs(rel) <= 128
        vals = t5[_t5_bucket(rel), :]
        vals = np.where(ok[:, :, None], vals, np.float32(NEG))
        out[jb] = np.transpose(vals, (0, 2, 1))
    if s != 0:
        out[3] = out[0]
    if s != 3:
        out[4] = out[2]
    return np.ascontiguousarray(np.transpose(out, (1, 0, 2, 3)).reshape(128, 5 * 1024))


class _Ev:
    def __init__(self, nc, name):
        self.sem = nc.alloc_semaphore(name)
        self.n = 0

    def inc(self, ins, dma=False):
        k = 16 if dma else 1
        ins.then_inc(self.sem, k)
        self.n += k
        return (self, self.n)


def build_program():
    nc = bass.Bass("TRN2", target_bir_lowering=False)
    PE, DVE, ACT, POOL, SP = nc.tensor, nc.vector, nc.scalar, nc.gpsimd, nc.sync
    eng = {"pe": PE, "dve": DVE, "act": ACT, "pool": POOL, "sp": SP}
    waited = {}

    def W(e, tok):
        if tok is None:
            return
        if isinstance(tok, list):
            for x in tok:
                W(e, x)
            return
        ev, val = tok
        key = (e, id(ev))
        if waited.get(key, 0) >= val:
            return
        waited[key] = val
        eng[e].wait_ge(ev.sem, val)

    evs = {}

    def EV(name):
        if name not in evs:
            evs[name] = _Ev(nc, name)
        return evs[name]

    def din(name, shape, dt=F32):
        return nc.dram_tensor(name, list(shape), dt, kind="ExternalInput").ap()

    xe = din("xe", [NE * 128, D])
    xeT = din("xeT", [NE * 128, D])
    gmixT = din("gmixT", [128, 8])
    w_in = din("w_in", [D, 4352])
    gmix = din("gmix", [1, D])
    gffn = din("gffn", [1, D])
    qna = din("qna", [64, 1])
    kna = din("kna", [64, 1])
    qnb = din("qnb", [64, 1])
    knb = din("knb", [64, 1])
    sink = din("sink", [1, 8])
    tabA = din("tabA", [128, 15 * 1024])
    tabB = din("tabB", [128, 5 * 1024])
    w_ba = din("w_ba", [512, D])
    w_bb = din("w_bb", [512, D])
    w_out = din("w_out", [D, D])
    w_gate = din("w_gate", [D, FF])
    w_up = din("w_up", [D, FF])
    w_down = din("w_down", [FF, D])
    out = nc.dram_tensor("out", [TOK, D], F32, kind="ExternalOutput").ap()
    gts = nc.dram_tensor("gts", [NT, 128, 2048], BF16).ap()

    def at(name, shape, dt, rel):
        nbytes = int(np.prod(shape[1:])) * (4 if dt == F32 else 2)
        assert rel % 32 == 0, (name, rel)
        assert rel + nbytes <= TOTAL, (name, rel, nbytes, TOTAL)
        return nc.alloc_sbuf_tensor_at(name, list(shape), dt, offset=MYBASE + rel)

    QAT = at("QAT", [128, 4, TOK], BF16, 0)
    KAT = at("KAT", [128, 4, NE * 128], BF16, 16384)
    VA = at("VA", [128, NE, 8, 65], BF16, 36864)
    QBT = at("QBT", [128, 4, TOK], BF16, 57696)
    KBT = at("KBT", [128, 18 * 128], BF16, 74080)
    VB = at("VB", [128, 18, 2, 65], BF16, 78688)
    QKV_END = 83392
    Win = at("Win", [128, 8, 4352], BF16, QKV_END)
    W1 = QKV_END + 69632
    CB = TOTAL - 5152
    gvec = at("gvec", [128, D], F32, CB)
    ident = at("ident", [128, 128], BF16, CB + 4096)
    SM = CB + 4096 + 256
    qsA = at("qsA", [128, 1], F32, SM)
    qsB = at("qsB", [128, 1], F32, SM + 32)
    gtmp = at("gtmp", [128, 4], F32, SM + 64)
    sinkexp = at("sinkexp", [128, 8], F32, SM + 96)
    mhalf = at("mhalf", [128, 32], F32, SM + 128)
    ssqx = at("ssqx", [128, 20], F32, SM + 256)
    rsx = at("rsx", [128, 20], F32, SM + 352)
    ssq2 = at("ssq2", [128, 16], F32, SM + 448)
    rs2 = at("rs2", [128, 16], F32, SM + 512)
    rdenA = at("rdenA", [128, 8], F32, SM + 576)
    rdenB = at("rdenB", [128, 8], F32, SM + 608)
    epsq = at("epsq", [128, 20], F32, SM + 640)
    gcolT = at("gcolT", [128, 8], F32, SM + 736)
    assert SM + 768 <= TOTAL

    o = W1
    xt = [at("xt%d" % i, [128, D], F32, o + 4096 * i) for i in range(2)]; o += 8192
    xT = [at("xT%d" % i, [128, 8, 128], F32, o + 4096 * i) for i in range(2)]
    hb = [at("hb%d" % i, [128, D], BF16, o + 4096 + 2048 * i) for i in range(2)]; o += 8192
    raw = [at("raw%d" % i, [128, 1664], F32, o + 6656 * i) for i in range(2)]; o += 13312
    junk = at("junk", [128, D], BF16, o); o += 2048
    hT = [at("hT%d" % i, [128, D], BF16, o + 2048 * i) for i in range(2)]; o += 4096
    sq = [at("sq%d" % i, [128, 512], F32, o + 2048 * i) for i in range(2)]; o += 4096
    qn = [at("qn%d" % i, [128, 1664], BF16, o + 3328 * i) for i in range(2)]; o += 6656
    gsb = [at("gsb%d" % i, [128, 512], BF16, o + 1024 * i) for i in range(2)]; o += 2048
    ssq = [at("ssq%d" % i, [128, 32], F32, o + 128 * i) for i in range(2)]; o += 256
    rstd = [at("rstd%d" % i, [128, 32], F32, o + 128 * i) for i in range(2)]; o += 256
    identf = at("identf", [128, 128], F32, o); o += 512
    assert o <= CB

    tabAsb = at("tabAsb", [128, 15, 1024], BF16, QKV_END)
    tabM = [at("tabM%d" % k, [128, 1024], BF16, QKV_END + (5 + k // 2) * 8704 + (k % 2) * 2048) for k in range(5)]
    tabBsb = at("tabBsb", [128, 5, 1024], BF16, QKV_END + 30720)
    OTOK = QKV_END + 40960
    otok = at("otok", [128, NT, 1024], BF16, OTOK)
    WB = OTOK + 32768
    Wba = at("Wba", [128, 4, D], BF16, WB)
    Wbb = at("Wbb", [128, 4, D], BF16, WB + 8192)
    Wout = at("Wout", [128, 8, D], BF16, WB + 16384)
    W2 = WB + 32768
    PR = [at("PR%d" % i, [128, 1024], BF16, W2 + 2048 * i) for i in range(3)]
    PT = [at("PT%d" % i, [128, 1024], BF16, W2 + 8192 + 2048 * i) for i in range(3)]
    assert W2 + 8192 + 6144 <= CB

    h2T = at("h2T", [128, 8, TOK], BF16, 0)
    Wg = at("Wg", [128, 8, FF], BF16, 32768)
    Wu1 = at("Wu1", [128, 8, 1408], BF16, 77824)
    Wu2 = at("Wu2", [128, 8, 1408], BF16, 100352)
    WU_SPLIT = 1408
    X2B = 77824 + 8 * WU_SPLIT * 2
    o = X2B
    xt2 = [at("xt2_%d" % i, [128, D], F32, o + 4096 * i) for i in range(2)]; o += 8192
    ytok = [at("ytok%d" % i, [128, D], BF16, o + 2048 * i) for i in range(2)]; o += 4096
    h2b = [at("h2b%d" % i, [128, D], BF16, o + 2048 * i) for i in range(2)]; o += 4096
    oTb = [at("oTb%d" % i, [128, D], BF16, o + 2048 * i) for i in range(2)]; o += 4096
    junk2 = at("junk2", [128, D], BF16, o); o += 2048
    assert o <= 122880
    o = W2
    yTb = [at("yTb%d" % i, [128, D], BF16, o + 2048 * i) for i in range(2)]; o += 4096
    usb = [at("usb%d" % i, [128, 512], F32, o + 2048 * i) for i in range(2)]; o += 4096
    gsh = [at("gsh%d" % i, [128, 2, 512], BF16, o + 2048 * i) for i in range(2)]; o += 4096
    assert o <= CB
    Wd = at("Wd", [128, NFC, D], BF16, 122880)
    uT = at("uT", [128, NFC, 512], BF16, 167936)
    o = 190464
    x2t = [at("x2t%d" % i, [128, D], F32, o + 4096 * i) for i in range(2)]; o += 8192
    sgb = [at("sgb%d" % i, [128, 512], F32, o + 2048 * i) for i in range(2)]; o += 4096
    assert o <= CB + 4096

    ps = nc.alloc_psum_tensor("ps", [128, 4096], F32)

    def bank(k, n=1):
        return ps[:, 512 * k:512 * (k + n)]

    def bankbf(k, n=1):
        return ps[:, 512 * k:512 * (k + n)].bitcast(BF16)

    ev_setp = EV("setp")
    ev_setv = EV("setv")
    ev_setd = EV("setd")
    t = ev_setp.inc(POOL.memset(identf[:], 0.0))
    W("pool", t)
    t = ev_setp.inc(POOL.affine_select(out=identf[:], in_=identf[:], pattern=[[-1, 128]],
                                      compare_op=ALU.not_equal, fill=1.0, base=0, channel_multiplier=1))
    W("dve", t)
    tok_ident = ev_setv.inc(DVE.tensor_copy(out=ident[:], in_=identf[:]))
    tok_mhalf = ev_setp.inc(POOL.memset(mhalf[:], -0.5))
    ev_gv = EV("gv")
    tok_gcol = EV("gcol").inc(SP.dma_start(out=gcolT[:], in_=gmixT), dma=True)
    tok_gvec = ev_gv.inc(SP.dma_start(out=gvec[:], in_=gmix.partition_broadcast(128)), dma=True)
    DVE.memset(ssq[0][:], 1.0)
    DVE.memset(ssq[1][:], 1.0)
    late = {}

    def late_setup():
        for k_, src in enumerate([qna, kna, qnb, knb]):
            ev_setd.inc(SP.dma_start(out=gtmp[0:64, k_:k_ + 1], in_=src), dma=True)
            ev_setd.inc(SP.dma_start(out=gtmp[64:128, k_:k_ + 1], in_=src), dma=True)
        late["sinkld"] = ev_setd.inc(SP.dma_start(out=sinkexp[:], in_=sink.partition_broadcast(128)), dma=True)
        W("dve", late["sinkld"])
        DVE.scalar_tensor_tensor(out=qsA[:], in0=gtmp[:, 0:1], scalar=0.125, in1=gtmp[:, 1:2],
                                 op0=ALU.mult, op1=ALU.mult)
        late["qs"] = ev_setv.inc(DVE.scalar_tensor_tensor(out=qsB[:], in0=gtmp[:, 2:3], scalar=0.125,
                                                          in1=gtmp[:, 3:4], op0=ALU.mult, op1=ALU.mult))

    col_groups = {
        "qA": (0, 512), "kA": (512, 1024), "vA": (1024, 1536), "qB": (1536, 2048), "kvB": (2048, 2304),
        "g0": (2304, 2816), "g1": (2816, 3328), "g2": (3328, 3840), "g3": (3840, 4352),
    }
    w_in_v = w_in.rearrange("(c p) n -> p c n", p=128)
    tokW = {}

    def issue_w(names):
        for gname in names:
            a, b = col_groups[gname]
            tokW[gname] = EV("w_" + gname).inc(
                POOL.dma_start(out=Win[:, :, a:b], in_=w_in_v[:, :, a:b]), dma=True)

    issue_w(["kA", "vA", "kvB"])

    DVE.memset(VA[:, :, :, 64:65], 1.0)
    tok_vones = ev_setv.inc(DVE.memset(VB[:, :, :, 64:65], 1.0))
    ev_x = EV("xld")
    ev_a2 = EV("a2")
    ev_p = EV("pool")
    ev_a4 = EV("a4")
    ev_tx = EV("tx")
    ev_a6 = EV("a6")
    ev_g = EV("grp")
    ev_sq = EV("sq")
    ev_red = EV("red")
    ev_qn = EV("qn")
    ev_vc = EV("vcopy")
    ev_sg = EV("sig")
    ev_gst = EV("gst")
    ev_ttr = EV("ttr")
    ev_evd = EV("evd")
    ev_eva = EV("eva")

    tok_x = {}
    tok_a4 = {}
    tok_tx = {}
    tok_a6 = {}
    xt_free = [None, None]
    hb_free = [None, None]
    hT_free = [None, None]
    qn_free = [None, None]
    sq_free = [None, None]
    gsb_free = [None, None]
    bank_free = [None] * 5
    st = {"psT1_free": None, "psT2_free": None, "n": 0, "m": 0, "gm": 0}
    tok_qn_last = {}
    tok_lastgrp = {}
    PS_T1 = 0
    PS_G = 1
    PS_T2 = 6

    def is_own(e):
        return 2 <= e <= 17

    ORDER = [0, 1, 18, 19] + list(range(2, 18))
    POS = {e_: i_ for i_, e_ in enumerate(ORDER)}
    raw_free = [None, None]
    tok_rstd = {}
    tok_stat = {}
    NGB = 5

    xT_free = [None, None]
    tok_xT = {}
    OLD = set(ORDER[0:8])

    def A_load(e):
        b = POS[e] % 2
        W("sp", xt_free[b])
        tok_x[e] = EV('xld%d' % b).inc(SP.dma_start(out=xt[b][:], in_=xe[e * 128:(e + 1) * 128, :]), dma=True)
        if e in OLD:
            return
        W("sp", xT_free[b])
        tok_xT[e] = EV('xTld%d' % b).inc(SP.dma_start(
            out=xT[b][:], in_=xeT[e * 128:(e + 1) * 128, :].rearrange("p (c t) -> p c t", c=8)), dma=True)

    def A_stat(e):
        b = POS[e] % 2
        W("act", tok_x[e])
        W("act", st.get("junk_tok"))
        t2 = ev_a2.inc(ACT.activation(out=junk[:], in_=xt[b][:], func=AF.Square, accum_out=ssqx[:, e:e + 1]))
        st["junk_tok"] = t2
        xt_free[b] = t2
        W("pool", t2)
        W("pool", tok_mhalf)
        t3 = ev_p.inc(POOL.tensor_scalar(out=rsx[:, e:e + 1], in0=ssqx[:, e:e + 1], scalar1=1.0 / D, scalar2=EPS,
                                         op0=ALU.mult, op1=ALU.add))
        ev_p.inc(POOL.tensor_scalar(out=epsq[:, e:e + 1], in0=ssqx[:, e:e + 1], scalar1=EPS / D, scalar2=EPS * EPS,
                                    op0=ALU.mult, op1=ALU.add))
        W("pool", t3)
        tok_stat[e] = ev_p.inc(POOL.tensor_tensor(out=rsx[:, e:e + 1], in0=rsx[:, e:e + 1], in1=mhalf[:, 0:1],
                                                  op=ALU.pow))

    def A_scale(e):
        b = POS[e] % 2
        if e in OLD:
            W("dve", tok_x[e])
            W("dve", hb_free[b])
            W("dve", tok_gvec)
            tok_a4[e] = ev_a4.inc(DVE.tensor_tensor(out=hb[b][:], in0=xt[b][:], in1=gvec[:], op=ALU.mult))
            xt_free[b] = [xt_free[b], tok_a4[e]]
            return
        W("dve", tok_xT[e])
        W("dve", hT_free[b])
        W("dve", tok_gcol)
        tok_a4[e] = ev_a4.inc(DVE.tensor_tensor(out=hT[b][:].rearrange("p (c t) -> p c t", c=8), in0=xT[b][:],
                                                in1=gcolT[:].unsqueeze(2).to_broadcast([128, 8, 128]),
                                                op=ALU.mult))
        xT_free[b] = tok_a4[e]
        tok_a6[e] = tok_a4[e]

    def A_tx_old(e):
        b = POS[e] % 2
        if True:
            W("pe", tok_a4[e])
            W("pe", st["psT1_free"])
            W("pe", tok_ident)
            pT = bankbf(PS_T1)
            for c in range(8):
                ins = PE.transpose(out=pT[:, c * 128:(c + 1) * 128], in_=hb[b][:, c * 128:(c + 1) * 128],
                                   identity=ident[:])
            tok_tx[e] = ev_tx.inc(ins)
            hb_free[b] = tok_tx[e]
            xT_free[1] = [xT_free[1], tok_tx[e]]
            W("act", tok_tx[e])
            W("act", hT_free[b])
            tok_a6[e] = ev_a6.inc(ACT.activation(out=hT[b][:], in_=pT, func=AF.Copy))
            st["psT1_free"] = tok_a6[e]

    NORM_OFF = {"qA": 0, "kA": 512, "qB": 1024, "kvB": 1536}
    NORM_C0 = {"qA": 0, "kA": 8, "qB": 16, "kvB": 24}

    early = {"tok": None}

    def emit_rstd(e, tred_last):
        par = POS[e] % 2
        W("pool", tred_last)
        W("pool", tok_stat[e])
        tp = ev_p.inc(POOL.tensor_scalar(out=rstd[par][:, 0:26], in0=ssq[par][:, 0:26],
                                         scalar1=1.0 / 64, scalar2=epsq[:, e:e + 1], op0=ALU.mult, op1=ALU.add))
        W("pool", tp)
        tok_rstd[e] = ev_p.inc(POOL.tensor_tensor(out=rstd[par][:, 0:26], in0=rstd[par][:, 0:26],
                                                  in1=mhalf[:, 0:26], op=ALU.pow))

    def A_groups(e, mid_hook=None):
        b = POS[e] % 2
        par = POS[e] % 2
        own = is_own(e)
        last = (e == ORDER[NE - 1])
        if own and last:
            glist = ["qA", "kA", "qB", "kvB", "vA", "g0", "g1", "g2", "g3"]
        elif own:
            glist = ["qA", "g0", "kA", "g1", "vA", "g2", "qB", "g3", "kvB"]
        else:
            glist = ["kA", "vA"] + (["kvB"] if 1 <= e <= 18 else [])
        tt = e - 2
        tg_last = None
        tred_last = None
        W("act", tok_stat[e])
        hook_at = min(2, len(glist) - 1)
        for gi, gname in enumerate(glist):
            a, bb = col_groups[gname]
            w = bb - a
            n = st["n"]; st["n"] += 1
            bk = n % NGB
            pb = bank(PS_G + bk)
            W("pe", bank_free[bk])
            W("pe", tok_a6[e])
            W("pe", tokW[gname])
            for c in range(8):
                ins = PE.matmul(pb[:, 0:w], lhsT=hT[b][:, c * 128:(c + 1) * 128], rhs=Win[:, c, a:bb],
                                start=(c == 0), stop=(c == 7))
            tg_ = ev_g.inc(ins)
            tg_last = tg_
            if gname in ("qA", "kA", "qB", "kvB"):
                m = st["m"]; st["m"] += 1
                s = m % 2
                nh, wn = (2, 128) if gname == "kvB" else (8, 512)
                c0 = NORM_C0[gname]
                q0 = NORM_OFF[gname]
                W("act", tg_)
                W("act", raw_free[par])
                ACT.activation(out=raw[par][:, q0:q0 + wn], in_=pb[:, 0:wn], func=AF.Copy)
                if gname == "kvB":
                    W("act", tok_vones)
                    ACT.activation(out=VB[:, e - 1, :, 0:64],
                                   in_=pb[:, 128:256].rearrange("p (h d) -> p h d", d=64), func=AF.Copy,
                                   scale=rsx[:, e:e + 1])
                W("act", sq_free[s])
                tsq = ev_sq.inc(ACT.activation(out=sq[s][:, 0:wn], in_=pb[:, 0:wn], func=AF.Square))
                bank_free[bk] = tsq
                W("dve", tsq)
                tred = ev_red.inc(DVE.tensor_reduce(out=ssq[par][:, c0:c0 + nh],
                                                    in_=sq[s][:, 0:wn].rearrange("p (h d) -> p h d", d=64),
                                                    axis=AX.X, op=ALU.add))
                sq_free[s] = tred
                tred_last = tred
            elif gname == "vA":
                W("act", tg_)
                W("act", tok_vones)
                tv = ev_sq.inc(ACT.activation(out=VA[:, e, :, 0:64],
                                              in_=pb[:, 0:512].rearrange("p (h d) -> p h d", d=64), func=AF.Copy,
                                              scale=rsx[:, e:e + 1]))
                bank_free[bk] = tv
            else:
                k = int(gname[1])
                gm = st["gm"]; st["gm"] += 1
                s = gm % 2
                W("act", tg_)
                W("act", gsb_free[s])
                tsg = ev_sq.inc(ACT.activation(out=gsb[s][:], in_=pb[:, 0:512], func=AF.Sigmoid,
                                               scale=rsx[:, e:e + 1]))
                bank_free[bk] = tsg
                W("sp", tsg)
                gsb_free[s] = EV('gst%d' % s).inc(SP.dma_start(out=gts[tt, :, k * 512:(k + 1) * 512], in_=gsb[s][:]), dma=True)
            if gi == hook_at and mid_hook is not None:
                mid_hook()
            if last and gname == "vA":
                W("pool", tg_)
                for k_ in range(5):
                    early["tok"] = EV("tabM").inc(POOL.dma_start(out=tabM[k_][:], in_=tabA[:, k_ * 1024:(k_ + 1) * 1024]),
                                                  dma=True)
            if last and gi == 3:
                emit_rstd(e, tred_last)
            if last and gi == 6:
                A_normalize(e)
        hT_free[b] = tg_last
        tok_lastgrp[e] = tg_last
        if not last:
            emit_rstd(e, tred_last)

    def A_normalize(e):
        par = POS[e] % 2
        own = is_own(e)
        W("dve", tok_rstd[e])
        W("dve", qn_free[par])
        names = (["qA"] if own else []) + ["kA"] + (["qB"] if own else []) + (["kvB"] if 1 <= e <= 18 else [])
        tq = None
        for gname in names:
            q0 = NORM_OFF[gname]
            c0 = NORM_C0[gname]
            if gname == "qB":
                o_v = qn[par][:, 1024:1536].rearrange("p (r g d) -> p g r d", r=4, g=2, d=64)
                i_v = raw[par][:, 1024:1536].rearrange("p (g r d) -> p g r d", g=2, r=4, d=64)
                r_v = rstd[par][:, 16:24].rearrange("p (g r) -> p g r", g=2).unsqueeze(3).to_broadcast([128, 2, 4, 64])
            else:
                nh, wn = (2, 128) if gname == "kvB" else (8, 512)
                o_v = qn[par][:, q0:q0 + wn].rearrange("p (h d) -> p h d", d=64)
                i_v = raw[par][:, q0:q0 + wn].rearrange("p (h d) -> p h d", d=64)
                r_v = rstd[par][:, c0:c0 + nh].unsqueeze(2).to_broadcast([128, nh, 64])
            tq = ev_qn.inc(DVE.tensor_tensor(out=o_v, in0=i_v, in1=r_v, op=ALU.mult))
        tok_qn_last[e] = tq
        raw_free[par] = tq

    def A_ttr(e):
        par = POS[e] % 2
        own = is_own(e)
        tt = e - 2
        pT2 = bankbf(PS_T2, 2)
        W("pe", tok_qn_last[e])
        W("pe", st["psT2_free"])
        srcs = []
        if own:
            srcs += [(p, p * 128) for p in range(4)]
            srcs += [(4 + r, 1024 + r * 128) for r in range(4)]
        srcs += [(8 + p, 512 + p * 128) for p in range(4)]
        if 1 <= e <= 18:
            srcs += [(12, 1536)]
        for slot, c0 in srcs:
            ins = PE.transpose(out=pT2[:, slot * 128:(slot + 1) * 128], in_=qn[par][:, c0:c0 + 128], identity=ident[:])
        tt_ = ev_ttr.inc(ins)
        qn_free[par] = tt_
        frees = []
        if own:
            W("dve", tt_)
            W("dve", late["qs"])
            DVE.tensor_scalar(out=QAT[:, :, tt * 128:(tt + 1) * 128],
                              in0=pT2[:, 0:512].rearrange("p (a t) -> p a t", a=4),
                              scalar1=qsA[:, 0:1], scalar2=None, op0=ALU.mult)
            td = ev_evd.inc(DVE.tensor_scalar(out=QBT[:, :, tt * 128:(tt + 1) * 128],
                                              in0=pT2[:, 512:1024].rearrange("p (a t) -> p a t", a=4),
                                              scalar1=qsB[:, 0:1], scalar2=None, op0=ALU.mult))
            frees.append(td)
        W("act", tt_)
        ta = ev_eva.inc(ACT.activation(out=KAT[:, :, e * 128:(e + 1) * 128],
                                       in_=pT2[:, 1024:1536].rearrange("p (a t) -> p a t", a=4), func=AF.Copy))
        if 1 <= e <= 18:
            ta = ev_eva.inc(ACT.activation(out=KBT[:, (e - 1) * 128:e * 128], in_=pT2[:, 1536:1664], func=AF.Copy))
        frees.append(ta)
        st["psT2_free"] = frees

    A_load(ORDER[0])
    A_load(ORDER[1])
    A_stat(ORDER[0])
    issue_w(["qA", "qB", "g0", "g1", "g2", "g3"])
    A_scale(ORDER[0])
    A_stat(ORDER[1])
    A_scale(ORDER[1])
    A_load(ORDER[2])
    A_tx_old(ORDER[0])

    def mid_hook(i):
        if i >= 1:
            A_normalize(ORDER[i - 1])
        if i + 2 < NE and ORDER[i + 2] in OLD:
            A_scale(ORDER[i + 2])
        elif i + 1 < NE and ORDER[i + 1] not in OLD:
            A_scale(ORDER[i + 1])
    for i in range(NE + 1):
        if i == 1:
            late_setup()
        if i + 1 < NE and ORDER[i + 1] in OLD:
            A_tx_old(ORDER[i + 1])
        if i + 2 < NE:
            A_stat(ORDER[i + 2])
        if i < NE:
            A_groups(ORDER[i], (lambda j=i: mid_hook(j)))
        if i >= 1:
            A_ttr(ORDER[i - 1])
        if i + 3 < NE:
            A_load(ORDER[i + 3])

    p1_done = [st["psT2_free"], gsb_free[0], gsb_free[1], tok_lastgrp[ORDER[NE - 1]], tok_lastgrp[ORDER[NE - 2]]]

    W("pool", [tok_lastgrp[ORDER[NE - 1]], tok_lastgrp[ORDER[NE - 2]]])
    tok_tabB = EV("tabB").inc(POOL.dma_start(out=tabBsb[:, 0:3, :].rearrange("p a b -> p (a b)"),
                                             in_=tabB[:, 0:3 * 1024]), dma=True)
    W("pool", tok_tabB)
    tok_tabA = EV("tabA").inc(POOL.dma_start(out=tabAsb[:, 0:5, :].rearrange("p a b -> p (a b)"),
                                             in_=tabA[:, 0:5 * 1024]), dma=True)
    W("pool", tok_tabA)
    tok_tabB2 = EV("tabB2").inc(POOL.dma_start(out=tabBsb[:, 3:5, :].rearrange("p a b -> p (a b)"),
                                               in_=tabB[:, 3 * 1024:5 * 1024]), dma=True)
    W("pool", tok_tabB2)
    tok_tabE = EV("tabE").inc(POOL.dma_start(out=tabAsb[:, 5:15, :].rearrange("p a b -> p (a b)"),
                                             in_=tabA[:, 5 * 1024:15 * 1024]), dma=True)
    ev_wb = EV("wb")
    W("pool", tok_tabE)
    W("pool", p1_done)
    ev_wb.inc(POOL.dma_start(out=Wba[:], in_=w_ba.rearrange("(c p) n -> p c n", p=128)), dma=True)
    ev_wb.inc(POOL.dma_start(out=Wbb[:], in_=w_bb.rearrange("(c p) n -> p c n", p=128)), dma=True)
    tok_wb = ev_wb.inc(POOL.dma_start(out=Wout[:], in_=w_out.rearrange("(c p) n -> p c n", p=128)), dma=True)
    W("act", late["sinkld"])
    tok_sinkexp = EV("sinkexp").inc(ACT.activation(out=sinkexp[:], in_=sinkexp[:], func=AF.Exp))
    ev_te = EV("tabexp")
    tab_ready = {}

    def table_tok(kind, idx):
        key = (kind, idx)
        if key not in tab_ready:
            if kind == "M":
                W("act", early["tok"])
                v = tabM[idx][:]
            elif kind == "A":
                W("act", tok_tabA if idx < 5 else tok_tabE)
                v = tabAsb[:, idx, :]
            else:
                W("act", tok_tabB if idx < 3 else tok_tabB2)
                v = tabBsb[:, idx, :]
            tab_ready[key] = ev_te.inc(ACT.activation(out=v, in_=v, func=AF.Exp))
        return tab_ready[key]
    tabE = {"tok": None, "tokB": None}

    def exp_B_table():
        W("act", tok_tabB)
        tabE["tokB"] = ev_te.inc(ACT.activation(out=tabBsb[:], in_=tabBsb[:], func=AF.Exp))

    def exp_edge_tables():
        W("act", tok_tabE)
        for k in range(1, 3):
            v = tabAsb[:, 5 * k:5 * (k + 1), :]
            tabE["tok"] = ev_te.inc(ACT.activation(out=v, in_=v, func=AF.Exp))

    ev_S = EV("S")
    ev_add = EV("add")
    ev_exp = EV("exp")
    ev_pv = EV("pv")
    ev_na = EV("na")
    PS_S = [0, 2]
    PS_OA = 4
    PS_OB = 6
    psS_free = [None, None]
    PR_free = [None, None, None]
    PT_free = [None, None, None]
    O_free = {"A": None, "B": None}
    tok_norm = {}

    slots = []
    for t_ in list(range(2, 14)) + [0, 1, 14, 15]:
        for j in range(5):
            slots.append(("A", t_, j))
        for j in range(3):
            slots.append(("B", t_, j))

    def Oview(bk):
        return ps[:, 512 * bk:512 * (bk + 2)].rearrange("p (b c) -> p b c", b=2)[:, :, 0:260].rearrange(
            "p b (h d) -> p b h d", d=65)

    def emit_S(n):
        kind, t_, j = slots[n]
        pS = bank(PS_S[n % 2], 2)
        W("pe", psS_free[n % 2])
        if n == 0:
            W("pe", p1_done)
        for h in range(8):
            if kind == "A":
                e_ = t_ + j
                p_, hp = h // 2, (h % 2) * 64
                lhsT = KAT[hp:hp + 64, p_, e_ * 128:(e_ + 1) * 128]
                rhs = QAT[hp:hp + 64, p_, t_ * 128:(t_ + 1) * 128]
            else:
                e_ = t_ + 1 + j
                g_, r_ = h // 4, h % 4
                lhsT = KBT[g_ * 64:(g_ + 1) * 64, (e_ - 1) * 128:e_ * 128]
                rhs = QBT[g_ * 64:(g_ + 1) * 64, r_, t_ * 128:(t_ + 1) * 128]
            cb = _cbA(h) if kind == "A" else h
            ins = PE.matmul(pS[:, cb * 128:(cb + 1) * 128], lhsT=lhsT, rhs=rhs, start=True, stop=True)
        tS = ev_S.inc(ins)
        use_early = (kind == "A" and n < 5)
        if use_early:
            ttab = table_tok("M", tabA_index(t_, j))
        else:
            ttab = table_tok(kind, tabA_index(t_, j) if kind == "A" else tabB_index(t_, j))
        W("act", tS)
        W("act", PR_free[n % 3])
        tE = ev_exp.inc(ACT.activation(out=PR[n % 3][:], in_=pS, func=AF.Exp))
        psS_free[n % 2] = tE
        W("dve", tE)
        W("dve", PT_free[n % 3])
        if use_early:
            tb = tabM[tabA_index(t_, j)][:]
            W("dve", ttab)
        elif kind == "A":
            tb = tabAsb[:, tabA_index(t_, j), :]
            W("dve", ttab)
        else:
            tb = tabBsb[:, tabB_index(t_, j), :]
            W("dve", ttab)
        tA = ev_add.inc(DVE.tensor_tensor(out=PT[n % 3][:], in0=PR[n % 3][:], in1=tb, op=ALU.mult))
        PR_free[n % 3] = tA
        return tA

    def emit_PV(n, tE):
        kind, t_, j = slots[n]
        nslot = 5 if kind == "A" else 3
        bk = PS_OA if kind == "A" else PS_OB
        W("pe", tE)
        if j == 0:
            W("pe", O_free[kind])
        for h in range(8):
            if kind == "A":
                rhs = VA[:, t_ + j, h, :]
            else:
                rhs = VB[:, t_ + j, h // 4, :]
            o_ap = ps[:, 512 * (bk + h // 4) + (h % 4) * 65: 512 * (bk + h // 4) + (h % 4) * 65 + 65]
            cb = _cbA(h) if kind == "A" else h
            ins = PE.matmul(o_ap, lhsT=PT[n % 3][:, cb * 128:(cb + 1) * 128], rhs=rhs,
                            start=(j == 0 and h % 4 == 0), stop=(j == nslot - 1 and h % 4 == 3))
        tP = ev_pv.inc(ins)
        PT_free[n % 3] = tP
        if j == nslot - 1:
            ov = Oview(bk)
            W("dve", tP)
            if kind == "A":
                rd = rdenA
                t1 = ev_na.inc(DVE.reciprocal(out=rd[:].rearrange("p (b h o) -> p b h o", b=2, h=4, o=1),
                                              in_=ov[:, :, :, 64:65]))
            else:
                rd = rdenB
                W("dve", tok_sinkexp)
                t0 = ev_na.inc(DVE.tensor_tensor(out=rd[:].rearrange("p (b h o) -> p b h o", b=2, h=4, o=1),
                                                 in0=ov[:, :, :, 64:65],
                                                 in1=sinkexp[:].rearrange("p (b h o) -> p b h o", b=2, h=4, o=1),
                                                 op=ALU.add))
                W("dve", t0)
                t1 = ev_na.inc(DVE.reciprocal(out=rd[:], in_=rd[:]))
            W("dve", t1)
            c0 = 0 if kind == "A" else 512
            t2 = ev_na.inc(DVE.tensor_tensor(
                out=otok[:, t_, c0:c0 + 512].rearrange("p (b h d) -> p b h d", b=2, h=4),
                in0=ov[:, :, :, 0:64],
                in1=rd[:].rearrange("p (b h) -> p b h", b=2).unsqueeze(3).to_broadcast([128, 2, 4, 64]),
                op=ALU.mult))
            O_free[kind] = t2
            tok_norm[(kind, t_)] = t2

    toks = {}
    for n in range(len(slots)):
        toks[n] = emit_S(n)
        if n >= 2:
            emit_PV(n - 2, toks[n - 2])
    emit_PV(len(slots) - 2, toks[len(slots) - 2])
    emit_PV(len(slots) - 1, toks[len(slots) - 1])
    p2a_done = [tok_norm[("A", NT - 1)], tok_norm[("B", NT - 1)], PT_free[0], PT_free[1], PT_free[2]]

    ev_wg = EV("wg")
    w_gate_v = w_gate.rearrange("(c p) n -> p c n", p=128)
    w_up_v = w_up.rearrange("(c p) n -> p c n", p=128)
    wtok = {}

    def issue_wg_prefetch(k):
        W("pool", p2a_done)
        if k < 8:
            wtok["wg"] = ev_wg.inc(POOL.dma_start(out=Wg[:, k, :], in_=w_gate_v[:, k, :]), dma=True)
        elif k < 12:
            c0 = 2 * (k - 8)
            wtok["wu1"] = EV('wu1').inc(POOL.dma_start(out=Wu1[:, c0:c0 + 2, :],
                                                       in_=w_up_v[:, c0:c0 + 2, 0:WU_SPLIT]), dma=True)
    W("sp", p1_done)
    W("sp", tok_a4[ORDER[NE - 1]])
    tok_gvec2 = ev_gv.inc(SP.dma_start(out=gvec[:], in_=gffn.partition_broadcast(128)), dma=True)

    ev_gl = EV("gl")
    ev_xl = EV("xl2")
    ev_otr = EV("otr")
    ev_ev2 = EV("ev2")
    ev_y = EV("ymm")
    ev_cmb = EV("cmb")
    ev_ytr = EV("ytr")
    ev_x2mm = EV("x2mm")
    ev_res = EV("res")
    ev_st2 = EV("st2")
    ev_sq2 = EV("sq2")
    ev_h2 = EV("h2")
    ev_h2tr = EV("h2tr")
    PS_TB = [0, 1]
    PS_Y = [[2, 3], [4, 5]]
    PS_X = 6
    psT_free = [None, None]
    st2 = {"k": 0}
    gsh_free = [None, None]
    xt2_free = [None, None]
    ytok_free = [None, None]
    h2b_free = [None, None]
    oT_free = [None, None]
    yT_free = [None, None]
    usb_free = [None, None]
    psY_free = [None, None]
    psX_free = [None]
    tok_oT = {}
    tok_cmb = {}
    tok_yT = {}
    tok_h2 = {}
    tok_h2T = {}
    tok_x2st = {}
    tok_res = {}
    tok_gl = {}
    tok_xl = {}

    def B_loads(t_):
        b = t_ % 2
        W("sp", xt2_free[b])
        if t_ < 2:
            W("sp", p2a_done)
        tok_xl[t_] = EV('xl2_%d' % b).inc(SP.dma_start(out=xt2[b][:], in_=xe[(t_ + 2) * 128:(t_ + 3) * 128, :]), dma=True)

    def B_gload(t_, c):
        W("sp", gsh_free[c])
        if t_ == 0:
            W("sp", p2a_done)
        tok_gl[(t_, c)] = EV('gl%d' % c).inc(SP.dma_start(
            out=gsh[c][:], in_=gts[t_].rearrange("p (g n) -> p g n", g=2)[:, :, c * 512:(c + 1) * 512]), dma=True)

    def tr8(src, dst_free_tok, extra_wait):
        k = st2["k"]; st2["k"] += 1
        pT = bankbf(PS_TB[k % 2])
        W("pe", psT_free[k % 2])
        W("pe", extra_wait)
        for c in range(8):
            ins = PE.transpose(out=pT[:, c * 128:(c + 1) * 128], in_=src[:, c * 128:(c + 1) * 128], identity=ident[:])
        return k % 2, pT, ins

    def B_otr(t_):
        b = t_ % 2
        kk, pT, ins = tr8(otok[:, t_, :], None, p2a_done if t_ == 0 else None)
        tt_ = ev_otr.inc(ins)
        W("act", tt_)
        W("act", oT_free[b])
        ta_ = ev_ev2.inc(ACT.activation(out=oTb[b][:, 0:512], in_=pT[:, 0:512], func=AF.Copy))
        tb_ = ev_ev2.inc(ACT.activation(out=oTb[b][:, 512:1024], in_=pT[:, 512:1024], func=AF.Copy))
        tok_oT[t_] = (ta_, tb_)
        psT_free[kk] = tb_

    def B_ymm(t_):
        b = t_ % 2
        W("pe", tok_oT[t_][0])
        W("pe", tok_wb)
        for c in range(2):
            W("pe", psY_free[c])
            pa = bank(PS_Y[c][0])
            pbk = bank(PS_Y[c][1])
            for k in range(4):
                PE.matmul(pa, lhsT=oTb[b][:, k * 128:(k + 1) * 128], rhs=Wba[:, k, c * 512:(c + 1) * 512],
                          start=(k == 0), stop=(k == 3))
            W("pe", tok_oT[t_][1])
            for k in range(4):
                ins = PE.matmul(pbk, lhsT=oTb[b][:, (4 + k) * 128:(5 + k) * 128], rhs=Wbb[:, k, c * 512:(c + 1) * 512],
                                start=(k == 0), stop=(k == 3))
            ty = ev_y.inc(ins)
            if c == 1:
                oT_free[b] = ty
            W("dve", ty)
            W("dve", tok_gl[(t_, c)])
            W("dve", usb_free[c])
            t1 = ev_cmb.inc(DVE.tensor_tensor(out=usb[c][:], in0=pa, in1=gsh[c][:, 0, :], op=ALU.mult))
            t2 = ev_cmb.inc(DVE.tensor_tensor(out=pbk, in0=pbk, in1=gsh[c][:, 1, :], op=ALU.mult))
            gsh_free[c] = t2
            W("dve", t2)
            W("dve", ytok_free[b])
            t3 = ev_cmb.inc(DVE.tensor_tensor(out=ytok[b][:, c * 512:(c + 1) * 512], in0=pbk, in1=usb[c][:], op=ALU.add))
            usb_free[c] = t3
            psY_free[c] = t3
            tok_cmb[t_] = t3
            if t_ + 1 < NT:
                B_gload(t_ + 1, c)

    def B_ytr(t_):
        b = t_ % 2
        kk, pT, ins = tr8(ytok[b], None, tok_cmb[t_])
        tt_ = ev_ytr.inc(ins)
        ytok_free[b] = tt_
        W("act", tt_)
        W("act", yT_free[b])
        ta_ = ev_ev2.inc(ACT.activation(out=yTb[b][:, 0:512], in_=pT[:, 0:512], func=AF.Copy))
        tb_ = ev_ev2.inc(ACT.activation(out=yTb[b][:, 512:1024], in_=pT[:, 512:1024], func=AF.Copy))
        tok_yT[t_] = (ta_, tb_)
        psT_free[kk] = tb_

    def B_x2mm(t_):
        b = t_ % 2
        W("pe", tok_yT[t_][0])
        W("pe", psX_free[0])
        pX = bank(PS_X, 2)
        for c in range(2):
            for k in range(8):
                if k == 4:
                    W("pe", tok_yT[t_][1])
                ins = PE.matmul(pX[:, c * 512:(c + 1) * 512], lhsT=yTb[b][:, k * 128:(k + 1) * 128],
                                rhs=Wout[:, k, c * 512:(c + 1) * 512], start=(k == 0), stop=(k == 7))
        tx_ = ev_x2mm.inc(ins)
        yT_free[b] = tx_
        W("dve", tx_)
        W("dve", tok_xl[t_])
        tr_ = ev_res.inc(DVE.tensor_tensor(out=xt2[b][:], in0=pX, in1=xt2[b][:], op=ALU.add))
        psX_free[0] = tr_
        W("sp", tr_)
        tok_x2st[t_] = EV('st2_%d' % b).inc(SP.dma_start(out=out[t_ * 128:(t_ + 1) * 128, :], in_=xt2[b][:]), dma=True)
        tok_res[t_] = tr_

    def B_norm(t_):
        b = t_ % 2
        tr_ = tok_res[t_]
        W("act", tr_)
        ts_ = ev_sq2.inc(ACT.activation(out=junk2[:], in_=xt2[b][:], func=AF.Square, accum_out=ssq2[:, t_:t_ + 1]))
        W("pool", ts_)
        tp = ev_p.inc(POOL.tensor_scalar(out=rs2[:, t_:t_ + 1], in0=ssq2[:, t_:t_ + 1], scalar1=1.0 / D, scalar2=EPS,
                                         op0=ALU.mult, op1=ALU.add))
        W("pool", tp)
        tp = ev_p.inc(POOL.tensor_tensor(out=rs2[:, t_:t_ + 1], in0=rs2[:, t_:t_ + 1], in1=mhalf[:, 0:1], op=ALU.pow))
        issue_wg_prefetch(t_)
        W("dve", tp)
        W("dve", h2b_free[b])
        W("dve", tok_gvec2)
        tok_h2[t_] = ev_h2.inc(DVE.scalar_tensor_tensor(out=h2b[b][:], in0=xt2[b][:], scalar=rs2[:, t_:t_ + 1],
                                                        in1=gvec[:], op0=ALU.mult, op1=ALU.mult))
        xt2_free[b] = [tok_x2st[t_], tok_h2[t_]]
        if t_ + 2 < NT:
            B_loads(t_ + 2)

    def B_h2tr(t_):
        b = t_ % 2
        kk, pT, ins = tr8(h2b[b], None, tok_h2[t_])
        tt_ = ev_h2tr.inc(ins)
        h2b_free[b] = tt_
        W("act", tt_)
        tok_h2T[t_] = ev_ev2.inc(ACT.activation(out=h2T[:, :, t_ * 128:(t_ + 1) * 128],
                                                in_=pT.rearrange("p (a t) -> p a t", a=8), func=AF.Copy))
        psT_free[kk] = tok_h2T[t_]

    B_loads(0)
    B_loads(1)
    B_gload(0, 0)
    B_gload(0, 1)
    B_otr(0)
    B_ymm(0)
    for i in range(NT):
        if i + 1 < NT:
            B_otr(i + 1)
        if i >= 1:
            B_norm(i - 1)
        B_ytr(i)
        if i + 1 < NT:
            B_ymm(i + 1)
        if i == NT - 1:
            B_x2mm(i)
            B_h2tr(i - 1)
        else:
            if i >= 1:
                B_h2tr(i - 1)
            B_x2mm(i)
    tok_wg = wtok["wg"]
    tok_wu1 = wtok["wu1"]
    ev_gu = EV("gu")
    ev_silu = EV("silu")
    ev_u = EV("umul")
    PS_GU = [[2, 3], [4, 5]]
    psGU_free = [psY_free[0], psY_free[1]]
    sgb_free = [[psY_free[0], psY_free[1]], [psY_free[0], psY_free[1]]]
    p3 = {"uT_free": None, "tok_wu2": None}
    EARLY_GU = 4

    def emit_GU(u, j, tok_uT):
        s_ = j % 2
        pG = bank(PS_GU[s_][0])
        pU = bank(PS_GU[s_][1])
        W("pe", psGU_free[s_])
        W("pe", tok_wg)
        for c in range(8):
            PE.matmul(pG, lhsT=Wg[:, c, j * 128:(j + 1) * 128], rhs=h2T[:, c, u * 512:(u + 1) * 512],
                      start=(c == 0), stop=(c == 7))
        W("pe", tok_wu1 if j < 11 else p3["tok_wu2"])
        Wuj, jj = (Wu1, j) if j < 11 else (Wu2, j - 11)
        for c in range(8):
            ins = PE.matmul(pU, lhsT=Wuj[:, c, jj * 128:(jj + 1) * 128], rhs=h2T[:, c, u * 512:(u + 1) * 512],
                            start=(c == 0), stop=(c == 7))
        tgu = ev_gu.inc(ins)
        W("act", tgu)
        W("act", sgb_free[s_])
        tsl = ev_silu.inc(ACT.activation(out=sgb[s_][:], in_=pG, func=AF.Silu))
        W("dve", tsl)
        if j == 0:
            W("dve", p3["uT_free"])
        tu = ev_u.inc(DVE.tensor_tensor(out=uT[:, j, :], in0=pU, in1=sgb[s_][:], op=ALU.mult))
        sgb_free[s_] = tu
        psGU_free[s_] = tu
        tok_uT[j] = tu

    tok_uT0 = {}
    B_norm(NT - 1)
    for j in range(EARLY_GU):
        emit_GU(0, j, tok_uT0)
    B_h2tr(NT - 1)
    p2b_done = [tok_h2T[NT - 1], tok_h2T[NT - 2], tok_x2st[NT - 1], tok_x2st[NT - 2], yT_free[0], yT_free[1]]

    W("pool", p2b_done)
    tok_wu2 = EV('wu2').inc(POOL.dma_start(out=Wu2[:], in_=w_up_v[:, :, WU_SPLIT:FF]), dma=True)
    ev_wd = EV("wd")
    w_down_v = w_down.rearrange("(c p) n -> p c n", p=128)
    tok_wd = {}
    for k0 in range(0, NFC, 6):
        k1 = min(NFC, k0 + 6)
        tkn = EV('wd%d' % k0).inc(POOL.dma_start(out=Wd[:, k0:k1, :], in_=w_down_v[:, k0:k1, :]), dma=True)
        for k in range(k0, k1):
            tok_wd[k] = tkn

    ev_d = EV("dmm")
    ev_fin = EV("fin")
    PS_D = [0, 6]
    psD_free = [p2b_done, p2b_done]
    x2t_free = [None, None]
    tok_x2l = {}
    tok_ost = {}
    p3["tok_wu2"] = tok_wu2

    def C_x2load(tile):
        b = tile % 2
        W("sp", x2t_free[b])
        W("sp", p2b_done)
        W("sp", tok_x2st[tile])
        tok_x2l[tile] = EV('x2l%d' % b).inc(SP.dma_start(out=x2t[b][:], in_=out[tile * 128:(tile + 1) * 128, :]), dma=True)

    C_x2load(0)
    C_x2load(1)
    for u in range(4):
        tok_uT = tok_uT0 if u == 0 else {}
        for j in range(EARLY_GU if u == 0 else 0, NFC):
            emit_GU(u, j, tok_uT)
        for i in range(4):
            tile = 4 * u + i
            b = tile % 2
            pD = bank(PS_D[i % 2], 2)
            W("pe", psD_free[i % 2])
            for c in range(2):
                for j in range(NFC):
                    W("pe", tok_uT[j])
                    W("pe", tok_wd[j])
                    ins = PE.matmul(pD[:, c * 512:(c + 1) * 512], lhsT=uT[:, j, i * 128:(i + 1) * 128],
                                    rhs=Wd[:, j, c * 512:(c + 1) * 512], start=(j == 0), stop=(j == NFC - 1))
            td = ev_d.inc(ins)
            if i == 3:
                p3["uT_free"] = td
            W("dve", td)
            W("dve", tok_x2l[tile])
            tf = ev_fin.inc(DVE.tensor_tensor(out=x2t[b][:], in0=pD, in1=x2t[b][:], op=ALU.add))
            psD_free[i % 2] = tf
            W("sp", tf)
            tok_ost[tile] = EV('ost%d' % b).inc(SP.dma_start(out=out[tile * 128:(tile + 1) * 128, :], in_=x2t[b][:]), dma=True)
            x2t_free[b] = tok_ost[tile]
            if tile + 2 < NT:
                C_x2load(tile + 2)
    W("sp", tok_ost[NT - 1])
    W("sp", tok_ost[NT - 2])
    return nc


_CACHE = {}


def _host_inputs(x, norm_mix, w_in, q_norm_a, k_norm_a, rpb_a, q_norm_b, k_norm_b, sink_b, t5_table,
                 w_branch_a, w_branch_b, w_out, norm_ffn, w_gate, w_up, w_down):
    f = lambda a: np.ascontiguousarray(np.asarray(a, dtype=np.float32))
    x = f(x)
    shared = {
        "w_in": f(w_in[0]), "gmix": f(norm_mix[0]).reshape(1, D),
        "gmixT": np.ascontiguousarray(f(norm_mix[0]).reshape(8, 128).T), "gffn": f(norm_ffn[0]).reshape(1, D),
        "qna": f(q_norm_a[0]).reshape(64, 1), "kna": f(k_norm_a[0]).reshape(64, 1),
        "qnb": f(q_norm_b[0]).reshape(64, 1), "knb": f(k_norm_b[0]).reshape(64, 1),
        "sink": f(sink_b[0]).reshape(1, 8),
        "w_ba": f(w_branch_a[0]), "w_bb": f(w_branch_b[0]), "w_out": f(w_out[0]),
        "w_gate": f(w_gate[0]), "w_up": f(w_up[0]), "w_down": f(w_down[0]),
    }
    rpb = f(rpb_a[0])
    t5 = f(t5_table)
    tabAs = [_build_tabA(rpb, s) for s in range(4)]
    tabBs = [_build_tabB(t5, s) for s in range(4)]
    in_maps = []
    for c in range(NCORES):
        b, s = c // 4, c % 4
        xe = np.zeros((NE * 128, D), dtype=np.float32)
        for e in range(NE):
            r0 = _ext_rows(s, e)
            if r0 is None:
                continue
            xe[e * 128:(e + 1) * 128] = x[b, r0 * 64:r0 * 64 + 128]
        m = dict(shared)
        m["xe"] = xe
        m["xeT"] = np.ascontiguousarray(xe.reshape(NE, 128, 8, 128).transpose(0, 3, 2, 1).reshape(NE * 128, D))
        m["tabA"] = tabAs[s]
        m["tabB"] = tabBs[s]
        in_maps.append(m)
    return in_maps


def kernel(**inputs):
    if "nc" not in _CACHE:
        _CACHE["nc"] = build_program()
    nc = _CACHE["nc"]
    in_maps = _host_inputs(**inputs)
    res = run_bass_kernel_spmd(nc, in_maps, core_ids=list(range(NCORES)))
    outp = np.empty((2, T, D), dtype=np.float32)
    for c in range(NCORES):
        b, s = c // 4, c % 4
        outp[b, s * TOK:(s + 1) * TOK] = res.results[c]["out"]
    return outp
```

```python
import numpy as np
import concourse.bass as bass
import concourse.mybir as mybir
from concourse.bass_utils import run_bass_kernel_spmd

F32 = mybir.dt.float32
BF16 = mybir.dt.bfloat16
AF = mybir.ActivationFunctionType
ALU = mybir.AluOpType
AX = mybir.AxisListType

NCORES = 8
D = 1024
T = 8192
TOK = 2048
NT = 16
NE = 20
FF = 2816
NFC = 22
NEG = -30000.0
EPS = 1e-6

MYBASE = 17920
SB_TOP = 229376
TOTAL = SB_TOP - MYBASE


def _t5_bucket(rel):
    half = 16
    max_exact = 8
    ret = (rel > 0).astype(np.int32) * half
    n = np.abs(rel)
    large = max_exact + (np.log(np.maximum(n, 1) / max_exact)
                         / np.log(128 / max_exact) * (half - max_exact)).astype(np.int32)
    large = np.minimum(large, half - 1)
    return ret + np.where(n < max_exact, n, large)


_HPERM_A = [0, 2, 4, 6, 1, 3, 5, 7]


def _cbA(h):
    return (h % 2) * 4 + h // 2


def tabA_index(t, j):
    if t == 0:
        return {0: 5, 1: 6, 4: 7}.get(j, j)
    if t == 1:
        return {0: 8, 4: 9}.get(j, j)
    if t == 14:
        return {0: 10, 4: 11}.get(j, j)
    if t == 15:
        return {0: 12, 3: 13, 4: 14}.get(j, j)
    return j


def tabB_index(t, j):
    if t == 0 and j == 0:
        return 3
    if t == 15 and j == 2:
        return 4
    return j


def _ext_rows(s, e):
    if s == 0 and e == 0:
        return 6
    if s == 0 and e == 1:
        return None
    if s == 3 and e == 18:
        return None
    if s == 3 and e == 19:
        return 120
    return 32 * s - 4 + 2 * e


def _build_tabA(rpb, s):
    out = np.full((15, 128, 8, 128), NEG, dtype=np.float32)
    kl = np.arange(128)
    krl, kc = kl // 64, kl % 64
    ql = np.arange(128)
    qrl, qc = ql // 64, ql % 64
    ws = np.clip(qc - 8, 0, 48)
    colok = (kc[:, None] >= ws[None, :]) & (kc[:, None] < ws[None, :] + 16)
    dc = np.clip(kc[:, None] - qc[None, :], -15, 15) + 15
    reps = {}
    for t in range(16):
        for j in range(5):
            idx = tabA_index(t, j)
            if idx >= 5 or (t == 5):
                reps[idx] = (t, j)
    for idx, (t, j) in reps.items():
        k0 = _ext_rows(s, t + j)
        if k0 is None:
            continue
        q0 = 32 * s + 2 * t
        kr = k0 + krl
        qr = q0 + qrl
        rs = np.clip(qr - 4, 0, 120)
        rowok = (kr[:, None] >= rs[None, :]) & (kr[:, None] < rs[None, :] + 8)
        dr = kr[:, None] - qr[None, :] + 7
        ok = rowok & colok
        drc = np.clip(dr, 0, 14)
        vals = rpb[drc, dc, :]
        vals = np.where(ok[:, :, None], vals, np.float32(NEG))
        out[idx] = np.transpose(vals, (0, 2, 1))[:, _HPERM_A, :]
    return np.ascontiguousarray(np.transpose(out, (1, 0, 2, 3)).reshape(128, 15 * 1024))


def _build_tabB(t5, s):
    out = np.full((5, 128, 8, 128), NEG, dtype=np.float32)
    k = np.arange(128)[:, None]
    q = np.arange(128)[None, :]
    for jb in range(3):
        rel = (jb - 1) * 128 + k - q
        ok = np.abs(rel) <= 128
        vals = t5[_t5_bucket(rel), :]
        vals = np.where(ok[:, :, None], vals, np.float32(NEG))
        out[jb] = np.transpose(vals, (0, 2, 1))
    if s != 0:
        out[3] = out[0]
    if s != 3:
        out[4] = out[2]
    return np.ascontiguousarray(np.transpose(out, (1, 0, 2, 3)).reshape(128, 5 * 1024))


class _Ev:
    def __init__(self, nc, name):
        self.sem = nc.alloc_semaphore(name)
        self.n = 0

    def inc(self, ins, dma=False):
        k = 16 if dma else 1
        ins.then_inc(self.sem, k)
        self.n += k
        return (self, self.n)


def build_program():
    nc = bass.Bass("TRN2", target_bir_lowering=False)
    PE, DVE, ACT, POOL, SP = nc.tensor, nc.vector, nc.scalar, nc.gpsimd, nc.sync
    eng = {"pe": PE, "dve": DVE, "act": ACT, "pool": POOL, "sp": SP}
    waited = {}

    def W(e, tok):
        if tok is None:
            return
        if isinstance(tok, list):
            for x in tok:
                W(e, x)
            return
        ev, val = tok
        key = (e, id(ev))
        if waited.get(key, 0) >= val:
            return
        waited[key] = val
        eng[e].wait_ge(ev.sem, val)

    evs = {}

    def EV(name):
        if name not in evs:
            evs[name] = _Ev(nc, name)
        return evs[name]

    def din(name, shape, dt=F32):
        return nc.dram_tensor(name, list(shape), dt, kind="ExternalInput").ap()

    xe = din("xe", [NE * 128, D])
    xeT = din("xeT", [NE * 128, D])
    gmixT = din("gmixT", [128, 8])
    w_in = din("w_in", [D, 4352])
    gmix = din("gmix", [1, D])
    gffn = din("gffn", [1, D])
    qna = din("qna", [64, 1])
    kna = din("kna", [64, 1])
    qnb = din("qnb", [64, 1])
    knb = din("knb", [64, 1])
    sink = din("sink", [1, 8])
    tabA = din("tabA", [128, 15 * 1024])
    tabB = din("tabB", [128, 5 * 1024])
    w_ba = din("w_ba", [512, D])
    w_bb = din("w_bb", [512, D])
    w_out = din("w_out", [D, D])
    w_gate = din("w_gate", [D, FF])
    w_up = din("w_up", [D, FF])
    w_down = din("w_down", [FF, D])
    out = nc.dram_tensor("out", [TOK, D], F32, kind="ExternalOutput").ap()
    gts = nc.dram_tensor("gts", [NT, 128, 2048], BF16).ap()

    def at(name, shape, dt, rel):
        nbytes = int(np.prod(shape[1:])) * (4 if dt == F32 else 2)
        assert rel % 32 == 0, (name, rel)
        assert rel + nbytes <= TOTAL, (name, rel, nbytes, TOTAL)
        return nc.alloc_sbuf_tensor_at(name, list(shape), dt, offset=MYBASE + rel)

    QAT = at("QAT", [128, 4, TOK], BF16, 0)
    KAT = at("KAT", [128, 4, NE * 128], BF16, 16384)
    VA = at("VA", [128, NE, 8, 65], BF16, 36864)
    QBT = at("QBT", [128, 4, TOK], BF16, 57696)
    KBT = at("KBT", [128, 18 * 128], BF16, 74080)
    VB = at("VB", [128, 18, 2, 65], BF16, 78688)
    QKV_END = 83392
    Win = at("Win", [128, 8, 4352], BF16, QKV_END)
    W1 = QKV_END + 69632
    CB = TOTAL - 5152
    gvec = at("gvec", [128, D], F32, CB)
    ident = at("ident", [128, 128], BF16, CB + 4096)
    SM = CB + 4096 + 256
    qsA = at("qsA", [128, 1], F32, SM)
    qsB = at("qsB", [128, 1], F32, SM + 32)
    gtmp = at("gtmp", [128, 4], F32, SM + 64)
    sinkexp = at("sinkexp", [128, 8], F32, SM + 96)
    mhalf = at("mhalf", [128, 32], F32, SM + 128)
    ssqx = at("ssqx", [128, 20], F32, SM + 256)
    rsx = at("rsx", [128, 20], F32, SM + 352)
    ssq2 = at("ssq2", [128, 16], F32, SM + 448)
    rs2 = at("rs2", [128, 16], F32, SM + 512)
    rdenA = at("rdenA", [128, 8], F32, SM + 576)
    rdenB = at("rdenB", [128, 8], F32, SM + 608)
    epsq = at("epsq", [128, 20], F32, SM + 640)
    gcolT = at("gcolT", [128, 8], F32, SM + 736)
    assert SM + 768 <= TOTAL

    o = W1
    xt = [at("xt%d" % i, [128, D], F32, o + 4096 * i) for i in range(2)]; o += 8192
    xT = [at("xT%d" % i, [128, 8, 128], F32, o + 4096 * i) for i in range(2)]
    hb = [at("hb%d" % i, [128, D], BF16, o + 4096 + 2048 * i) for i in range(2)]; o += 8192
    raw = [at("raw%d" % i, [128, 1664], F32, o + 6656 * i) for i in range(2)]; o += 13312
    junk = at("junk", [128, D], BF16, o); o += 2048
    hT = [at("hT%d" % i, [128, D], BF16, o + 2048 * i) for i in range(2)]; o += 4096
    sq = [at("sq%d" % i, [128, 512], F32, o + 2048 * i) for i in range(2)]; o += 4096
    qn = [at("qn%d" % i, [128, 1664], BF16, o + 3328 * i) for i in range(2)]; o += 6656
    gsb = [at("gsb%d" % i, [128, 512], BF16, o + 1024 * i) for i in range(2)]; o += 2048
    ssq = [at("ssq%d" % i, [128, 32], F32, o + 128 * i) for i in range(2)]; o += 256
    rstd = [at("rstd%d" % i, [128, 32], F32, o + 128 * i) for i in range(2)]; o += 256
    identf = at("identf", [128, 128], F32, o); o += 512
    assert o <= CB

    tabS = at("tabS", [128, 20, 1024], BF16, QKV_END)
    E_SLOTS = [0, 1, 5, 9, 13, 14, 17, 18]
    L_SLOTS = [2, 3, 4, 6, 7, 8, 10, 11, 12, 15, 16, 19]
    for k_ in E_SLOTS:
        c_ = (k_ * 2048) // 8704
        assert c_ * 8704 <= k_ * 2048 and (k_ + 1) * 2048 <= c_ * 8704 + 4608, k_
    I_TIDS = [0, 1, 2, 3, 4, 15, 16, 17]
    O_TIDS = list(range(5, 15)) + [18, 19]
    SLOT = {tid: E_SLOTS[i] for i, tid in enumerate(I_TIDS)}
    SLOT.update({tid: L_SLOTS[i] for i, tid in enumerate(O_TIDS)})

    def tab_src(tid):
        return tabA[:, tid * 1024:(tid + 1) * 1024] if tid < 15 else tabB[:, (tid - 15) * 1024:(tid - 14) * 1024]
    OTOK = QKV_END + 40960
    otok = at("otok", [128, NT, 1024], BF16, OTOK)
    WB = OTOK + 32768
    Wba = at("Wba", [128, 4, D], BF16, WB)
    Wbb = at("Wbb", [128, 4, D], BF16, WB + 8192)
    Wout = at("Wout", [128, 8, D], BF16, WB + 16384)
    W2 = WB + 32768
    PR = [at("PR%d" % i, [128, 1024], BF16, W2 + 2048 * i) for i in range(3)]
    PT = [at("PT%d" % i, [128, 1024], BF16, W2 + 8192 + 2048 * i) for i in range(3)]
    assert W2 + 8192 + 6144 <= CB

    h2T = at("h2T", [128, 8, TOK], BF16, 0)
    Wg = at("Wg", [128, 8, FF], BF16, 32768)
    Wu1 = at("Wu1", [128, 8, 1408], BF16, 77824)
    Wu2 = at("Wu2", [128, 8, 1408], BF16, 100352)
    WU_SPLIT = 1408
    X2B = 77824 + 8 * WU_SPLIT * 2
    o = X2B
    xt2 = [at("xt2_%d" % i, [128, D], F32, o + 4096 * i) for i in range(2)]; o += 8192
    ytok = [at("ytok%d" % i, [128, D], BF16, o + 2048 * i) for i in range(2)]; o += 4096
    h2b = [at("h2b%d" % i, [128, D], BF16, o + 2048 * i) for i in range(2)]; o += 4096
    oTb = [at("oTb%d" % i, [128, D], BF16, o + 2048 * i) for i in range(2)]; o += 4096
    junk2 = at("junk2", [128, D], BF16, o); o += 2048
    assert o <= 122880
    o = W2
    yTb = [at("yTb%d" % i, [128, D], BF16, o + 2048 * i) for i in range(2)]; o += 4096
    usb = [at("usb%d" % i, [128, 512], F32, o + 2048 * i) for i in range(2)]; o += 4096
    gsh = [at("gsh%d" % i, [128, 2, 512], BF16, o + 2048 * i) for i in range(2)]; o += 4096
    assert o <= CB
    Wd = at("Wd", [128, NFC, D], BF16, 122880)
    uT = at("uT", [128, NFC, 512], BF16, 167936)
    o = 190464
    x2t = [at("x2t%d" % i, [128, D], F32, o + 4096 * i) for i in range(2)]; o += 8192
    sgb = [at("sgb%d" % i, [128, 512], F32, o + 2048 * i) for i in range(2)]; o += 4096
    assert o <= CB + 4096

    ps = nc.alloc_psum_tensor("ps", [128, 4096], F32)

    def bank(k, n=1):
        return ps[:, 512 * k:512 * (k + n)]

    def bankbf(k, n=1):
        return ps[:, 512 * k:512 * (k + n)].bitcast(BF16)

    ev_setp = EV("setp")
    ev_setv = EV("setv")
    ev_setd = EV("setd")
    t = ev_setp.inc(POOL.memset(identf[:], 0.0))
    W("pool", t)
    t = ev_setp.inc(POOL.affine_select(out=identf[:], in_=identf[:], pattern=[[-1, 128]],
                                      compare_op=ALU.not_equal, fill=1.0, base=0, channel_multiplier=1))
    W("dve", t)
    tok_ident = ev_setv.inc(DVE.tensor_copy(out=ident[:], in_=identf[:]))
    tok_mhalf = ev_setp.inc(POOL.memset(mhalf[:], -0.5))
    ev_gv = EV("gv")
    tok_gcol = EV("gcol").inc(SP.dma_start(out=gcolT[:], in_=gmixT), dma=True)
    tok_gvec = ev_gv.inc(SP.dma_start(out=gvec[:], in_=gmix.partition_broadcast(128)), dma=True)
    DVE.memset(ssq[0][:], 1.0)
    DVE.memset(ssq[1][:], 1.0)
    late = {}

    def late_setup():
        for k_, src in enumerate([qna, kna, qnb, knb]):
            ev_setd.inc(SP.dma_start(out=gtmp[0:64, k_:k_ + 1], in_=src), dma=True)
            ev_setd.inc(SP.dma_start(out=gtmp[64:128, k_:k_ + 1], in_=src), dma=True)
        late["sinkld"] = ev_setd.inc(SP.dma_start(out=sinkexp[:], in_=sink.partition_broadcast(128)), dma=True)
        W("dve", late["sinkld"])
        DVE.scalar_tensor_tensor(out=qsA[:], in0=gtmp[:, 0:1], scalar=0.125, in1=gtmp[:, 1:2],
                                 op0=ALU.mult, op1=ALU.mult)
        late["qs"] = ev_setv.inc(DVE.scalar_tensor_tensor(out=qsB[:], in0=gtmp[:, 2:3], scalar=0.125,
                                                          in1=gtmp[:, 3:4], op0=ALU.mult, op1=ALU.mult))

    col_groups = {
        "qA": (0, 512), "kA": (512, 1024), "vA": (1024, 1536), "qB": (1536, 2048), "kvB": (2048, 2304),
        "g0": (2304, 2816), "g1": (2816, 3328), "g2": (3328, 3840), "g3": (3840, 4352),
    }
    w_in_v = w_in.rearrange("(c p) n -> p c n", p=128)
    tokW = {}

    def issue_w(names):
        for gname in names:
            a, b = col_groups[gname]
            tokW[gname] = EV("w_" + gname).inc(
                POOL.dma_start(out=Win[:, :, a:b], in_=w_in_v[:, :, a:b]), dma=True)

    issue_w(["kA", "vA", "kvB"])

    DVE.memset(VA[:, :, :, 64:65], 1.0)
    tok_vones = ev_setv.inc(DVE.memset(VB[:, :, :, 64:65], 1.0))
    ev_x = EV("xld")
    ev_a2 = EV("a2")
    ev_p = EV("pool")
    ev_a4 = EV("a4")
    ev_tx = EV("tx")
    ev_a6 = EV("a6")
    ev_g = EV("grp")
    ev_sq = EV("sq")
    ev_red = EV("red")
    ev_qn = EV("qn")
    ev_vc = EV("vcopy")
    ev_sg = EV("sig")
    ev_gst = EV("gst")
    ev_ttr = EV("ttr")
    ev_evd = EV("evd")
    ev_eva = EV("eva")

    tok_x = {}
    tok_a4 = {}
    tok_tx = {}
    tok_a6 = {}
    xt_free = [None, None]
    hb_free = [None, None]
    hT_free = [None, None]
    qn_free = [None, None]
    sq_free = [None, None]
    gsb_free = [None, None]
    bank_free = [None] * 5
    st = {"psT1_free": None, "psT2_free": None, "n": 0, "m": 0, "gm": 0}
    tok_qn_last = {}
    tok_lastgrp = {}
    PS_T1 = 0
    PS_G = 1
    PS_T2 = 6

    def is_own(e):
        return 2 <= e <= 17

    ORDER = [0, 1, 18, 19] + list(range(2, 18))
    POS = {e_: i_ for i_, e_ in enumerate(ORDER)}
    raw_free = [None, None]
    tok_rstd = {}
    tok_stat = {}
    NGB = 5

    xT_free = [None, None]
    tok_xT = {}
    OLD = set(ORDER[0:8])

    def A_load(e):
        b = POS[e] % 2
        W("sp", xt_free[b])
        tok_x[e] = EV('xld%d' % b).inc(SP.dma_start(out=xt[b][:], in_=xe[e * 128:(e + 1) * 128, :]), dma=True)
        if e in OLD:
            return
        W("sp", xT_free[b])
        tok_xT[e] = EV('xTld%d' % b).inc(SP.dma_start(
            out=xT[b][:], in_=xeT[e * 128:(e + 1) * 128, :].rearrange("p (c t) -> p c t", c=8)), dma=True)

    def A_stat(e):
        b = POS[e] % 2
        W("act", tok_x[e])
        W("act", st.get("junk_tok"))
        t2 = ev_a2.inc(ACT.activation(out=junk[:], in_=xt[b][:], func=AF.Square, accum_out=ssqx[:, e:e + 1]))
        st["junk_tok"] = t2
        xt_free[b] = t2
        W("pool", t2)
        W("pool", tok_mhalf)
        t3 = ev_p.inc(POOL.tensor_scalar(out=rsx[:, e:e + 1], in0=ssqx[:, e:e + 1], scalar1=1.0 / D, scalar2=EPS,
                                         op0=ALU.mult, op1=ALU.add))
        ev_p.inc(POOL.tensor_scalar(out=epsq[:, e:e + 1], in0=ssqx[:, e:e + 1], scalar1=EPS / D, scalar2=EPS * EPS,
                                    op0=ALU.mult, op1=ALU.add))
        W("pool", t3)
        tok_stat[e] = ev_p.inc(POOL.tensor_tensor(out=rsx[:, e:e + 1], in0=rsx[:, e:e + 1], in1=mhalf[:, 0:1],
                                                  op=ALU.pow))

    def A_scale(e):
        b = POS[e] % 2
        if e in OLD:
            W("dve", tok_x[e])
            W("dve", hb_free[b])
            W("dve", tok_gvec)
            tok_a4[e] = ev_a4.inc(DVE.tensor_tensor(out=hb[b][:], in0=xt[b][:], in1=gvec[:], op=ALU.mult))
            xt_free[b] = [xt_free[b], tok_a4[e]]
            return
        W("dve", tok_xT[e])
        W("dve", hT_free[b])
        W("dve", tok_gcol)
        tok_a4[e] = ev_a4.inc(DVE.tensor_tensor(out=hT[b][:].rearrange("p (c t) -> p c t", c=8), in0=xT[b][:],
                                                in1=gcolT[:].unsqueeze(2).to_broadcast([128, 8, 128]),
                                                op=ALU.mult))
        xT_free[b] = tok_a4[e]
        tok_a6[e] = tok_a4[e]

    def A_tx_old(e):
        b = POS[e] % 2
        if True:
            W("pe", tok_a4[e])
            W("pe", st["psT1_free"])
            W("pe", tok_ident)
            pT = bankbf(PS_T1)
            for c in range(8):
                ins = PE.transpose(out=pT[:, c * 128:(c + 1) * 128], in_=hb[b][:, c * 128:(c + 1) * 128],
                                   identity=ident[:])
            tok_tx[e] = ev_tx.inc(ins)
            hb_free[b] = tok_tx[e]
            xT_free[1] = [xT_free[1], tok_tx[e]]
            W("act", tok_tx[e])
            W("act", hT_free[b])
            tok_a6[e] = ev_a6.inc(ACT.activation(out=hT[b][:], in_=pT, func=AF.Copy))
            st["psT1_free"] = tok_a6[e]

    NORM_OFF = {"qA": 0, "kA": 512, "qB": 1024, "kvB": 1536}
    NORM_C0 = {"qA": 0, "kA": 8, "qB": 16, "kvB": 24}

    early = {"tok": None}

    def emit_rstd(e, tred_last):
        par = POS[e] % 2
        W("pool", tred_last)
        W("pool", tok_stat[e])
        tp = ev_p.inc(POOL.tensor_scalar(out=rstd[par][:, 0:26], in0=ssq[par][:, 0:26],
                                         scalar1=1.0 / 64, scalar2=epsq[:, e:e + 1], op0=ALU.mult, op1=ALU.add))
        W("pool", tp)
        tok_rstd[e] = ev_p.inc(POOL.tensor_tensor(out=rstd[par][:, 0:26], in0=rstd[par][:, 0:26],
                                                  in1=mhalf[:, 0:26], op=ALU.pow))

    def A_groups(e, mid_hook=None):
        b = POS[e] % 2
        par = POS[e] % 2
        own = is_own(e)
        last = (e == ORDER[NE - 1])
        if own and last:
            glist = ["qA", "kA", "qB", "kvB", "vA", "g0", "g1", "g2", "g3"]
        elif own:
            glist = ["qA", "g0", "kA", "g1", "vA", "g2", "qB", "g3", "kvB"]
        else:
            glist = ["kA", "vA"] + (["kvB"] if 1 <= e <= 18 else [])
        tt = e - 2
        tg_last = None
        tred_last = None
        W("act", tok_stat[e])
        hook_at = min(2, len(glist) - 1)
        for gi, gname in enumerate(glist):
            a, bb = col_groups[gname]
            w = bb - a
            n = st["n"]; st["n"] += 1
            bk = n % NGB
            pb = bank(PS_G + bk)
            W("pe", bank_free[bk])
            W("pe", tok_a6[e])
            W("pe", tokW[gname])
            for c in range(8):
                ins = PE.matmul(pb[:, 0:w], lhsT=hT[b][:, c * 128:(c + 1) * 128], rhs=Win[:, c, a:bb],
                                start=(c == 0), stop=(c == 7))
            tg_ = ev_g.inc(ins)
            tg_last = tg_
            if gname in ("qA", "kA", "qB", "kvB"):
                m = st["m"]; st["m"] += 1
                s = m % 2
                nh, wn = (2, 128) if gname == "kvB" else (8, 512)
                c0 = NORM_C0[gname]
                q0 = NORM_OFF[gname]
                W("act", tg_)
                W("act", raw_free[par])
                ACT.activation(out=raw[par][:, q0:q0 + wn], in_=pb[:, 0:wn], func=AF.Copy)
                if gname == "kvB":
                    W("act", tok_vones)
                    ACT.activation(out=VB[:, e - 1, :, 0:64],
                                   in_=pb[:, 128:256].rearrange("p (h d) -> p h d", d=64), func=AF.Copy,
                                   scale=rsx[:, e:e + 1])
                W("act", sq_free[s])
                tsq = ev_sq.inc(ACT.activation(out=sq[s][:, 0:wn], in_=pb[:, 0:wn], func=AF.Square))
                bank_free[bk] = tsq
                W("dve", tsq)
                tred = ev_red.inc(DVE.tensor_reduce(out=ssq[par][:, c0:c0 + nh],
                                                    in_=sq[s][:, 0:wn].rearrange("p (h d) -> p h d", d=64),
                                                    axis=AX.X, op=ALU.add))
                sq_free[s] = tred
                tred_last = tred
            elif gname == "vA":
                W("act", tg_)
                W("act", tok_vones)
                tv = ev_sq.inc(ACT.activation(out=VA[:, e, :, 0:64],
                                              in_=pb[:, 0:512].rearrange("p (h d) -> p h d", d=64), func=AF.Copy,
                                              scale=rsx[:, e:e + 1]))
                bank_free[bk] = tv
            else:
                k = int(gname[1])
                gm = st["gm"]; st["gm"] += 1
                s = gm % 2
                W("act", tg_)
                W("act", gsb_free[s])
                tsg = ev_sq.inc(ACT.activation(out=gsb[s][:], in_=pb[:, 0:512], func=AF.Sigmoid,
                                               scale=rsx[:, e:e + 1]))
                bank_free[bk] = tsg
                W("sp", tsg)
                gsb_free[s] = EV('gst%d' % s).inc(SP.dma_start(out=gts[tt, :, k * 512:(k + 1) * 512], in_=gsb[s][:]), dma=True)
            if gi == hook_at and mid_hook is not None:
                mid_hook()
            if last and gname == "vA":
                W("pool", tg_)
                for tid in I_TIDS:
                    early["tok"] = EV("tabEarly").inc(POOL.dma_start(out=tabS[:, SLOT[tid], :], in_=tab_src(tid)),
                                                      dma=True)
            if last and gi == 3:
                emit_rstd(e, tred_last)
            if last and gi == 6:
                A_normalize(e)
        hT_free[b] = tg_last
        tok_lastgrp[e] = tg_last
        if not last:
            emit_rstd(e, tred_last)

    def A_normalize(e):
        par = POS[e] % 2
        own = is_own(e)
        W("dve", tok_rstd[e])
        W("dve", qn_free[par])
        names = (["qA"] if own else []) + ["kA"] + (["qB"] if own else []) + (["kvB"] if 1 <= e <= 18 else [])
        tq = None
        for gname in names:
            q0 = NORM_OFF[gname]
            c0 = NORM_C0[gname]
            if gname == "qB":
                o_v = qn[par][:, 1024:1536].rearrange("p (r g d) -> p g r d", r=4, g=2, d=64)
                i_v = raw[par][:, 1024:1536].rearrange("p (g r d) -> p g r d", g=2, r=4, d=64)
                r_v = rstd[par][:, 16:24].rearrange("p (g r) -> p g r", g=2).unsqueeze(3).to_broadcast([128, 2, 4, 64])
            else:
                nh, wn = (2, 128) if gname == "kvB" else (8, 512)
                o_v = qn[par][:, q0:q0 + wn].rearrange("p (h d) -> p h d", d=64)
                i_v = raw[par][:, q0:q0 + wn].rearrange("p (h d) -> p h d", d=64)
                r_v = rstd[par][:, c0:c0 + nh].unsqueeze(2).to_broadcast([128, nh, 64])
            tq = ev_qn.inc(DVE.tensor_tensor(out=o_v, in0=i_v, in1=r_v, op=ALU.mult))
        tok_qn_last[e] = tq
        raw_free[par] = tq

    def A_ttr(e):
        par = POS[e] % 2
        own = is_own(e)
        tt = e - 2
        pT2 = bankbf(PS_T2, 2)
        W("pe", tok_qn_last[e])
        W("pe", st["psT2_free"])
        srcs = []
        if own:
            srcs += [(p, p * 128) for p in range(4)]
            srcs += [(4 + r, 1024 + r * 128) for r in range(4)]
        srcs += [(8 + p, 512 + p * 128) for p in range(4)]
        if 1 <= e <= 18:
            srcs += [(12, 1536)]
        for slot, c0 in srcs:
            ins = PE.transpose(out=pT2[:, slot * 128:(slot + 1) * 128], in_=qn[par][:, c0:c0 + 128], identity=ident[:])
        tt_ = ev_ttr.inc(ins)
        qn_free[par] = tt_
        frees = []
        if own:
            W("dve", tt_)
            W("dve", late["qs"])
            DVE.tensor_scalar(out=QAT[:, :, tt * 128:(tt + 1) * 128],
                              in0=pT2[:, 0:512].rearrange("p (a t) -> p a t", a=4),
                              scalar1=qsA[:, 0:1], scalar2=None, op0=ALU.mult)
            td = ev_evd.inc(DVE.tensor_scalar(out=QBT[:, :, tt * 128:(tt + 1) * 128],
                                              in0=pT2[:, 512:1024].rearrange("p (a t) -> p a t", a=4),
                                              scalar1=qsB[:, 0:1], scalar2=None, op0=ALU.mult))
            frees.append(td)
        W("act", tt_)
        ta = ev_eva.inc(ACT.activation(out=KAT[:, :, e * 128:(e + 1) * 128],
                                       in_=pT2[:, 1024:1536].rearrange("p (a t) -> p a t", a=4), func=AF.Copy))
        if 1 <= e <= 18:
            ta = ev_eva.inc(ACT.activation(out=KBT[:, (e - 1) * 128:e * 128], in_=pT2[:, 1536:1664], func=AF.Copy))
        frees.append(ta)
        st["psT2_free"] = frees

    A_load(ORDER[0])
    A_load(ORDER[1])
    A_stat(ORDER[0])
    issue_w(["qA", "qB", "g0", "g1", "g2", "g3"])
    A_scale(ORDER[0])
    A_stat(ORDER[1])
    A_scale(ORDER[1])
    A_load(ORDER[2])
    A_tx_old(ORDER[0])

    def mid_hook(i):
        if i >= 1:
            A_normalize(ORDER[i - 1])
        if i + 2 < NE and ORDER[i + 2] in OLD:
            A_scale(ORDER[i + 2])
        elif i + 1 < NE and ORDER[i + 1] not in OLD:
            A_scale(ORDER[i + 1])
    for i in range(NE + 1):
        if i == 1:
            late_setup()
        if i + 1 < NE and ORDER[i + 1] in OLD:
            A_tx_old(ORDER[i + 1])
        if i + 2 < NE:
            A_stat(ORDER[i + 2])
        if i < NE:
            A_groups(ORDER[i], (lambda j=i: mid_hook(j)))
        if i >= 1:
            A_ttr(ORDER[i - 1])
        if i + 3 < NE:
            A_load(ORDER[i + 3])

    p1_done = [st["psT2_free"], gsb_free[0], gsb_free[1], tok_lastgrp[ORDER[NE - 1]], tok_lastgrp[ORDER[NE - 2]]]

    W("pool", [tok_lastgrp[ORDER[NE - 1]], tok_lastgrp[ORDER[NE - 2]]])
    for tid in O_TIDS:
        tok_late = EV("tabLate").inc(POOL.dma_start(out=tabS[:, SLOT[tid], :], in_=tab_src(tid)), dma=True)
    tok_tabE = tok_late
    ev_wb = EV("wb")
    W("pool", tok_tabE)
    W("pool", p1_done)
    ev_wb.inc(POOL.dma_start(out=Wba[:], in_=w_ba.rearrange("(c p) n -> p c n", p=128)), dma=True)
    ev_wb.inc(POOL.dma_start(out=Wbb[:], in_=w_bb.rearrange("(c p) n -> p c n", p=128)), dma=True)
    tok_wb = ev_wb.inc(POOL.dma_start(out=Wout[:], in_=w_out.rearrange("(c p) n -> p c n", p=128)), dma=True)
    W("act", late["sinkld"])
    tok_sinkexp = EV("sinkexp").inc(ACT.activation(out=sinkexp[:], in_=sinkexp[:], func=AF.Exp))
    ev_te = EV("tabexp")
    tab_ready = {}

    def table_tok(tid):
        if tid not in tab_ready:
            W("act", early["tok"] if tid in I_TIDS else tok_late)
            v = tabS[:, SLOT[tid], :]
            tab_ready[tid] = ev_te.inc(ACT.activation(out=v, in_=v, func=AF.Exp))
        return tab_ready[tid]

    ev_S = EV("S")
    ev_add = EV("add")
    ev_exp = EV("exp")
    ev_pv = EV("pv")
    ev_na = EV("na")
    PS_S = [0, 2]
    PS_OA = 4
    PS_OB = 6
    psS_free = [None, None]
    PR_free = [None, None, None]
    PT_free = [None, None, None]
    O_free = {"A": None, "B": None}
    tok_norm = {}

    slots = []
    for t_ in list(range(2, 14)) + [0, 1, 14, 15]:
        for j in range(5):
            slots.append(("A", t_, j))
        for j in range(3):
            slots.append(("B", t_, j))

    def Oview(bk):
        return ps[:, 512 * bk:512 * (bk + 2)].rearrange("p (b c) -> p b c", b=2)[:, :, 0:260].rearrange(
            "p b (h d) -> p b h d", d=65)

    def emit_S(n):
        kind, t_, j = slots[n]
        pS = bank(PS_S[n % 2], 2)
        W("pe", psS_free[n % 2])
        if n == 0:
            W("pe", p1_done)
        for h in range(8):
            if kind == "A":
                e_ = t_ + j
                p_, hp = h // 2, (h % 2) * 64
                lhsT = KAT[hp:hp + 64, p_, e_ * 128:(e_ + 1) * 128]
                rhs = QAT[hp:hp + 64, p_, t_ * 128:(t_ + 1) * 128]
            else:
                e_ = t_ + 1 + j
                g_, r_ = h // 4, h % 4
                lhsT = KBT[g_ * 64:(g_ + 1) * 64, (e_ - 1) * 128:e_ * 128]
                rhs = QBT[g_ * 64:(g_ + 1) * 64, r_, t_ * 128:(t_ + 1) * 128]
            cb = _cbA(h) if kind == "A" else h
            ins = PE.matmul(pS[:, cb * 128:(cb + 1) * 128], lhsT=lhsT, rhs=rhs, start=True, stop=True)
        tS = ev_S.inc(ins)
        tid = tabA_index(t_, j) if kind == "A" else 15 + tabB_index(t_, j)
        ttab = table_tok(tid)
        W("act", tS)
        W("act", PR_free[n % 3])
        tE = ev_exp.inc(ACT.activation(out=PR[n % 3][:], in_=pS, func=AF.Exp))
        psS_free[n % 2] = tE
        W("dve", tE)
        W("dve", PT_free[n % 3])
        tb = tabS[:, SLOT[tid], :]
        W("dve", ttab)
        tA = ev_add.inc(DVE.tensor_tensor(out=PT[n % 3][:], in0=PR[n % 3][:], in1=tb, op=ALU.mult))
        PR_free[n % 3] = tA
        return tA

    def emit_PV(n, tE):
        kind, t_, j = slots[n]
        nslot = 5 if kind == "A" else 3
        bk = PS_OA if kind == "A" else PS_OB
        W("pe", tE)
        if j == 0:
            W("pe", O_free[kind])
        for h in range(8):
            if kind == "A":
                rhs = VA[:, t_ + j, h, :]
            else:
                rhs = VB[:, t_ + j, h // 4, :]
            o_ap = ps[:, 512 * (bk + h // 4) + (h % 4) * 65: 512 * (bk + h // 4) + (h % 4) * 65 + 65]
            cb = _cbA(h) if kind == "A" else h
            ins = PE.matmul(o_ap, lhsT=PT[n % 3][:, cb * 128:(cb + 1) * 128], rhs=rhs,
                            start=(j == 0 and h % 4 == 0), stop=(j == nslot - 1 and h % 4 == 3))
        tP = ev_pv.inc(ins)
        PT_free[n % 3] = tP
        if j == nslot - 1:
            ov = Oview(bk)
            W("dve", tP)
            if kind == "A":
                rd = rdenA
                t1 = ev_na.inc(DVE.reciprocal(out=rd[:].rearrange("p (b h o) -> p b h o", b=2, h=4, o=1),
                                              in_=ov[:, :, :, 64:65]))
            else:
                rd = rdenB
                W("dve", tok_sinkexp)
                t0 = ev_na.inc(DVE.tensor_tensor(out=rd[:].rearrange("p (b h o) -> p b h o", b=2, h=4, o=1),
                                                 in0=ov[:, :, :, 64:65],
                                                 in1=sinkexp[:].rearrange("p (b h o) -> p b h o", b=2, h=4, o=1),
                                                 op=ALU.add))
                W("dve", t0)
                t1 = ev_na.inc(DVE.reciprocal(out=rd[:], in_=rd[:]))
            W("dve", t1)
            c0 = 0 if kind == "A" else 512
            t2 = ev_na.inc(DVE.tensor_tensor(
                out=otok[:, t_, c0:c0 + 512].rearrange("p (b h d) -> p b h d", b=2, h=4),
                in0=ov[:, :, :, 0:64],
                in1=rd[:].rearrange("p (b h) -> p b h", b=2).unsqueeze(3).to_broadcast([128, 2, 4, 64]),
                op=ALU.mult))
            O_free[kind] = t2
            tok_norm[(kind, t_)] = t2

    toks = {}
    for n in range(len(slots)):
        toks[n] = emit_S(n)
        if n >= 2:
            emit_PV(n - 2, toks[n - 2])
    emit_PV(len(slots) - 2, toks[len(slots) - 2])
    emit_PV(len(slots) - 1, toks[len(slots) - 1])
    p2a_done = [tok_norm[("A", NT - 1)], tok_norm[("B", NT - 1)], PT_free[0], PT_free[1], PT_free[2]]

    ev_wg = EV("wg")
    w_gate_v = w_gate.rearrange("(c p) n -> p c n", p=128)
    w_up_v = w_up.rearrange("(c p) n -> p c n", p=128)
    wtok = {}

    def issue_wg_prefetch(k):
        W("pool", p2a_done)
        if k < 8:
            wtok["wg"] = ev_wg.inc(POOL.dma_start(out=Wg[:, k, :], in_=w_gate_v[:, k, :]), dma=True)
        elif k < 12:
            c0 = 2 * (k - 8)
            wtok["wu1"] = EV('wu1').inc(POOL.dma_start(out=Wu1[:, c0:c0 + 2, :],
                                                       in_=w_up_v[:, c0:c0 + 2, 0:WU_SPLIT]), dma=True)
    W("sp", p1_done)
    W("sp", tok_a4[ORDER[NE - 1]])
    tok_gvec2 = ev_gv.inc(SP.dma_start(out=gvec[:], in_=gffn.partition_broadcast(128)), dma=True)

    ev_gl = EV("gl")
    ev_xl = EV("xl2")
    ev_otr = EV("otr")
    ev_ev2 = EV("ev2")
    ev_y = EV("ymm")
    ev_cmb = EV("cmb")
    ev_ytr = EV("ytr")
    ev_x2mm = EV("x2mm")
    ev_res = EV("res")
    ev_st2 = EV("st2")
    ev_sq2 = EV("sq2")
    ev_h2 = EV("h2")
    ev_h2tr = EV("h2tr")
    PS_TB = [0, 1]
    PS_Y = [[2, 3], [4, 5]]
    PS_X = 6
    psT_free = [None, None]
    st2 = {"k": 0}
    gsh_free = [None, None]
    xt2_free = [None, None]
    ytok_free = [None, None]
    h2b_free = [None, None]
    oT_free = [None, None]
    yT_free = [None, None]
    usb_free = [None, None]
    psY_free = [None, None]
    psX_free = [None]
    tok_oT = {}
    tok_cmb = {}
    tok_yT = {}
    tok_h2 = {}
    tok_h2T = {}
    tok_x2st = {}
    tok_res = {}
    tok_gl = {}
    tok_xl = {}

    def B_loads(t_):
        b = t_ % 2
        W("sp", xt2_free[b])
        if t_ < 2:
            W("sp", p2a_done)
        tok_xl[t_] = EV('xl2_%d' % b).inc(SP.dma_start(out=xt2[b][:], in_=xe[(t_ + 2) * 128:(t_ + 3) * 128, :]), dma=True)

    def B_gload(t_, c):
        W("sp", gsh_free[c])
        if t_ == 0:
            W("sp", p2a_done)
        tok_gl[(t_, c)] = EV('gl%d' % c).inc(SP.dma_start(
            out=gsh[c][:], in_=gts[t_].rearrange("p (g n) -> p g n", g=2)[:, :, c * 512:(c + 1) * 512]), dma=True)

    def tr8(src, dst_free_tok, extra_wait):
        k = st2["k"]; st2["k"] += 1
        pT = bankbf(PS_TB[k % 2])
        W("pe", psT_free[k % 2])
        W("pe", extra_wait)
        for c in range(8):
            ins = PE.transpose(out=pT[:, c * 128:(c + 1) * 128], in_=src[:, c * 128:(c + 1) * 128], identity=ident[:])
        return k % 2, pT, ins

    def B_otr(t_):
        b = t_ % 2
        kk, pT, ins = tr8(otok[:, t_, :], None, p2a_done if t_ == 0 else None)
        tt_ = ev_otr.inc(ins)
        W("act", tt_)
        W("act", oT_free[b])
        ta_ = ev_ev2.inc(ACT.activation(out=oTb[b][:, 0:512], in_=pT[:, 0:512], func=AF.Copy))
        tb_ = ev_ev2.inc(ACT.activation(out=oTb[b][:, 512:1024], in_=pT[:, 512:1024], func=AF.Copy))
        tok_oT[t_] = (ta_, tb_)
        psT_free[kk] = tb_

    def B_ymm(t_):
        b = t_ % 2
        W("pe", tok_oT[t_][0])
        W("pe", tok_wb)
        for c in range(2):
            W("pe", psY_free[c])
            pa = bank(PS_Y[c][0])
            pbk = bank(PS_Y[c][1])
            for k in range(4):
                PE.matmul(pa, lhsT=oTb[b][:, k * 128:(k + 1) * 128], rhs=Wba[:, k, c * 512:(c + 1) * 512],
                          start=(k == 0), stop=(k == 3))
            W("pe", tok_oT[t_][1])
            for k in range(4):
                ins = PE.matmul(pbk, lhsT=oTb[b][:, (4 + k) * 128:(5 + k) * 128], rhs=Wbb[:, k, c * 512:(c + 1) * 512],
                                start=(k == 0), stop=(k == 3))
            ty = ev_y.inc(ins)
            if c == 1:
                oT_free[b] = ty
            W("dve", ty)
            W("dve", tok_gl[(t_, c)])
            W("dve", usb_free[c])
            t1 = ev_cmb.inc(DVE.tensor_tensor(out=usb[c][:], in0=pa, in1=gsh[c][:, 0, :], op=ALU.mult))
            t2 = ev_cmb.inc(DVE.tensor_tensor(out=pbk, in0=pbk, in1=gsh[c][:, 1, :], op=ALU.mult))
            gsh_free[c] = t2
            W("dve", t2)
            W("dve", ytok_free[b])
            t3 = ev_cmb.inc(DVE.tensor_tensor(out=ytok[b][:, c * 512:(c + 1) * 512], in0=pbk, in1=usb[c][:], op=ALU.add))
            usb_free[c] = t3
            psY_free[c] = t3
            tok_cmb[t_] = t3
            if t_ + 1 < NT:
                B_gload(t_ + 1, c)

    def B_ytr(t_):
        b = t_ % 2
        kk, pT, ins = tr8(ytok[b], None, tok_cmb[t_])
        tt_ = ev_ytr.inc(ins)
        ytok_free[b] = tt_
        W("act", tt_)
        W("act", yT_free[b])
        ta_ = ev_ev2.inc(ACT.activation(out=yTb[b][:, 0:512], in_=pT[:, 0:512], func=AF.Copy))
        tb_ = ev_ev2.inc(ACT.activation(out=yTb[b][:, 512:1024], in_=pT[:, 512:1024], func=AF.Copy))
        tok_yT[t_] = (ta_, tb_)
        psT_free[kk] = tb_

    def B_x2mm(t_):
        b = t_ % 2
        W("pe", tok_yT[t_][0])
        W("pe", psX_free[0])
        pX = bank(PS_X, 2)
        for c in range(2):
            for k in range(8):
                if k == 4:
                    W("pe", tok_yT[t_][1])
                ins = PE.matmul(pX[:, c * 512:(c + 1) * 512], lhsT=yTb[b][:, k * 128:(k + 1) * 128],
                                rhs=Wout[:, k, c * 512:(c + 1) * 512], start=(k == 0), stop=(k == 7))
        tx_ = ev_x2mm.inc(ins)
        yT_free[b] = tx_
        W("dve", tx_)
        W("dve", tok_xl[t_])
        tr_ = ev_res.inc(DVE.tensor_tensor(out=xt2[b][:], in0=pX, in1=xt2[b][:], op=ALU.add))
        psX_free[0] = tr_
        W("sp", tr_)
        tok_x2st[t_] = EV('st2_%d' % b).inc(SP.dma_start(out=out[t_ * 128:(t_ + 1) * 128, :], in_=xt2[b][:]), dma=True)
        tok_res[t_] = tr_

    def B_norm(t_):
        b = t_ % 2
        tr_ = tok_res[t_]
        W("act", tr_)
        ts_ = ev_sq2.inc(ACT.activation(out=junk2[:], in_=xt2[b][:], func=AF.Square, accum_out=ssq2[:, t_:t_ + 1]))
        W("pool", ts_)
        tp = ev_p.inc(POOL.tensor_scalar(out=rs2[:, t_:t_ + 1], in0=ssq2[:, t_:t_ + 1], scalar1=1.0 / D, scalar2=EPS,
                                         op0=ALU.mult, op1=ALU.add))
        W("pool", tp)
        tp = ev_p.inc(POOL.tensor_tensor(out=rs2[:, t_:t_ + 1], in0=rs2[:, t_:t_ + 1], in1=mhalf[:, 0:1], op=ALU.pow))
        issue_wg_prefetch(t_)
        W("dve", tp)
        W("dve", h2b_free[b])
        W("dve", tok_gvec2)
        tok_h2[t_] = ev_h2.inc(DVE.scalar_tensor_tensor(out=h2b[b][:], in0=xt2[b][:], scalar=rs2[:, t_:t_ + 1],
                                                        in1=gvec[:], op0=ALU.mult, op1=ALU.mult))
        xt2_free[b] = [tok_x2st[t_], tok_h2[t_]]
        if t_ + 2 < NT:
            B_loads(t_ + 2)

    def B_h2tr(t_):
        b = t_ % 2
        kk, pT, ins = tr8(h2b[b], None, tok_h2[t_])
        tt_ = ev_h2tr.inc(ins)
        h2b_free[b] = tt_
        W("act", tt_)
        tok_h2T[t_] = ev_ev2.inc(ACT.activation(out=h2T[:, :, t_ * 128:(t_ + 1) * 128],
                                                in_=pT.rearrange("p (a t) -> p a t", a=8), func=AF.Copy))
        psT_free[kk] = tok_h2T[t_]

    B_loads(0)
    B_loads(1)
    B_gload(0, 0)
    B_gload(0, 1)
    B_otr(0)
    B_ymm(0)
    for i in range(NT):
        if i + 1 < NT:
            B_otr(i + 1)
        if i >= 1:
            B_norm(i - 1)
        B_ytr(i)
        if i + 1 < NT:
            B_ymm(i + 1)
        if i == NT - 1:
            B_x2mm(i)
            B_h2tr(i - 1)
        else:
            if i >= 1:
                B_h2tr(i - 1)
            B_x2mm(i)
    tok_wg = wtok["wg"]
    tok_wu1 = wtok["wu1"]
    ev_gu = EV("gu")
    ev_silu = EV("silu")
    ev_u = EV("umul")
    PS_GU = [[2, 3], [4, 5]]
    psGU_free = [psY_free[0], psY_free[1]]
    sgb_free = [[psY_free[0], psY_free[1]], [psY_free[0], psY_free[1]]]
    p3 = {"uT_free": None, "tok_wu2": None}
    EARLY_GU = 4

    def emit_GU(u, j, tok_uT):
        s_ = j % 2
        pG = bank(PS_GU[s_][0])
        pU = bank(PS_GU[s_][1])
        W("pe", psGU_free[s_])
        W("pe", tok_wg)
        for c in range(8):
            PE.matmul(pG, lhsT=Wg[:, c, j * 128:(j + 1) * 128], rhs=h2T[:, c, u * 512:(u + 1) * 512],
                      start=(c == 0), stop=(c == 7))
        W("pe", tok_wu1 if j < 11 else p3["tok_wu2"])
        Wuj, jj = (Wu1, j) if j < 11 else (Wu2, j - 11)
        for c in range(8):
            ins = PE.matmul(pU, lhsT=Wuj[:, c, jj * 128:(jj + 1) * 128], rhs=h2T[:, c, u * 512:(u + 1) * 512],
                            start=(c == 0), stop=(c == 7))
        tgu = ev_gu.inc(ins)
        W("act", tgu)
        W("act", sgb_free[s_])
        tsl = ev_silu.inc(ACT.activation(out=sgb[s_][:], in_=pG, func=AF.Silu))
        W("dve", tsl)
        if j == 0:
            W("dve", p3["uT_free"])
        tu = ev_u.inc(DVE.tensor_tensor(out=uT[:, j, :], in0=pU, in1=sgb[s_][:], op=ALU.mult))
        sgb_free[s_] = tu
        psGU_free[s_] = tu
        tok_uT[j] = tu

    tok_uT0 = {}
    B_norm(NT - 1)
    for j in range(EARLY_GU):
        emit_GU(0, j, tok_uT0)
    B_h2tr(NT - 1)
    p2b_done = [tok_h2T[NT - 1], tok_h2T[NT - 2], tok_x2st[NT - 1], tok_x2st[NT - 2], yT_free[0], yT_free[1]]

    W("pool", p2b_done)
    tok_wu2 = EV('wu2').inc(POOL.dma_start(out=Wu2[:], in_=w_up_v[:, :, WU_SPLIT:FF]), dma=True)
    ev_wd = EV("wd")
    w_down_v = w_down.rearrange("(c p) n -> p c n", p=128)
    tok_wd = {}
    for k0 in range(0, NFC, 6):
        k1 = min(NFC, k0 + 6)
        tkn = EV('wd%d' % k0).inc(POOL.dma_start(out=Wd[:, k0:k1, :], in_=w_down_v[:, k0:k1, :]), dma=True)
        for k in range(k0, k1):
            tok_wd[k] = tkn

    ev_d = EV("dmm")
    ev_fin = EV("fin")
    PS_D = [0, 6]
    psD_free = [p2b_done, p2b_done]
    x2t_free = [None, None]
    tok_x2l = {}
    tok_ost = {}
    p3["tok_wu2"] = tok_wu2

    def C_x2load(tile):
        b = tile % 2
        W("sp", x2t_free[b])
        W("sp", p2b_done)
        W("sp", tok_x2st[tile])
        tok_x2l[tile] = EV('x2l%d' % b).inc(SP.dma_start(out=x2t[b][:], in_=out[tile * 128:(tile + 1) * 128, :]), dma=True)

    C_x2load(0)
    C_x2load(1)
    for u in range(4):
        tok_uT = tok_uT0 if u == 0 else {}
        for j in range(EARLY_GU if u == 0 else 0, NFC):
            emit_GU(u, j, tok_uT)
        for i in range(4):
            tile = 4 * u + i
            b = tile % 2
            pD = bank(PS_D[i % 2], 2)
            W("pe", psD_free[i % 2])
            for c in range(2):
                for j in range(NFC):
                    W("pe", tok_uT[j])
                    W("pe", tok_wd[j])
                    ins = PE.matmul(pD[:, c * 512:(c + 1) * 512], lhsT=uT[:, j, i * 128:(i + 1) * 128],
                                    rhs=Wd[:, j, c * 512:(c + 1) * 512], start=(j == 0), stop=(j == NFC - 1))
            td = ev_d.inc(ins)
            if i == 3:
                p3["uT_free"] = td
            W("dve", td)
            W("dve", tok_x2l[tile])
            tf = ev_fin.inc(DVE.tensor_tensor(out=x2t[b][:], in0=pD, in1=x2t[b][:], op=ALU.add))
            psD_free[i % 2] = tf
            W("sp", tf)
            tok_ost[tile] = EV('ost%d' % b).inc(SP.dma_start(out=out[tile * 128:(tile + 1) * 128, :], in_=x2t[b][:]), dma=True)
            x2t_free[b] = tok_ost[tile]
            if tile + 2 < NT:
                C_x2load(tile + 2)
    W("sp", tok_ost[NT - 1])
    W("sp", tok_ost[NT - 2])
    return nc


_CACHE = {}


def _host_inputs(x, norm_mix, w_in, q_norm_a, k_norm_a, rpb_a, q_norm_b, k_norm_b, sink_b, t5_table,
                 w_branch_a, w_branch_b, w_out, norm_ffn, w_gate, w_up, w_down):
    f = lambda a: np.ascontiguousarray(np.asarray(a, dtype=np.float32))
    x = f(x)
    shared = {
        "w_in": f(w_in[0]), "gmix": f(norm_mix[0]).reshape(1, D),
        "gmixT": np.ascontiguousarray(f(norm_mix[0]).reshape(8, 128).T), "gffn": f(norm_ffn[0]).reshape(1, D),
        "qna": f(q_norm_a[0]).reshape(64, 1), "kna": f(k_norm_a[0]).reshape(64, 1),
        "qnb": f(q_norm_b[0]).reshape(64, 1), "knb": f(k_norm_b[0]).reshape(64, 1),
        "sink": f(sink_b[0]).reshape(1, 8),
        "w_ba": f(w_branch_a[0]), "w_bb": f(w_branch_b[0]), "w_out": f(w_out[0]),
        "w_gate": f(w_gate[0]), "w_up": f(w_up[0]), "w_down": f(w_down[0]),
    }
    rpb = f(rpb_a[0])
    t5 = f(t5_table)
    tabAs = [_build_tabA(rpb, s) for s in range(4)]
    tabBs = [_build_tabB(t5, s) for s in range(4)]
    in_maps = []
    for c in range(NCORES):
        b, s = c // 4, c % 4
        xe = np.zeros((NE * 128, D), dtype=np.float32)
        for e in range(NE):
            r0 = _ext_rows(s, e)
            if r0 is None:
                continue
            xe[e * 128:(e + 1) * 128] = x[b, r0 * 64:r0 * 64 + 128]
        m = dict(shared)
        m["xe"] = xe
        m["xeT"] = np.ascontiguousarray(xe.reshape(NE, 128, 8, 128).transpose(0, 3, 2, 1).reshape(NE * 128, D))
        m["tabA"] = tabAs[s]
        m["tabB"] = tabBs[s]
        in_maps.append(m)
    return in_maps


def kernel(**inputs):
    if "nc" not in _CACHE:
        _CACHE["nc"] = build_program()
    nc = _CACHE["nc"]
    in_maps = _host_inputs(**inputs)
    res = run_bass_kernel_spmd(nc, in_maps, core_ids=list(range(NCORES)))
    outp = np.empty((2, T, D), dtype=np.float32)
    for c in range(NCORES):
        b, s = c // 4, c % 4
        outp[b, s * TOK:(s + 1) * TOK] = res.results[c]["out"]
    return outp
```

```python
import numpy as np
import concourse.bass as bass
import concourse.mybir as mybir
from concourse.bass_utils import run_bass_kernel_spmd

F32 = mybir.dt.float32
BF16 = mybir.dt.bfloat16
AF = mybir.ActivationFunctionType
ALU = mybir.AluOpType
AX = mybir.AxisListType

NCORES = 8
D = 1024
T = 8192
TOK = 2048
NT = 16
NE = 20
FF = 2816
NFC = 22
NEG = -30000.0
EPS = 1e-6

MYBASE = 17920
SB_TOP = 229376
TOTAL = SB_TOP - MYBASE


def _t5_bucket(rel):
    half = 16
    max_exact = 8
    ret = (rel > 0).astype(np.int32) * half
    n = np.abs(rel)
    large = max_exact + (np.log(np.maximum(n, 1) / max_exact)
                         / np.log(128 / max_exact) * (half - max_exact)).astype(np.int32)
    large = np.minimum(large, half - 1)
    return ret + np.where(n < max_exact, n, large)


_HPERM_A = [0, 2, 4, 6, 1, 3, 5, 7]


def _cbA(h):
    return (h % 2) * 4 + h // 2


def tabA_index(t, j):
    if t == 0:
        return {0: 5, 1: 6, 4: 7}.get(j, j)
    if t == 1:
        return {0: 8, 4: 9}.get(j, j)
    if t == 14:
        return {0: 10, 4: 11}.get(j, j)
    if t == 15:
        return {0: 12, 3: 13, 4: 14}.get(j, j)
    return j


def tabB_index(t, j):
    if t == 0 and j == 0:
        return 3
    if t == 15 and j == 2:
        return 4
    return j


def _ext_rows(s, e):
    if s == 0 and e == 0:
        return 6
    if s == 0 and e == 1:
        return None
    if s == 3 and e == 18:
        return None
    if s == 3 and e == 19:
        return 120
    return 32 * s - 4 + 2 * e


def _build_tabA(rpb, s):
    out = np.full((15, 128, 8, 128), NEG, dtype=np.float32)
    kl = np.arange(128)
    krl, kc = kl // 64, kl % 64
    ql = np.arange(128)
    qrl, qc = ql // 64, ql % 64
    ws = np.clip(qc - 8, 0, 48)
    colok = (kc[:, None] >= ws[None, :]) & (kc[:, None] < ws[None, :] + 16)
    dc = np.clip(kc[:, None] - qc[None, :], -15, 15) + 15
    reps = {}
    for t in range(16):
        for j in range(5):
            idx = tabA_index(t, j)
            if idx >= 5 or (t == 5):
                reps[idx] = (t, j)
    for idx, (t, j) in reps.items():
        k0 = _ext_rows(s, t + j)
        if k0 is None:
            continue
        q0 = 32 * s + 2 * t
        kr = k0 + krl
        qr = q0 + qrl
        rs = np.clip(qr - 4, 0, 120)
        rowok = (kr[:, None] >= rs[None, :]) & (kr[:, None] < rs[None, :] + 8)
        dr = kr[:, None] - qr[None, :] + 7
        ok = rowok & colok
        drc = np.clip(dr, 0, 14)
        vals = rpb[drc, dc, :]
        vals = np.where(ok[:, :, None], vals, np.float32(NEG))
        out[idx] = np.transpose(vals, (0, 2, 1))[:, _HPERM_A, :]
    return np.ascontiguousarray(np.transpose(out, (1, 0, 2, 3)).reshape(128, 15 * 1024))


def _build_tabB(t5, s):
    out = np.full((5, 128, 8, 128), NEG, dtype=np.float32)
    k = np.arange(128)[:, None]
    q = np.arange(128)[None, :]
    for jb in range(3):
        rel = (jb - 1) * 128 + k - q
        ok = np.abs(rel) <= 128
        vals = t5[_t5_bucket(rel), :]
        vals = np.where(ok[:, :, None], vals, np.float32(NEG))
        out[jb] = np.transpose(vals, (0, 2, 1))
    if s != 0:
        out[3] = out[0]
    if s != 3:
        out[4] = out[2]
    return np.ascontiguousarray(np.transpose(out, (1, 0, 2, 3)).reshape(128, 5 * 1024))


class _Ev:
    def __init__(self, nc, name):
        self.sem = nc.alloc_semaphore(name)
        self.n = 0

    def inc(self, ins, dma=False):
        k = 16 if dma else 1
        ins.then_inc(self.sem, k)
        self.n += k
        return (self, self.n)


def build_program():
    nc = bass.Bass("TRN2", target_bir_lowering=False)
    PE, DVE, ACT, POOL, SP = nc.tensor, nc.vector, nc.scalar, nc.gpsimd, nc.sync
    eng = {"pe": PE, "dve": DVE, "act": ACT, "pool": POOL, "sp": SP}
    waited = {}

    def W(e, tok):
        if tok is None:
            return
        if isinstance(tok, list):
            for x in tok:
                W(e, x)
            return
        ev, val = tok
        key = (e, id(ev))
        if waited.get(key, 0) >= val:
            return
        waited[key] = val
        eng[e].wait_ge(ev.sem, val)

    evs = {}

    def EV(name):
        if name not in evs:
            evs[name] = _Ev(nc, name)
        return evs[name]

    def din(name, shape, dt=F32):
        return nc.dram_tensor(name, list(shape), dt, kind="ExternalInput").ap()

    xe = din("xe", [NE * 128, D])
    xeT = din("xeT", [NE * 128, D])
    gmixT = din("gmixT", [128, 8])
    w_in = din("w_in", [D, 4352])
    gmix = din("gmix", [1, D])
    gffn = din("gffn", [1, D])
    qna = din("qna", [64, 1])
    kna = din("kna", [64, 1])
    qnb = din("qnb", [64, 1])
    knb = din("knb", [64, 1])
    sink = din("sink", [1, 8])
    tabA = din("tabA", [128, 15 * 1024])
    tabB = din("tabB", [128, 5 * 1024])
    w_ba = din("w_ba", [512, D])
    w_bb = din("w_bb", [512, D])
    w_out = din("w_out", [D, D])
    w_gate = din("w_gate", [D, FF])
    w_up = din("w_up", [D, FF])
    w_down = din("w_down", [FF, D])
    out = nc.dram_tensor("out", [TOK, D], F32, kind="ExternalOutput").ap()
    gts = nc.dram_tensor("gts", [NT, 128, 2048], BF16).ap()

    def at(name, shape, dt, rel):
        nbytes = int(np.prod(shape[1:])) * (4 if dt == F32 else 2)
        assert rel % 32 == 0, (name, rel)
        assert rel + nbytes <= TOTAL, (name, rel, nbytes, TOTAL)
        return nc.alloc_sbuf_tensor_at(name, list(shape), dt, offset=MYBASE + rel)

    QAT = at("QAT", [128, 4, TOK], BF16, 0)
    KAT = at("KAT", [128, 4, NE * 128], BF16, 16384)
    VA = at("VA", [128, NE, 8, 65], BF16, 36864)
    QBT = at("QBT", [128, 4, TOK], BF16, 57696)
    KBT = at("KBT", [128, 18 * 128], BF16, 74080)
    VB = at("VB", [128, 18, 2, 65], BF16, 78688)
    QKV_END = 83392
    Win = at("Win", [128, 8, 4352], BF16, QKV_END)
    W1 = QKV_END + 69632
    CB = TOTAL - 5152
    gvec = at("gvec", [128, D], F32, CB)
    ident = at("ident", [128, 128], BF16, CB + 4096)
    SM = CB + 4096 + 256
    qsA = at("qsA", [128, 1], F32, SM)
    qsB = at("qsB", [128, 1], F32, SM + 32)
    gtmp = at("gtmp", [128, 4], F32, SM + 64)
    sinkexp = at("sinkexp", [128, 8], F32, SM + 96)
    mhalf = at("mhalf", [128, 32], F32, SM + 128)
    ssqx = at("ssqx", [128, 20], F32, SM + 256)
    rsx = at("rsx", [128, 20], F32, SM + 352)
    ssq2 = at("ssq2", [128, 16], F32, SM + 448)
    rs2 = at("rs2", [128, 16], F32, SM + 512)
    rdenA = at("rdenA", [128, 8], F32, SM + 576)
    rdenB = at("rdenB", [128, 8], F32, SM + 608)
    epsq = at("epsq", [128, 20], F32, SM + 640)
    gcolT = at("gcolT", [128, 8], F32, SM + 736)
    assert SM + 768 <= TOTAL

    o = W1
    xt = [at("xt%d" % i, [128, D], F32, o + 4096 * i) for i in range(2)]; o += 8192
    xT = [at("xT%d" % i, [128, 8, 128], F32, o + 4096 * i) for i in range(2)]
    hb = [at("hb%d" % i, [128, D], BF16, o + 4096 + 2048 * i) for i in range(2)]; o += 8192
    raw = [at("raw%d" % i, [128, 1664], F32, o + 6656 * i) for i in range(2)]; o += 13312
    junk = at("junk", [128, D], BF16, o); o += 2048
    hT = [at("hT%d" % i, [128, D], BF16, o + 2048 * i) for i in range(2)]; o += 4096
    sq = [at("sq%d" % i, [128, 512], F32, o + 2048 * i) for i in range(2)]; o += 4096
    qn = [at("qn%d" % i, [128, 1664], BF16, o + 3328 * i) for i in range(2)]; o += 6656
    gsb = [at("gsb%d" % i, [128, 512], BF16, o + 1024 * i) for i in range(2)]; o += 2048
    ssq = [at("ssq%d" % i, [128, 32], F32, o + 128 * i) for i in range(2)]; o += 256
    rstd = [at("rstd%d" % i, [128, 32], F32, o + 128 * i) for i in range(2)]; o += 256
    identf = at("identf", [128, 128], F32, o); o += 512
    assert o <= CB

    tabAsb = at("tabAsb", [128, 15, 1024], BF16, QKV_END)
    tabBsb = at("tabBsb", [128, 5, 1024], BF16, QKV_END + 30720)
    OTOK = QKV_END + 40960
    otok = at("otok", [128, NT, 1024], BF16, OTOK)
    WB = OTOK + 32768
    Wba = at("Wba", [128, 4, D], BF16, WB)
    Wbb = at("Wbb", [128, 4, D], BF16, WB + 8192)
    Wout = at("Wout", [128, 8, D], BF16, WB + 16384)
    W2 = WB + 32768
    PR = [at("PR%d" % i, [128, 1024], BF16, W2 + 2048 * i) for i in range(3)]
    PT = [at("PT%d" % i, [128, 1024], BF16, W2 + 8192 + 2048 * i) for i in range(3)]
    assert W2 + 8192 + 6144 <= CB

    h2T = at("h2T", [128, 8, TOK], BF16, 0)
    Wg = at("Wg", [128, 8, FF], BF16, 32768)
    Wu1 = at("Wu1", [128, 8, 1408], BF16, 77824)
    Wu2 = at("Wu2", [128, 8, 1408], BF16, 100352)
    WU_SPLIT = 1408
    X2B = 77824 + 8 * WU_SPLIT * 2
    o = X2B
    xt2 = [at("xt2_%d" % i, [128, D], F32, o + 4096 * i) for i in range(2)]; o += 8192
    ytok = [at("ytok%d" % i, [128, D], BF16, o + 2048 * i) for i in range(2)]; o += 4096
    h2b = [at("h2b%d" % i, [128, D], BF16, o + 2048 * i) for i in range(2)]; o += 4096
    oTb = [at("oTb%d" % i, [128, D], BF16, o + 2048 * i) for i in range(2)]; o += 4096
    junk2 = at("junk2", [128, D], BF16, o); o += 2048
    assert o <= 122880
    o = W2
    yTb = [at("yTb%d" % i, [128, D], BF16, o + 2048 * i) for i in range(2)]; o += 4096
    usb = [at("usb%d" % i, [128, 512], F32, o + 2048 * i) for i in range(2)]; o += 4096
    gsh = [at("gsh%d" % i, [128, 2, 512], BF16, o + 2048 * i) for i in range(2)]; o += 4096
    assert o <= CB
    Wd = at("Wd", [128, NFC, D], BF16, 122880)
    uT = at("uT", [128, NFC, 512], BF16, 167936)
    o = 190464
    x2t = [at("x2t%d" % i, [128, D], F32, o + 4096 * i) for i in range(2)]; o += 8192
    sgb = [at("sgb%d" % i, [128, 512], F32, o + 2048 * i) for i in range(2)]; o += 4096
    assert o <= CB + 4096

    ps = nc.alloc_psum_tensor("ps", [128, 4096], F32)

    def bank(k, n=1):
        return ps[:, 512 * k:512 * (k + n)]

    def bankbf(k, n=1):
        return ps[:, 512 * k:512 * (k + n)].bitcast(BF16)

    ev_setp = EV("setp")
    ev_setv = EV("setv")
    ev_setd = EV("setd")
    t = ev_setp.inc(POOL.memset(identf[:], 0.0))
    W("pool", t)
    t = ev_setp.inc(POOL.affine_select(out=identf[:], in_=identf[:], pattern=[[-1, 128]],
                                      compare_op=ALU.not_equal, fill=1.0, base=0, channel_multiplier=1))
    W("dve", t)
    tok_ident = ev_setv.inc(DVE.tensor_copy(out=ident[:], in_=identf[:]))
    tok_mhalf = ev_setp.inc(POOL.memset(mhalf[:], -0.5))
    ev_gv = EV("gv")
    tok_gcol = EV("gcol").inc(SP.dma_start(out=gcolT[:], in_=gmixT), dma=True)
    tok_gvec = ev_gv.inc(SP.dma_start(out=gvec[:], in_=gmix.partition_broadcast(128)), dma=True)
    DVE.memset(ssq[0][:], 1.0)
    DVE.memset(ssq[1][:], 1.0)
    late = {}

    def late_setup():
        for k_, src in enumerate([qna, kna, qnb, knb]):
            ev_setd.inc(SP.dma_start(out=gtmp[0:64, k_:k_ + 1], in_=src), dma=True)
            ev_setd.inc(SP.dma_start(out=gtmp[64:128, k_:k_ + 1], in_=src), dma=True)
        late["sinkld"] = ev_setd.inc(SP.dma_start(out=sinkexp[:], in_=sink.partition_broadcast(128)), dma=True)
        W("dve", late["sinkld"])
        DVE.scalar_tensor_tensor(out=qsA[:], in0=gtmp[:, 0:1], scalar=0.125, in1=gtmp[:, 1:2],
                                 op0=ALU.mult, op1=ALU.mult)
        late["qs"] = ev_setv.inc(DVE.scalar_tensor_tensor(out=qsB[:], in0=gtmp[:, 2:3], scalar=0.125,
                                                          in1=gtmp[:, 3:4], op0=ALU.mult, op1=ALU.mult))

    col_groups = {
        "qA": (0, 512), "kA": (512, 1024), "vA": (1024, 1536), "qB": (1536, 2048), "kvB": (2048, 2304),
        "g0": (2304, 2816), "g1": (2816, 3328), "g2": (3328, 3840), "g3": (3840, 4352),
    }
    w_in_v = w_in.rearrange("(c p) n -> p c n", p=128)
    tokW = {}

    def issue_w(names):
        for gname in names:
            a, b = col_groups[gname]
            tokW[gname] = EV("w_" + gname).inc(
                POOL.dma_start(out=Win[:, :, a:b], in_=w_in_v[:, :, a:b]), dma=True)

    issue_w(["kA", "vA", "kvB"])

    DVE.memset(VA[:, :, :, 64:65], 1.0)
    tok_vones = ev_setv.inc(DVE.memset(VB[:, :, :, 64:65], 1.0))
    ev_x = EV("xld")
    ev_a2 = EV("a2")
    ev_p = EV("pool")
    ev_a4 = EV("a4")
    ev_tx = EV("tx")
    ev_a6 = EV("a6")
    ev_g = EV("grp")
    ev_sq = EV("sq")
    ev_red = EV("red")
    ev_qn = EV("qn")
    ev_vc = EV("vcopy")
    ev_sg = EV("sig")
    ev_gst = EV("gst")
    ev_ttr = EV("ttr")
    ev_evd = EV("evd")
    ev_eva = EV("eva")

    tok_x = {}
    tok_a4 = {}
    tok_tx = {}
    tok_a6 = {}
    xt_free = [None, None]
    hb_free = [None, None]
    hT_free = [None, None]
    qn_free = [None, None]
    sq_free = [None, None]
    gsb_free = [None, None]
    bank_free = [None] * 5
    st = {"psT1_free": None, "psT2_free": None, "n": 0, "m": 0, "gm": 0}
    tok_qn_last = {}
    tok_lastgrp = {}
    PS_T1 = 0
    PS_G = 1
    PS_T2 = 6

    def is_own(e):
        return 2 <= e <= 17

    ORDER = [0, 1, 18, 19] + list(range(2, 18))
    POS = {e_: i_ for i_, e_ in enumerate(ORDER)}
    raw_free = [None, None]
    tok_rstd = {}
    tok_stat = {}
    NGB = 5

    xT_free = [None, None]
    tok_xT = {}
    OLD = set(ORDER[0:8])

    def A_load(e):
        b = POS[e] % 2
        W("sp", xt_free[b])
        tok_x[e] = EV('xld%d' % b).inc(SP.dma_start(out=xt[b][:], in_=xe[e * 128:(e + 1) * 128, :]), dma=True)
        if e in OLD:
            return
        W("sp", xT_free[b])
        tok_xT[e] = EV('xTld%d' % b).inc(SP.dma_start(
            out=xT[b][:], in_=xeT[e * 128:(e + 1) * 128, :].rearrange("p (c t) -> p c t", c=8)), dma=True)

    def A_stat(e):
        b = POS[e] % 2
        W("act", tok_x[e])
        W("act", st.get("junk_tok"))
        t2 = ev_a2.inc(ACT.activation(out=junk[:], in_=xt[b][:], func=AF.Square, accum_out=ssqx[:, e:e + 1]))
        st["junk_tok"] = t2
        xt_free[b] = t2
        W("pool", t2)
        W("pool", tok_mhalf)
        t3 = ev_p.inc(POOL.tensor_scalar(out=rsx[:, e:e + 1], in0=ssqx[:, e:e + 1], scalar1=1.0 / D, scalar2=EPS,
                                         op0=ALU.mult, op1=ALU.add))
        ev_p.inc(POOL.tensor_scalar(out=epsq[:, e:e + 1], in0=ssqx[:, e:e + 1], scalar1=EPS / D, scalar2=EPS * EPS,
                                    op0=ALU.mult, op1=ALU.add))
        W("pool", t3)
        tok_stat[e] = ev_p.inc(POOL.tensor_tensor(out=rsx[:, e:e + 1], in0=rsx[:, e:e + 1], in1=mhalf[:, 0:1],
                                                  op=ALU.pow))

    def A_scale(e):
        b = POS[e] % 2
        if e in OLD:
            W("dve", tok_x[e])
            W("dve", hb_free[b])
            W("dve", tok_gvec)
            tok_a4[e] = ev_a4.inc(DVE.tensor_tensor(out=hb[b][:], in0=xt[b][:], in1=gvec[:], op=ALU.mult))
            xt_free[b] = [xt_free[b], tok_a4[e]]
            return
        W("dve", tok_xT[e])
        W("dve", hT_free[b])
        W("dve", tok_gcol)
        tok_a4[e] = ev_a4.inc(DVE.tensor_tensor(out=hT[b][:].rearrange("p (c t) -> p c t", c=8), in0=xT[b][:],
                                                in1=gcolT[:].unsqueeze(2).to_broadcast([128, 8, 128]),
                                                op=ALU.mult))
        xT_free[b] = tok_a4[e]
        tok_a6[e] = tok_a4[e]

    def A_tx_old(e):
        b = POS[e] % 2
        if True:
            W("pe", tok_a4[e])
            W("pe", st["psT1_free"])
            W("pe", tok_ident)
            pT = bankbf(PS_T1)
            for c in range(8):
                ins = PE.transpose(out=pT[:, c * 128:(c + 1) * 128], in_=hb[b][:, c * 128:(c + 1) * 128],
                                   identity=ident[:])
            tok_tx[e] = ev_tx.inc(ins)
            hb_free[b] = tok_tx[e]
            xT_free[1] = [xT_free[1], tok_tx[e]]
            W("act", tok_tx[e])
            W("act", hT_free[b])
            tok_a6[e] = ev_a6.inc(ACT.activation(out=hT[b][:], in_=pT, func=AF.Copy))
            st["psT1_free"] = tok_a6[e]

    NORM_OFF = {"qA": 0, "kA": 512, "qB": 1024, "kvB": 1536}
    NORM_C0 = {"qA": 0, "kA": 8, "qB": 16, "kvB": 24}

    def emit_rstd(e, tred_last):
        par = POS[e] % 2
        W("pool", tred_last)
        W("pool", tok_stat[e])
        tp = ev_p.inc(POOL.tensor_scalar(out=rstd[par][:, 0:26], in0=ssq[par][:, 0:26],
                                         scalar1=1.0 / 64, scalar2=epsq[:, e:e + 1], op0=ALU.mult, op1=ALU.add))
        W("pool", tp)
        tok_rstd[e] = ev_p.inc(POOL.tensor_tensor(out=rstd[par][:, 0:26], in0=rstd[par][:, 0:26],
                                                  in1=mhalf[:, 0:26], op=ALU.pow))

    def A_groups(e, mid_hook=None):
        b = POS[e] % 2
        par = POS[e] % 2
        own = is_own(e)
        last = (e == ORDER[NE - 1])
        if own and last:
            glist = ["qA", "kA", "qB", "kvB", "vA", "g0", "g1", "g2", "g3"]
        elif own:
            glist = ["qA", "g0", "kA", "g1", "vA", "g2", "qB", "g3", "kvB"]
        else:
            glist = ["kA", "vA"] + (["kvB"] if 1 <= e <= 18 else [])
        tt = e - 2
        tg_last = None
        tred_last = None
        W("act", tok_stat[e])
        hook_at = min(2, len(glist) - 1)
        for gi, gname in enumerate(glist):
            a, bb = col_groups[gname]
            w = bb - a
            n = st["n"]; st["n"] += 1
            bk = n % NGB
            pb = bank(PS_G + bk)
            W("pe", bank_free[bk])
            W("pe", tok_a6[e])
            W("pe", tokW[gname])
            for c in range(8):
                ins = PE.matmul(pb[:, 0:w], lhsT=hT[b][:, c * 128:(c + 1) * 128], rhs=Win[:, c, a:bb],
                                start=(c == 0), stop=(c == 7))
            tg_ = ev_g.inc(ins)
            tg_last = tg_
            if gname in ("qA", "kA", "qB", "kvB"):
                m = st["m"]; st["m"] += 1
                s = m % 2
                nh, wn = (2, 128) if gname == "kvB" else (8, 512)
                c0 = NORM_C0[gname]
                q0 = NORM_OFF[gname]
                W("act", tg_)
                W("act", raw_free[par])
                ACT.activation(out=raw[par][:, q0:q0 + wn], in_=pb[:, 0:wn], func=AF.Copy)
                if gname == "kvB":
                    W("act", tok_vones)
                    ACT.activation(out=VB[:, e - 1, :, 0:64],
                                   in_=pb[:, 128:256].rearrange("p (h d) -> p h d", d=64), func=AF.Copy,
                                   scale=rsx[:, e:e + 1])
                W("act", sq_free[s])
                tsq = ev_sq.inc(ACT.activation(out=sq[s][:, 0:wn], in_=pb[:, 0:wn], func=AF.Square))
                bank_free[bk] = tsq
                W("dve", tsq)
                tred = ev_red.inc(DVE.tensor_reduce(out=ssq[par][:, c0:c0 + nh],
                                                    in_=sq[s][:, 0:wn].rearrange("p (h d) -> p h d", d=64),
                                                    axis=AX.X, op=ALU.add))
                sq_free[s] = tred
                tred_last = tred
            elif gname == "vA":
                W("act", tg_)
                W("act", tok_vones)
                tv = ev_sq.inc(ACT.activation(out=VA[:, e, :, 0:64],
                                              in_=pb[:, 0:512].rearrange("p (h d) -> p h d", d=64), func=AF.Copy,
                                              scale=rsx[:, e:e + 1]))
                bank_free[bk] = tv
            else:
                k = int(gname[1])
                gm = st["gm"]; st["gm"] += 1
                s = gm % 2
                W("act", tg_)
                W("act", gsb_free[s])
                tsg = ev_sq.inc(ACT.activation(out=gsb[s][:], in_=pb[:, 0:512], func=AF.Sigmoid,
                                               scale=rsx[:, e:e + 1]))
                bank_free[bk] = tsg
                W("sp", tsg)
                gsb_free[s] = EV('gst%d' % s).inc(SP.dma_start(out=gts[tt, :, k * 512:(k + 1) * 512], in_=gsb[s][:]), dma=True)
            if gi == hook_at and mid_hook is not None:
                mid_hook()
            if last and gi == 3:
                emit_rstd(e, tred_last)
            if last and gi == 6:
                A_normalize(e)
        hT_free[b] = tg_last
        tok_lastgrp[e] = tg_last
        if not last:
            emit_rstd(e, tred_last)

    def A_normalize(e):
        par = POS[e] % 2
        own = is_own(e)
        W("dve", tok_rstd[e])
        W("dve", qn_free[par])
        names = (["qA"] if own else []) + ["kA"] + (["qB"] if own else []) + (["kvB"] if 1 <= e <= 18 else [])
        tq = None
        for gname in names:
            q0 = NORM_OFF[gname]
            c0 = NORM_C0[gname]
            if gname == "qB":
                o_v = qn[par][:, 1024:1536].rearrange("p (r g d) -> p g r d", r=4, g=2, d=64)
                i_v = raw[par][:, 1024:1536].rearrange("p (g r d) -> p g r d", g=2, r=4, d=64)
                r_v = rstd[par][:, 16:24].rearrange("p (g r) -> p g r", g=2).unsqueeze(3).to_broadcast([128, 2, 4, 64])
            else:
                nh, wn = (2, 128) if gname == "kvB" else (8, 512)
                o_v = qn[par][:, q0:q0 + wn].rearrange("p (h d) -> p h d", d=64)
                i_v = raw[par][:, q0:q0 + wn].rearrange("p (h d) -> p h d", d=64)
                r_v = rstd[par][:, c0:c0 + nh].unsqueeze(2).to_broadcast([128, nh, 64])
            tq = ev_qn.inc(DVE.tensor_tensor(out=o_v, in0=i_v, in1=r_v, op=ALU.mult))
        tok_qn_last[e] = tq
        raw_free[par] = tq

    def A_ttr(e):
        par = POS[e] % 2
        own = is_own(e)
        tt = e - 2
        pT2 = bankbf(PS_T2, 2)
        W("pe", tok_qn_last[e])
        W("pe", st["psT2_free"])
        srcs = []
        if own:
            srcs += [(p, p * 128) for p in range(4)]
            srcs += [(4 + r, 1024 + r * 128) for r in range(4)]
        srcs += [(8 + p, 512 + p * 128) for p in range(4)]
        if 1 <= e <= 18:
            srcs += [(12, 1536)]
        for slot, c0 in srcs:
            ins = PE.transpose(out=pT2[:, slot * 128:(slot + 1) * 128], in_=qn[par][:, c0:c0 + 128], identity=ident[:])
        tt_ = ev_ttr.inc(ins)
        qn_free[par] = tt_
        frees = []
        if own:
            W("dve", tt_)
            W("dve", late["qs"])
            DVE.tensor_scalar(out=QAT[:, :, tt * 128:(tt + 1) * 128],
                              in0=pT2[:, 0:512].rearrange("p (a t) -> p a t", a=4),
                              scalar1=qsA[:, 0:1], scalar2=None, op0=ALU.mult)
            td = ev_evd.inc(DVE.tensor_scalar(out=QBT[:, :, tt * 128:(tt + 1) * 128],
                                              in0=pT2[:, 512:1024].rearrange("p (a t) -> p a t", a=4),
                                              scalar1=qsB[:, 0:1], scalar2=None, op0=ALU.mult))
            frees.append(td)
        W("act", tt_)
        ta = ev_eva.inc(ACT.activation(out=KAT[:, :, e * 128:(e + 1) * 128],
                                       in_=pT2[:, 1024:1536].rearrange("p (a t) -> p a t", a=4), func=AF.Copy))
        if 1 <= e <= 18:
            ta = ev_eva.inc(ACT.activation(out=KBT[:, (e - 1) * 128:e * 128], in_=pT2[:, 1536:1664], func=AF.Copy))
        frees.append(ta)
        st["psT2_free"] = frees

    A_load(ORDER[0])
    A_load(ORDER[1])
    A_stat(ORDER[0])
    issue_w(["qA", "qB", "g0", "g1", "g2", "g3"])
    A_scale(ORDER[0])
    A_stat(ORDER[1])
    A_scale(ORDER[1])
    A_load(ORDER[2])
    A_tx_old(ORDER[0])

    def mid_hook(i):
        if i >= 1:
            A_normalize(ORDER[i - 1])
        if i + 2 < NE and ORDER[i + 2] in OLD:
            A_scale(ORDER[i + 2])
        elif i + 1 < NE and ORDER[i + 1] not in OLD:
            A_scale(ORDER[i + 1])
    for i in range(NE + 1):
        if i == 1:
            late_setup()
        if i + 1 < NE and ORDER[i + 1] in OLD:
            A_tx_old(ORDER[i + 1])
        if i + 2 < NE:
            A_stat(ORDER[i + 2])
        if i < NE:
            A_groups(ORDER[i], (lambda j=i: mid_hook(j)))
        if i >= 1:
            A_ttr(ORDER[i - 1])
        if i + 3 < NE:
            A_load(ORDER[i + 3])

    p1_done = [st["psT2_free"], gsb_free[0], gsb_free[1], tok_lastgrp[ORDER[NE - 1]], tok_lastgrp[ORDER[NE - 2]]]

    W("pool", [tok_lastgrp[ORDER[NE - 1]], tok_lastgrp[ORDER[NE - 2]]])
    tok_tab1 = {}
    for k_ in range(5):
        tok_tab1[("A", k_)] = EV("tabA%d" % k_).inc(
            POOL.dma_start(out=tabAsb[:, k_, :], in_=tabA[:, k_ * 1024:(k_ + 1) * 1024]), dma=True)
    for k_ in range(3):
        tok_tab1[("B", k_)] = EV("tabB%d" % k_).inc(
            POOL.dma_start(out=tabBsb[:, k_, :], in_=tabB[:, k_ * 1024:(k_ + 1) * 1024]), dma=True)
    tok_tabA = tok_tab1[("A", 4)]
    tok_tabB = EV("tabB").inc(POOL.dma_start(out=tabBsb[:, 3:5, :].rearrange("p a b -> p (a b)"),
                                             in_=tabB[:, 3 * 1024:5 * 1024]), dma=True)
    W("pool", [tok_tab1[("B", 2)], tok_tabB])
    tok_tabE = EV("tabE").inc(POOL.dma_start(out=tabAsb[:, 5:15, :].rearrange("p a b -> p (a b)"),
                                             in_=tabA[:, 5 * 1024:15 * 1024]), dma=True)
    ev_wb = EV("wb")
    W("pool", tok_tabE)
    W("pool", p1_done)
    ev_wb.inc(POOL.dma_start(out=Wba[:], in_=w_ba.rearrange("(c p) n -> p c n", p=128)), dma=True)
    ev_wb.inc(POOL.dma_start(out=Wbb[:], in_=w_bb.rearrange("(c p) n -> p c n", p=128)), dma=True)
    tok_wb = ev_wb.inc(POOL.dma_start(out=Wout[:], in_=w_out.rearrange("(c p) n -> p c n", p=128)), dma=True)
    W("act", late["sinkld"])
    tok_sinkexp = EV("sinkexp").inc(ACT.activation(out=sinkexp[:], in_=sinkexp[:], func=AF.Exp))
    ev_te = EV("tabexp")
    tab_ready = {}

    def table_tok(kind, idx):
        key = (kind, idx)
        if key not in tab_ready:
            if kind == "A":
                W("act", tok_tab1[("A", idx)] if idx < 5 else tok_tabE)
                v = tabAsb[:, idx, :]
            else:
                W("act", tok_tab1[("B", idx)] if idx < 3 else tok_tabB)
                v = tabBsb[:, idx, :]
            tab_ready[key] = ev_te.inc(ACT.activation(out=v, in_=v, func=AF.Exp))
        return tab_ready[key]
    tabE = {"tok": None, "tokB": None}

    def exp_B_table():
        W("act", tok_tabB)
        tabE["tokB"] = ev_te.inc(ACT.activation(out=tabBsb[:], in_=tabBsb[:], func=AF.Exp))

    def exp_edge_tables():
        W("act", tok_tabE)
        for k in range(1, 3):
            v = tabAsb[:, 5 * k:5 * (k + 1), :]
            tabE["tok"] = ev_te.inc(ACT.activation(out=v, in_=v, func=AF.Exp))

    ev_S = EV("S")
    ev_add = EV("add")
    ev_exp = EV("exp")
    ev_pv = EV("pv")
    ev_na = EV("na")
    PS_S = [0, 2]
    PS_OA = 4
    PS_OB = 6
    psS_free = [None, None]
    PR_free = [None, None, None]
    PT_free = [None, None, None]
    O_free = {"A": None, "B": None}
    tok_norm = {}

    slots = []
    for t_ in list(range(2, 14)) + [0, 1, 14, 15]:
        for j in range(5):
            slots.append(("A", t_, j))
        for j in range(3):
            slots.append(("B", t_, j))

    def Oview(bk):
        return ps[:, 512 * bk:512 * (bk + 2)].rearrange("p (b c) -> p b c", b=2)[:, :, 0:260].rearrange(
            "p b (h d) -> p b h d", d=65)

    def emit_S(n):
        kind, t_, j = slots[n]
        pS = bank(PS_S[n % 2], 2)
        W("pe", psS_free[n % 2])
        if n == 0:
            W("pe", p1_done)
        for h in range(8):
            if kind == "A":
                e_ = t_ + j
                p_, hp = h // 2, (h % 2) * 64
                lhsT = KAT[hp:hp + 64, p_, e_ * 128:(e_ + 1) * 128]
                rhs = QAT[hp:hp + 64, p_, t_ * 128:(t_ + 1) * 128]
            else:
                e_ = t_ + 1 + j
                g_, r_ = h // 4, h % 4
                lhsT = KBT[g_ * 64:(g_ + 1) * 64, (e_ - 1) * 128:e_ * 128]
                rhs = QBT[g_ * 64:(g_ + 1) * 64, r_, t_ * 128:(t_ + 1) * 128]
            cb = _cbA(h) if kind == "A" else h
            ins = PE.matmul(pS[:, cb * 128:(cb + 1) * 128], lhsT=lhsT, rhs=rhs, start=True, stop=True)
        tS = ev_S.inc(ins)
        ttab = table_tok(kind, tabA_index(t_, j) if kind == "A" else tabB_index(t_, j))
        W("act", tS)
        W("act", PR_free[n % 3])
        tE = ev_exp.inc(ACT.activation(out=PR[n % 3][:], in_=pS, func=AF.Exp))
        psS_free[n % 2] = tE
        W("dve", tE)
        W("dve", PT_free[n % 3])
        if kind == "A":
            tb = tabAsb[:, tabA_index(t_, j), :]
            W("dve", ttab)
        else:
            tb = tabBsb[:, tabB_index(t_, j), :]
            W("dve", ttab)
        tA = ev_add.inc(DVE.tensor_tensor(out=PT[n % 3][:], in0=PR[n % 3][:], in1=tb, op=ALU.mult))
        PR_free[n % 3] = tA
        return tA

    def emit_PV(n, tE):
        kind, t_, j = slots[n]
        nslot = 5 if kind == "A" else 3
        bk = PS_OA if kind == "A" else PS_OB
        W("pe", tE)
        if j == 0:
            W("pe", O_free[kind])
        for h in range(8):
            if kind == "A":
                rhs = VA[:, t_ + j, h, :]
            else:
                rhs = VB[:, t_ + j, h // 4, :]
            o_ap = ps[:, 512 * (bk + h // 4) + (h % 4) * 65: 512 * (bk + h // 4) + (h % 4) * 65 + 65]
            cb = _cbA(h) if kind == "A" else h
            ins = PE.matmul(o_ap, lhsT=PT[n % 3][:, cb * 128:(cb + 1) * 128], rhs=rhs,
                            start=(j == 0 and h % 4 == 0), stop=(j == nslot - 1 and h % 4 == 3))
        tP = ev_pv.inc(ins)
        PT_free[n % 3] = tP
        if j == nslot - 1:
            ov = Oview(bk)
            W("dve", tP)
            if kind == "A":
                rd = rdenA
                t1 = ev_na.inc(DVE.reciprocal(out=rd[:].rearrange("p (b h o) -> p b h o", b=2, h=4, o=1),
                                              in_=ov[:, :, :, 64:65]))
            else:
                rd = rdenB
                W("dve", tok_sinkexp)
                t0 = ev_na.inc(DVE.tensor_tensor(out=rd[:].rearrange("p (b h o) -> p b h o", b=2, h=4, o=1),
                                                 in0=ov[:, :, :, 64:65],
                                                 in1=sinkexp[:].rearrange("p (b h o) -> p b h o", b=2, h=4, o=1),
                                                 op=ALU.add))
                W("dve", t0)
                t1 = ev_na.inc(DVE.reciprocal(out=rd[:], in_=rd[:]))
            W("dve", t1)
            c0 = 0 if kind == "A" else 512
            t2 = ev_na.inc(DVE.tensor_tensor(
                out=otok[:, t_, c0:c0 + 512].rearrange("p (b h d) -> p b h d", b=2, h=4),
                in0=ov[:, :, :, 0:64],
                in1=rd[:].rearrange("p (b h) -> p b h", b=2).unsqueeze(3).to_broadcast([128, 2, 4, 64]),
                op=ALU.mult))
            O_free[kind] = t2
            tok_norm[(kind, t_)] = t2

    toks = {}
    for n in range(len(slots)):
        toks[n] = emit_S(n)
        if n >= 2:
            emit_PV(n - 2, toks[n - 2])
    emit_PV(len(slots) - 2, toks[len(slots) - 2])
    emit_PV(len(slots) - 1, toks[len(slots) - 1])
    p2a_done = [tok_norm[("A", NT - 1)], tok_norm[("B", NT - 1)], PT_free[0], PT_free[1], PT_free[2]]

    ev_wg = EV("wg")
    w_gate_v = w_gate.rearrange("(c p) n -> p c n", p=128)
    w_up_v = w_up.rearrange("(c p) n -> p c n", p=128)
    wtok = {}

    def issue_wg_prefetch(k):
        W("pool", p2a_done)
        if k < 8:
            wtok["wg"] = ev_wg.inc(POOL.dma_start(out=Wg[:, k, :], in_=w_gate_v[:, k, :]), dma=True)
        elif k < 12:
            c0 = 2 * (k - 8)
            wtok["wu1"] = EV('wu1').inc(POOL.dma_start(out=Wu1[:, c0:c0 + 2, :],
                                                       in_=w_up_v[:, c0:c0 + 2, 0:WU_SPLIT]), dma=True)
    W("sp", p1_done)
    W("sp", tok_a4[ORDER[NE - 1]])
    tok_gvec2 = ev_gv.inc(SP.dma_start(out=gvec[:], in_=gffn.partition_broadcast(128)), dma=True)

    ev_gl = EV("gl")
    ev_xl = EV("xl2")
    ev_otr = EV("otr")
    ev_ev2 = EV("ev2")
    ev_y = EV("ymm")
    ev_cmb = EV("cmb")
    ev_ytr = EV("ytr")
    ev_x2mm = EV("x2mm")
    ev_res = EV("res")
    ev_st2 = EV("st2")
    ev_sq2 = EV("sq2")
    ev_h2 = EV("h2")
    ev_h2tr = EV("h2tr")
    PS_TB = [0, 1]
    PS_Y = [[2, 3], [4, 5]]
    PS_X = 6
    psT_free = [None, None]
    st2 = {"k": 0}
    gsh_free = [None, None]
    xt2_free = [None, None]
    ytok_free = [None, None]
    h2b_free = [None, None]
    oT_free = [None, None]
    yT_free = [None, None]
    usb_free = [None, None]
    psY_free = [None, None]
    psX_free = [None]
    tok_oT = {}
    tok_cmb = {}
    tok_yT = {}
    tok_h2 = {}
    tok_h2T = {}
    tok_x2st = {}
    tok_res = {}
    tok_gl = {}
    tok_xl = {}

    def B_loads(t_):
        b = t_ % 2
        W("sp", xt2_free[b])
        if t_ < 2:
            W("sp", p2a_done)
        tok_xl[t_] = EV('xl2_%d' % b).inc(SP.dma_start(out=xt2[b][:], in_=xe[(t_ + 2) * 128:(t_ + 3) * 128, :]), dma=True)

    def B_gload(t_, c):
        W("sp", gsh_free[c])
        if t_ == 0:
            W("sp", p2a_done)
        tok_gl[(t_, c)] = EV('gl%d' % c).inc(SP.dma_start(
            out=gsh[c][:], in_=gts[t_].rearrange("p (g n) -> p g n", g=2)[:, :, c * 512:(c + 1) * 512]), dma=True)

    def tr8(src, dst_free_tok, extra_wait):
        k = st2["k"]; st2["k"] += 1
        pT = bankbf(PS_TB[k % 2])
        W("pe", psT_free[k % 2])
        W("pe", extra_wait)
        for c in range(8):
            ins = PE.transpose(out=pT[:, c * 128:(c + 1) * 128], in_=src[:, c * 128:(c + 1) * 128], identity=ident[:])
        return k % 2, pT, ins

    def B_otr(t_):
        b = t_ % 2
        kk, pT, ins = tr8(otok[:, t_, :], None, p2a_done if t_ == 0 else None)
        tt_ = ev_otr.inc(ins)
        W("act", tt_)
        W("act", oT_free[b])
        ta_ = ev_ev2.inc(ACT.activation(out=oTb[b][:, 0:512], in_=pT[:, 0:512], func=AF.Copy))
        tb_ = ev_ev2.inc(ACT.activation(out=oTb[b][:, 512:1024], in_=pT[:, 512:1024], func=AF.Copy))
        tok_oT[t_] = (ta_, tb_)
        psT_free[kk] = tb_

    def B_ymm(t_):
        b = t_ % 2
        W("pe", tok_oT[t_][0])
        W("pe", tok_wb)
        for c in range(2):
            W("pe", psY_free[c])
            pa = bank(PS_Y[c][0])
            pbk = bank(PS_Y[c][1])
            for k in range(4):
                PE.matmul(pa, lhsT=oTb[b][:, k * 128:(k + 1) * 128], rhs=Wba[:, k, c * 512:(c + 1) * 512],
                          start=(k == 0), stop=(k == 3))
            W("pe", tok_oT[t_][1])
            for k in range(4):
                ins = PE.matmul(pbk, lhsT=oTb[b][:, (4 + k) * 128:(5 + k) * 128], rhs=Wbb[:, k, c * 512:(c + 1) * 512],
                                start=(k == 0), stop=(k == 3))
            ty = ev_y.inc(ins)
            if c == 1:
                oT_free[b] = ty
            W("dve", ty)
            W("dve", tok_gl[(t_, c)])
            W("dve", usb_free[c])
            t1 = ev_cmb.inc(DVE.tensor_tensor(out=usb[c][:], in0=pa, in1=gsh[c][:, 0, :], op=ALU.mult))
            t2 = ev_cmb.inc(DVE.tensor_tensor(out=pbk, in0=pbk, in1=gsh[c][:, 1, :], op=ALU.mult))
            gsh_free[c] = t2
            W("dve", t2)
            W("dve", ytok_free[b])
            t3 = ev_cmb.inc(DVE.tensor_tensor(out=ytok[b][:, c * 512:(c + 1) * 512], in0=pbk, in1=usb[c][:], op=ALU.add))
            usb_free[c] = t3
            psY_free[c] = t3
            tok_cmb[t_] = t3
            if t_ + 1 < NT:
                B_gload(t_ + 1, c)

    def B_ytr(t_):
        b = t_ % 2
        kk, pT, ins = tr8(ytok[b], None, tok_cmb[t_])
        tt_ = ev_ytr.inc(ins)
        ytok_free[b] = tt_
        W("act", tt_)
        W("act", yT_free[b])
        ta_ = ev_ev2.inc(ACT.activation(out=yTb[b][:, 0:512], in_=pT[:, 0:512], func=AF.Copy))
        tb_ = ev_ev2.inc(ACT.activation(out=yTb[b][:, 512:1024], in_=pT[:, 512:1024], func=AF.Copy))
        tok_yT[t_] = (ta_, tb_)
        psT_free[kk] = tb_

    def B_x2mm(t_):
        b = t_ % 2
        W("pe", tok_yT[t_][0])
        W("pe", psX_free[0])
        pX = bank(PS_X, 2)
        for c in range(2):
            for k in range(8):
                if k == 4:
                    W("pe", tok_yT[t_][1])
                ins = PE.matmul(pX[:, c * 512:(c + 1) * 512], lhsT=yTb[b][:, k * 128:(k + 1) * 128],
                                rhs=Wout[:, k, c * 512:(c + 1) * 512], start=(k == 0), stop=(k == 7))
        tx_ = ev_x2mm.inc(ins)
        yT_free[b] = tx_
        W("dve", tx_)
        W("dve", tok_xl[t_])
        tr_ = ev_res.inc(DVE.tensor_tensor(out=xt2[b][:], in0=pX, in1=xt2[b][:], op=ALU.add))
        psX_free[0] = tr_
        W("sp", tr_)
        tok_x2st[t_] = EV('st2_%d' % b).inc(SP.dma_start(out=out[t_ * 128:(t_ + 1) * 128, :], in_=xt2[b][:]), dma=True)
        tok_res[t_] = tr_

    def B_norm(t_):
        b = t_ % 2
        tr_ = tok_res[t_]
        W("act", tr_)
        ts_ = ev_sq2.inc(ACT.activation(out=junk2[:], in_=xt2[b][:], func=AF.Square, accum_out=ssq2[:, t_:t_ + 1]))
        W("pool", ts_)
        tp = ev_p.inc(POOL.tensor_scalar(out=rs2[:, t_:t_ + 1], in0=ssq2[:, t_:t_ + 1], scalar1=1.0 / D, scalar2=EPS,
                                         op0=ALU.mult, op1=ALU.add))
        W("pool", tp)
        tp = ev_p.inc(POOL.tensor_tensor(out=rs2[:, t_:t_ + 1], in0=rs2[:, t_:t_ + 1], in1=mhalf[:, 0:1], op=ALU.pow))
        issue_wg_prefetch(t_)
        W("dve", tp)
        W("dve", h2b_free[b])
        W("dve", tok_gvec2)
        tok_h2[t_] = ev_h2.inc(DVE.scalar_tensor_tensor(out=h2b[b][:], in0=xt2[b][:], scalar=rs2[:, t_:t_ + 1],
                                                        in1=gvec[:], op0=ALU.mult, op1=ALU.mult))
        xt2_free[b] = [tok_x2st[t_], tok_h2[t_]]
        if t_ + 2 < NT:
            B_loads(t_ + 2)

    def B_h2tr(t_):
        b = t_ % 2
        kk, pT, ins = tr8(h2b[b], None, tok_h2[t_])
        tt_ = ev_h2tr.inc(ins)
        h2b_free[b] = tt_
        W("act", tt_)
        tok_h2T[t_] = ev_ev2.inc(ACT.activation(out=h2T[:, :, t_ * 128:(t_ + 1) * 128],
                                                in_=pT.rearrange("p (a t) -> p a t", a=8), func=AF.Copy))
        psT_free[kk] = tok_h2T[t_]

    B_loads(0)
    B_loads(1)
    B_gload(0, 0)
    B_gload(0, 1)
    B_otr(0)
    B_ymm(0)
    for i in range(NT):
        if i + 1 < NT:
            B_otr(i + 1)
        if i >= 1:
            B_norm(i - 1)
        B_ytr(i)
        if i + 1 < NT:
            B_ymm(i + 1)
        if i == NT - 1:
            B_x2mm(i)
            B_h2tr(i - 1)
        else:
            if i >= 1:
                B_h2tr(i - 1)
            B_x2mm(i)
    tok_wg = wtok["wg"]
    tok_wu1 = wtok["wu1"]
    ev_gu = EV("gu")
    ev_silu = EV("silu")
    ev_u = EV("umul")
    PS_GU = [[2, 3], [4, 5]]
    psGU_free = [psY_free[0], psY_free[1]]
    sgb_free = [[psY_free[0], psY_free[1]], [psY_free[0], psY_free[1]]]
    p3 = {"uT_free": None, "tok_wu2": None}
    EARLY_GU = 4

    def emit_GU(u, j, tok_uT):
        s_ = j % 2
        pG = bank(PS_GU[s_][0])
        pU = bank(PS_GU[s_][1])
        W("pe", psGU_free[s_])
        W("pe", tok_wg)
        for c in range(8):
            PE.matmul(pG, lhsT=Wg[:, c, j * 128:(j + 1) * 128], rhs=h2T[:, c, u * 512:(u + 1) * 512],
                      start=(c == 0), stop=(c == 7))
        W("pe", tok_wu1 if j < 11 else p3["tok_wu2"])
        Wuj, jj = (Wu1, j) if j < 11 else (Wu2, j - 11)
        for c in range(8):
            ins = PE.matmul(pU, lhsT=Wuj[:, c, jj * 128:(jj + 1) * 128], rhs=h2T[:, c, u * 512:(u + 1) * 512],
                            start=(c == 0), stop=(c == 7))
        tgu = ev_gu.inc(ins)
        W("act", tgu)
        W("act", sgb_free[s_])
        tsl = ev_silu.inc(ACT.activation(out=sgb[s_][:], in_=pG, func=AF.Silu))
        W("dve", tsl)
        if j == 0:
            W("dve", p3["uT_free"])
        tu = ev_u.inc(DVE.tensor_tensor(out=uT[:, j, :], in0=pU, in1=sgb[s_][:], op=ALU.mult))
        sgb_free[s_] = tu
        psGU_free[s_] = tu
        tok_uT[j] = tu

    tok_uT0 = {}
    B_norm(NT - 1)
    for j in range(EARLY_GU):
        emit_GU(0, j, tok_uT0)
    B_h2tr(NT - 1)
    p2b_done = [tok_h2T[NT - 1], tok_h2T[NT - 2], tok_x2st[NT - 1], tok_x2st[NT - 2], yT_free[0], yT_free[1]]

    W("pool", p2b_done)
    tok_wu2 = EV('wu2').inc(POOL.dma_start(out=Wu2[:], in_=w_up_v[:, :, WU_SPLIT:FF]), dma=True)
    ev_wd = EV("wd")
    w_down_v = w_down.rearrange("(c p) n -> p c n", p=128)
    tok_wd = {}
    for k0 in range(0, NFC, 6):
        k1 = min(NFC, k0 + 6)
        tkn = EV('wd%d' % k0).inc(POOL.dma_start(out=Wd[:, k0:k1, :], in_=w_down_v[:, k0:k1, :]), dma=True)
        for k in range(k0, k1):
            tok_wd[k] = tkn

    ev_d = EV("dmm")
    ev_fin = EV("fin")
    PS_D = [0, 6]
    psD_free = [p2b_done, p2b_done]
    x2t_free = [None, None]
    tok_x2l = {}
    tok_ost = {}
    p3["tok_wu2"] = tok_wu2

    def C_x2load(tile):
        b = tile % 2
        W("sp", x2t_free[b])
        W("sp", p2b_done)
        W("sp", tok_x2st[tile])
        tok_x2l[tile] = EV('x2l%d' % b).inc(SP.dma_start(out=x2t[b][:], in_=out[tile * 128:(tile + 1) * 128, :]), dma=True)

    C_x2load(0)
    C_x2load(1)
    for u in range(4):
        tok_uT = tok_uT0 if u == 0 else {}
        for j in range(EARLY_GU if u == 0 else 0, NFC):
            emit_GU(u, j, tok_uT)
        for i in range(4):
            tile = 4 * u + i
            b = tile % 2
            pD = bank(PS_D[i % 2], 2)
            W("pe", psD_free[i % 2])
            for c in range(2):
                for j in range(NFC):
                    W("pe", tok_uT[j])
                    W("pe", tok_wd[j])
                    ins = PE.matmul(pD[:, c * 512:(c + 1) * 512], lhsT=uT[:, j, i * 128:(i + 1) * 128],
                                    rhs=Wd[:, j, c * 512:(c + 1) * 512], start=(j == 0), stop=(j == NFC - 1))
            td = ev_d.inc(ins)
            if i == 3:
                p3["uT_free"] = td
            W("dve", td)
            W("dve", tok_x2l[tile])
            tf = ev_fin.inc(DVE.tensor_tensor(out=x2t[b][:], in0=pD, in1=x2t[b][:], op=ALU.add))
            psD_free[i % 2] = tf
            W("sp", tf)
            tok_ost[tile] = EV('ost%d' % b).inc(SP.dma_start(out=out[tile * 128:(tile + 1) * 128, :], in_=x2t[b][:]), dma=True)
            x2t_free[b] = tok_ost[tile]
            if tile + 2 < NT:
                C_x2load(tile + 2)
    W("sp", tok_ost[NT - 1])
    W("sp", tok_ost[NT - 2])
    return nc


_CACHE = {}


def _host_inputs(x, norm_mix, w_in, q_norm_a, k_norm_a, rpb_a, q_norm_b, k_norm_b, sink_b, t5_table,
                 w_branch_a, w_branch_b, w_out, norm_ffn, w_gate, w_up, w_down):
    f = lambda a: np.ascontiguousarray(np.asarray(a, dtype=np.float32))
    x = f(x)
    shared = {
        "w_in": f(w_in[0]), "gmix": f(norm_mix[0]).reshape(1, D),
        "gmixT": np.ascontiguousarray(f(norm_mix[0]).reshape(8, 128).T), "gffn": f(norm_ffn[0]).reshape(1, D),
        "qna": f(q_norm_a[0]).reshape(64, 1), "kna": f(k_norm_a[0]).reshape(64, 1),
        "qnb": f(q_norm_b[0]).reshape(64, 1), "knb": f(k_norm_b[0]).reshape(64, 1),
        "sink": f(sink_b[0]).reshape(1, 8),
        "w_ba": f(w_branch_a[0]), "w_bb": f(w_branch_b[0]), "w_out": f(w_out[0]),
        "w_gate": f(w_gate[0]), "w_up": f(w_up[0]), "w_down": f(w_down[0]),
    }
    rpb = f(rpb_a[0])
    t5 = f(t5_table)
    tabAs = [_build_tabA(rpb, s) for s in range(4)]
    tabBs = [_build_tabB(t5, s) for s in range(4)]
    in_maps = []
    for c in range(NCORES):
        b, s = c // 4, c % 4
        xe = np.zeros((NE * 128, D), dtype=np.float32)
        for e in range(NE):
            r0 = _ext_rows(s, e)
            if r0 is None:
                continue
            xe[e * 128:(e + 1) * 128] = x[b, r0 * 64:r0 * 64 + 128]
        m = dict(shared)
        m["xe"] = xe
        m["xeT"] = np.ascontiguousarray(xe.reshape(NE, 128, 8, 128).transpose(0, 3, 2, 1).reshape(NE * 128, D))
        m["tabA"] = tabAs[s]
        m["tabB"] = tabBs[s]
        in_maps.append(m)
    return in_maps


def kernel(**inputs):
    if "nc" not in _CACHE:
        _CACHE["nc"] = build_program()
    nc = _CACHE["nc"]
    in_maps = _host_inputs(**inputs)
    res = run_bass_kernel_spmd(nc, in_maps, core_ids=list(range(NCORES)))
    outp = np.empty((2, T, D), dtype=np.float32)
    for c in range(NCORES):
        b, s = c // 4, c % 4
        outp[b, s * TOK:(s + 1) * TOK] = res.results[c]["out"]
    return outp
```

```python
import numpy as np
import concourse.bass as bass
import concourse.mybir as mybir
from concourse.bass_utils import run_bass_kernel_spmd

F32 = mybir.dt.float32
BF16 = mybir.dt.bfloat16
AF = mybir.ActivationFunctionType
ALU = mybir.AluOpType
AX = mybir.AxisListType

NCORES = 8
D = 1024
T = 8192
TOK = 2048
NT = 16
NE = 20
FF = 2816
NFC = 22
NEG = -30000.0
EPS = 1e-6

MYBASE = 17920
SB_TOP = 229376
TOTAL = SB_TOP - MYBASE


def _t5_bucket(rel):
    half = 16
    max_exact = 8
    ret = (rel > 0).astype(np.int32) * half
    n = np.abs(rel)
    large = max_exact + (np.log(np.maximum(n, 1) / max_exact)
                         / np.log(128 / max_exact) * (half - max_exact)).astype(np.int32)
    large = np.minimum(large, half - 1)
    return ret + np.where(n < max_exact, n, large)


_HPERM_A = [0, 2, 4, 6, 1, 3, 5, 7]


def _cbA(h):
    return (h % 2) * 4 + h // 2


def tabA_index(t, j):
    if t == 0:
        return {0: 5, 1: 6, 4: 7}.get(j, j)
    if t == 1:
        return {0: 8, 4: 9}.get(j, j)
    if t == 14:
        return {0: 10, 4: 11}.get(j, j)
    if t == 15:
        return {0: 12, 3: 13, 4: 14}.get(j, j)
    return j


def tabB_index(t, j):
    if t == 0 and j == 0:
        return 3
    if t == 15 and j == 2:
        return 4
    return j


def _ext_rows(s, e):
    if s == 0 and e == 0:
        return 6
    if s == 0 and e == 1:
        return None
    if s == 3 and e == 18:
        return None
    if s == 3 and e == 19:
        return 120
    return 32 * s - 4 + 2 * e


def _build_tabA(rpb, s):
    out = np.full((15, 128, 8, 128), NEG, dtype=np.float32)
    kl = np.arange(128)
    krl, kc = kl // 64, kl % 64
    ql = np.arange(128)
    qrl, qc = ql // 64, ql % 64
    ws = np.clip(qc - 8, 0, 48)
    colok = (kc[:, None] >= ws[None, :]) & (kc[:, None] < ws[None, :] + 16)
    dc = np.clip(kc[:, None] - qc[None, :], -15, 15) + 15
    reps = {}
    for t in range(16):
        for j in range(5):
            idx = tabA_index(t, j)
            if idx >= 5 or (t == 5):
                reps[idx] = (t, j)
    for idx, (t, j) in reps.items():
        k0 = _ext_rows(s, t + j)
        if k0 is None:
            continue
        q0 = 32 * s + 2 * t
        kr = k0 + krl
        qr = q0 + qrl
        rs = np.clip(qr - 4, 0, 120)
        rowok = (kr[:, None] >= rs[None, :]) & (kr[:, None] < rs[None, :] + 8)
        dr = kr[:, None] - qr[None, :] + 7
        ok = rowok & colok
        drc = np.clip(dr, 0, 14)
        vals = rpb[drc, dc, :]
        vals = np.where(ok[:, :, None], vals, np.float32(NEG))
        out[idx] = np.transpose(vals, (0, 2, 1))[:, _HPERM_A, :]
    return np.ascontiguousarray(np.transpose(out, (1, 0, 2, 3)).reshape(128, 15 * 1024))


def _build_tabB(t5, s):
    out = np.full((5, 128, 8, 128), NEG, dtype=np.float32)
    k = np.arange(128)[:, None]
    q = np.arange(128)[None, :]
    for jb in range(3):
        rel = (jb - 1) * 128 + k - q
        ok = np.abs(rel) <= 128
        vals = t5[_t5_bucket(rel), :]
        vals = np.where(ok[:, :, None], vals, np.float32(NEG))
        out[jb] = np.transpose(vals, (0, 2, 1))
    if s != 0:
        out[3] = out[0]
    if s != 3:
        out[4] = out[2]
    return np.ascontiguousarray(np.transpose(out, (1, 0, 2, 3)).reshape(128, 5 * 1024))


class _Ev:
    def __init__(self, nc, name):
        self.sem = nc.alloc_semaphore(name)
        self.n = 0

    def inc(self, ins, dma=False):
        k = 16 if dma else 1
        ins.then_inc(self.sem, k)
        self.n += k
        return (self, self.n)


def build_program():
    nc = bass.Bass("TRN2", target_bir_lowering=False)
    PE, DVE, ACT, POOL, SP = nc.tensor, nc.vector, nc.scalar, nc.gpsimd, nc.sync
    eng = {"pe": PE, "dve": DVE, "act": ACT, "pool": POOL, "sp": SP}
    waited = {}

    def W(e, tok):
        if tok is None:
            return
        if isinstance(tok, list):
            for x in tok:
                W(e, x)
            return
        ev, val = tok
        key = (e, id(ev))
        if waited.get(key, 0) >= val:
            return
        waited[key] = val
        eng[e].wait_ge(ev.sem, val)

    evs = {}

    def EV(name):
        if name not in evs:
            evs[name] = _Ev(nc, name)
        return evs[name]

    def din(name, shape, dt=F32):
        return nc.dram_tensor(name, list(shape), dt, kind="ExternalInput").ap()

    xe = din("xe", [NE * 128, D])
    xeT = din("xeT", [NE * 128, D])
    gmixT = din("gmixT", [128, 8])
    w_in = din("w_in", [D, 4352])
    gmix = din("gmix", [1, D])
    gffn = din("gffn", [1, D])
    qna = din("qna", [64, 1])
    kna = din("kna", [64, 1])
    qnb = din("qnb", [64, 1])
    knb = din("knb", [64, 1])
    sink = din("sink", [1, 8])
    tabA = din("tabA", [128, 15 * 1024])
    tabB = din("tabB", [128, 5 * 1024])
    w_ba = din("w_ba", [512, D])
    w_bb = din("w_bb", [512, D])
    w_out = din("w_out", [D, D])
    w_gate = din("w_gate", [D, FF])
    w_up = din("w_up", [D, FF])
    w_down = din("w_down", [FF, D])
    out = nc.dram_tensor("out", [TOK, D], F32, kind="ExternalOutput").ap()
    gts = nc.dram_tensor("gts", [NT, 128, 2048], BF16).ap()

    def at(name, shape, dt, rel):
        nbytes = int(np.prod(shape[1:])) * (4 if dt == F32 else 2)
        assert rel % 32 == 0, (name, rel)
        assert rel + nbytes <= TOTAL, (name, rel, nbytes, TOTAL)
        return nc.alloc_sbuf_tensor_at(name, list(shape), dt, offset=MYBASE + rel)

    QAT = at("QAT", [128, 4, TOK], BF16, 0)
    KAT = at("KAT", [128, 4, NE * 128], BF16, 16384)
    VA = at("VA", [128, NE, 8, 65], BF16, 36864)
    QBT = at("QBT", [128, 4, TOK], BF16, 57696)
    KBT = at("KBT", [128, 18 * 128], BF16, 74080)
    VB = at("VB", [128, 18, 2, 65], BF16, 78688)
    QKV_END = 83392
    Win = at("Win", [128, 8, 4352], BF16, QKV_END)
    W1 = QKV_END + 69632
    CB = TOTAL - 5152
    gvec = at("gvec", [128, D], F32, CB)
    ident = at("ident", [128, 128], BF16, CB + 4096)
    SM = CB + 4096 + 256
    qsA = at("qsA", [128, 1], F32, SM)
    qsB = at("qsB", [128, 1], F32, SM + 32)
    gtmp = at("gtmp", [128, 4], F32, SM + 64)
    sinkexp = at("sinkexp", [128, 8], F32, SM + 96)
    mhalf = at("mhalf", [128, 32], F32, SM + 128)
    ssqx = at("ssqx", [128, 20], F32, SM + 256)
    rsx = at("rsx", [128, 20], F32, SM + 352)
    ssq2 = at("ssq2", [128, 16], F32, SM + 448)
    rs2 = at("rs2", [128, 16], F32, SM + 512)
    rdenA = at("rdenA", [128, 8], F32, SM + 576)
    rdenB = at("rdenB", [128, 8], F32, SM + 608)
    epsq = at("epsq", [128, 20], F32, SM + 640)
    gcolT = at("gcolT", [128, 8], F32, SM + 736)
    assert SM + 768 <= TOTAL

    o = W1
    xt = [at("xt%d" % i, [128, D], F32, o + 4096 * i) for i in range(2)]; o += 8192
    xT = [at("xT%d" % i, [128, 8, 128], F32, o + 4096 * i) for i in range(2)]
    hb = [at("hb%d" % i, [128, D], BF16, o + 4096 + 2048 * i) for i in range(2)]; o += 8192
    raw = [at("raw%d" % i, [128, 1664], F32, o + 6656 * i) for i in range(2)]; o += 13312
    junk = at("junk", [128, D], BF16, o); o += 2048
    hT = [at("hT%d" % i, [128, D], BF16, o + 2048 * i) for i in range(2)]; o += 4096
    sq = [at("sq%d" % i, [128, 512], F32, o + 2048 * i) for i in range(2)]; o += 4096
    qn = [at("qn%d" % i, [128, 1664], BF16, o + 3328 * i) for i in range(2)]; o += 6656
    gsb = [at("gsb%d" % i, [128, 512], BF16, o + 1024 * i) for i in range(2)]; o += 2048
    ssq = [at("ssq%d" % i, [128, 32], F32, o + 128 * i) for i in range(2)]; o += 256
    rstd = [at("rstd%d" % i, [128, 32], F32, o + 128 * i) for i in range(2)]; o += 256
    identf = at("identf", [128, 128], F32, o)
    hTk = [at("hTk%d" % i, [128, D], BF16, o + 2048 * i) for i in range(2)]; o += 4096
    assert o <= CB

    tabAsb = at("tabAsb", [128, 15, 1024], BF16, QKV_END)
    tabBsb = at("tabBsb", [128, 5, 1024], BF16, QKV_END + 30720)
    OTOK = QKV_END + 40960
    otok = at("otok", [128, NT, 1024], BF16, OTOK)
    WB = OTOK + 32768
    Wba = at("Wba", [128, 4, D], BF16, WB)
    Wbb = at("Wbb", [128, 4, D], BF16, WB + 8192)
    Wout = at("Wout", [128, 8, D], BF16, WB + 16384)
    W2 = WB + 32768
    PR = [at("PR%d" % i, [128, 1024], BF16, W2 + 2048 * i) for i in range(3)]
    PT = [at("PT%d" % i, [128, 1024], BF16, W2 + 8192 + 2048 * i) for i in range(3)]
    assert W2 + 8192 + 6144 <= CB

    h2T = at("h2T", [128, 8, TOK], BF16, 0)
    Wg = at("Wg", [128, 8, FF], BF16, 32768)
    Wu1 = at("Wu1", [128, 8, 1408], BF16, 77824)
    Wu2 = at("Wu2", [128, 8, 1408], BF16, 100352)
    WU_SPLIT = 1408
    X2B = 77824 + 8 * WU_SPLIT * 2
    o = X2B
    xt2 = [at("xt2_%d" % i, [128, D], F32, o + 4096 * i) for i in range(2)]; o += 8192
    ytok = [at("ytok%d" % i, [128, D], BF16, o + 2048 * i) for i in range(2)]; o += 4096
    h2b = [at("h2b%d" % i, [128, D], BF16, o + 2048 * i) for i in range(2)]; o += 4096
    oTb = [at("oTb%d" % i, [128, D], BF16, o + 2048 * i) for i in range(2)]; o += 4096
    junk2 = at("junk2", [128, D], BF16, o); o += 2048
    assert o <= 122880
    o = W2
    yTb = [at("yTb%d" % i, [128, D], BF16, o + 2048 * i) for i in range(2)]; o += 4096
    usb = [at("usb%d" % i, [128, 512], F32, o + 2048 * i) for i in range(2)]; o += 4096
    gsh = [at("gsh%d" % i, [128, 2, 512], BF16, o + 2048 * i) for i in range(2)]; o += 4096
    assert o <= CB
    Wd = at("Wd", [128, NFC, D], BF16, 122880)
    uT = at("uT", [128, NFC, 512], BF16, 167936)
    o = 190464
    x2t = [at("x2t%d" % i, [128, D], F32, o + 4096 * i) for i in range(2)]; o += 8192
    sgb = [at("sgb%d" % i, [128, 512], F32, o + 2048 * i) for i in range(2)]; o += 4096
    assert o <= CB + 4096

    ps = nc.alloc_psum_tensor("ps", [128, 4096], F32)

    def bank(k, n=1):
        return ps[:, 512 * k:512 * (k + n)]

    def bankbf(k, n=1):
        return ps[:, 512 * k:512 * (k + n)].bitcast(BF16)

    ev_setp = EV("setp")
    ev_setv = EV("setv")
    ev_setd = EV("setd")
    t = ev_setp.inc(POOL.memset(identf[:], 0.0))
    W("pool", t)
    t = ev_setp.inc(POOL.affine_select(out=identf[:], in_=identf[:], pattern=[[-1, 128]],
                                      compare_op=ALU.not_equal, fill=1.0, base=0, channel_multiplier=1))
    W("dve", t)
    tok_ident = ev_setv.inc(DVE.tensor_copy(out=ident[:], in_=identf[:]))
    tok_mhalf = ev_setp.inc(POOL.memset(mhalf[:], -0.5))
    ev_gv = EV("gv")
    tok_gcol = EV("gcol").inc(SP.dma_start(out=gcolT[:], in_=gmixT), dma=True)
    tok_gvec = ev_gv.inc(SP.dma_start(out=gvec[:], in_=gmix.partition_broadcast(128)), dma=True)
    DVE.memset(ssq[0][:], 1.0)
    DVE.memset(ssq[1][:], 1.0)
    late = {}

    def late_setup():
        for k_, src in enumerate([qna, kna, qnb, knb]):
            ev_setd.inc(SP.dma_start(out=gtmp[0:64, k_:k_ + 1], in_=src), dma=True)
            ev_setd.inc(SP.dma_start(out=gtmp[64:128, k_:k_ + 1], in_=src), dma=True)
        late["sinkld"] = ev_setd.inc(SP.dma_start(out=sinkexp[:], in_=sink.partition_broadcast(128)), dma=True)
        W("dve", late["sinkld"])
        DVE.scalar_tensor_tensor(out=qsA[:], in0=gtmp[:, 0:1], scalar=0.125, in1=gtmp[:, 1:2],
                                 op0=ALU.mult, op1=ALU.mult)
        late["qs"] = ev_setv.inc(DVE.scalar_tensor_tensor(out=qsB[:], in0=gtmp[:, 2:3], scalar=0.125,
                                                          in1=gtmp[:, 3:4], op0=ALU.mult, op1=ALU.mult))

    col_groups = {
        "qA": (0, 512), "kA": (512, 1024), "vA": (1024, 1536), "qB": (1536, 2048), "kvB": (2048, 2304),
        "g0": (2304, 2816), "g1": (2816, 3328), "g2": (3328, 3840), "g3": (3840, 4352),
    }
    w_in_v = w_in.rearrange("(c p) n -> p c n", p=128)
    tokW = {}

    def issue_w(names):
        for gname in names:
            a, b = col_groups[gname]
            tokW[gname] = EV("w_" + gname).inc(
                POOL.dma_start(out=Win[:, :, a:b], in_=w_in_v[:, :, a:b]), dma=True)

    issue_w(["kA", "vA", "kvB"])

    DVE.memset(VA[:, :, :, 64:65], 1.0)
    tok_vones = ev_setv.inc(DVE.memset(VB[:, :, :, 64:65], 1.0))
    ev_x = EV("xld")
    ev_a2 = EV("a2")
    ev_p = EV("pool")
    ev_a4 = EV("a4")
    ev_tx = EV("tx")
    ev_a6 = EV("a6")
    ev_g = EV("grp")
    ev_sq = EV("sq")
    ev_red = EV("red")
    ev_qn = EV("qn")
    ev_vc = EV("vcopy")
    ev_sg = EV("sig")
    ev_gst = EV("gst")
    ev_ttr = EV("ttr")
    ev_evd = EV("evd")
    ev_eva = EV("eva")

    tok_x = {}
    tok_a4 = {}
    tok_tx = {}
    tok_a6 = {}
    xt_free = [None, None]
    hb_free = [None, None]
    hT_free = [None, None]
    qn_free = [None, None]
    sq_free = [None, None]
    gsb_free = [None, None]
    bank_free = [None] * 5
    st = {"psT1_free": None, "psT2_free": None, "n": 0, "m": 0, "gm": 0}
    tok_qn_last = {}
    tok_lastgrp = {}
    PS_T1 = 0
    PS_G = 1
    PS_T2 = 6

    def is_own(e):
        return 2 <= e <= 17

    ORDER = [0, 1, 18, 19] + list(range(2, 18))
    POS = {e_: i_ for i_, e_ in enumerate(ORDER)}
    raw_free = [None, None]
    tok_rstd = {}
    tok_stat = {}
    NGB = 5

    xT_free = [None, None]
    tok_xT = {}
    DEFER = [ORDER[4], ORDER[5]]
    tok_hTk = {}
    OLD = set(ORDER[0:8])

    def A_load(e):
        b = POS[e] % 2
        W("sp", xt_free[b])
        tok_x[e] = EV('xld%d' % b).inc(SP.dma_start(out=xt[b][:], in_=xe[e * 128:(e + 1) * 128, :]), dma=True)
        if e in OLD:
            return
        W("sp", xT_free[b])
        tok_xT[e] = EV('xTld%d' % b).inc(SP.dma_start(
            out=xT[b][:], in_=xeT[e * 128:(e + 1) * 128, :].rearrange("p (c t) -> p c t", c=8)), dma=True)

    def A_stat(e):
        b = POS[e] % 2
        W("act", tok_x[e])
        W("act", st.get("junk_tok"))
        t2 = ev_a2.inc(ACT.activation(out=junk[:], in_=xt[b][:], func=AF.Square, accum_out=ssqx[:, e:e + 1]))
        st["junk_tok"] = t2
        xt_free[b] = t2
        W("pool", t2)
        W("pool", tok_mhalf)
        t3 = ev_p.inc(POOL.tensor_scalar(out=rsx[:, e:e + 1], in0=ssqx[:, e:e + 1], scalar1=1.0 / D, scalar2=EPS,
                                         op0=ALU.mult, op1=ALU.add))
        ev_p.inc(POOL.tensor_scalar(out=epsq[:, e:e + 1], in0=ssqx[:, e:e + 1], scalar1=EPS / D, scalar2=EPS * EPS,
                                    op0=ALU.mult, op1=ALU.add))
        W("pool", t3)
        tok_stat[e] = ev_p.inc(POOL.tensor_tensor(out=rsx[:, e:e + 1], in0=rsx[:, e:e + 1], in1=mhalf[:, 0:1],
                                                  op=ALU.pow))

    def A_scale(e):
        b = POS[e] % 2
        if e in OLD:
            W("dve", tok_x[e])
            W("dve", hb_free[b])
            W("dve", tok_gvec)
            tok_a4[e] = ev_a4.inc(DVE.tensor_tensor(out=hb[b][:], in0=xt[b][:], in1=gvec[:], op=ALU.mult))
            xt_free[b] = [xt_free[b], tok_a4[e]]
            return
        W("dve", tok_xT[e])
        W("dve", hT_free[b])
        W("dve", tok_gcol)
        tok_a4[e] = ev_a4.inc(DVE.tensor_tensor(out=hT[b][:].rearrange("p (c t) -> p c t", c=8), in0=xT[b][:],
                                                in1=gcolT[:].unsqueeze(2).to_broadcast([128, 8, 128]),
                                                op=ALU.mult))
        xT_free[b] = tok_a4[e]
        tok_a6[e] = tok_a4[e]

    def A_tx_old(e):
        b = POS[e] % 2
        if True:
            W("pe", tok_a4[e])
            W("pe", st["psT1_free"])
            W("pe", tok_ident)
            pT = bankbf(PS_T1)
            for c in range(8):
                ins = PE.transpose(out=pT[:, c * 128:(c + 1) * 128], in_=hb[b][:, c * 128:(c + 1) * 128],
                                   identity=ident[:])
            tok_tx[e] = ev_tx.inc(ins)
            hb_free[b] = tok_tx[e]
            xT_free[1] = [xT_free[1], tok_tx[e]]
            W("act", tok_tx[e])
            W("act", hT_free[b])
            tok_a6[e] = ev_a6.inc(ACT.activation(out=hT[b][:], in_=pT, func=AF.Copy))
            st["psT1_free"] = tok_a6[e]
            if e in DEFER:
                W("dve", tok_a6[e])
                W("dve", tok_ident)
                tok_hTk[e] = ev_a4.inc(DVE.tensor_copy(out=hTk[DEFER.index(e)][:], in_=hT[b][:]))

    NORM_OFF = {"qA": 0, "kA": 512, "qB": 1024, "kvB": 1536}
    NORM_C0 = {"qA": 0, "kA": 8, "qB": 16, "kvB": 24}

    def emit_rstd(e, tred_last):
        par = POS[e] % 2
        W("pool", tred_last)
        W("pool", tok_stat[e])
        tp = ev_p.inc(POOL.tensor_scalar(out=rstd[par][:, 0:26], in0=ssq[par][:, 0:26],
                                         scalar1=1.0 / 64, scalar2=epsq[:, e:e + 1], op0=ALU.mult, op1=ALU.add))
        W("pool", tp)
        tok_rstd[e] = ev_p.inc(POOL.tensor_tensor(out=rstd[par][:, 0:26], in0=rstd[par][:, 0:26],
                                                  in1=mhalf[:, 0:26], op=ALU.pow))

    def A_groups(e, mid_hook=None):
        b = POS[e] % 2
        par = POS[e] % 2
        own = is_own(e)
        last = (e == ORDER[NE - 1])
        if own and last:
            glist = ["qA", "kA", "qB", "kvB", "vA", "g0", "g1", "g2", "g3"]
        elif own and e in DEFER:
            glist = ["kA", "vA", "kvB", "qA", "qB"]
        elif own:
            glist = ["qA", "g0", "kA", "g1", "vA", "g2", "qB", "g3", "kvB"]
        else:
            glist = ["kA", "vA"] + (["kvB"] if 1 <= e <= 18 else [])
        tt = e - 2
        tg_last = None
        tred_last = None
        W("act", tok_stat[e])
        hook_at = min(2, len(glist) - 1)
        for gi, gname in enumerate(glist):
            a, bb = col_groups[gname]
            w = bb - a
            n = st["n"]; st["n"] += 1
            bk = n % NGB
            pb = bank(PS_G + bk)
            W("pe", bank_free[bk])
            W("pe", tok_a6[e])
            W("pe", tokW[gname])
            for c in range(8):
                ins = PE.matmul(pb[:, 0:w], lhsT=hT[b][:, c * 128:(c + 1) * 128], rhs=Win[:, c, a:bb],
                                start=(c == 0), stop=(c == 7))
            tg_ = ev_g.inc(ins)
            tg_last = tg_
            if gname in ("qA", "kA", "qB", "kvB"):
                m = st["m"]; st["m"] += 1
                s = m % 2
                nh, wn = (2, 128) if gname == "kvB" else (8, 512)
                c0 = NORM_C0[gname]
                q0 = NORM_OFF[gname]
                W("act", tg_)
                W("act", raw_free[par])
                ACT.activation(out=raw[par][:, q0:q0 + wn], in_=pb[:, 0:wn], func=AF.Copy)
                if gname == "kvB":
                    W("act", tok_vones)
                    ACT.activation(out=VB[:, e - 1, :, 0:64],
                                   in_=pb[:, 128:256].rearrange("p (h d) -> p h d", d=64), func=AF.Copy,
                                   scale=rsx[:, e:e + 1])
                W("act", sq_free[s])
                tsq = ev_sq.inc(ACT.activation(out=sq[s][:, 0:wn], in_=pb[:, 0:wn], func=AF.Square))
                bank_free[bk] = tsq
                W("dve", tsq)
                tred = ev_red.inc(DVE.tensor_reduce(out=ssq[par][:, c0:c0 + nh],
                                                    in_=sq[s][:, 0:wn].rearrange("p (h d) -> p h d", d=64),
                                                    axis=AX.X, op=ALU.add))
                sq_free[s] = tred
                tred_last = tred
            elif gname == "vA":
                W("act", tg_)
                W("act", tok_vones)
                tv = ev_sq.inc(ACT.activation(out=VA[:, e, :, 0:64],
                                              in_=pb[:, 0:512].rearrange("p (h d) -> p h d", d=64), func=AF.Copy,
                                              scale=rsx[:, e:e + 1]))
                bank_free[bk] = tv
            else:
                k = int(gname[1])
                gm = st["gm"]; st["gm"] += 1
                s = gm % 2
                W("act", tg_)
                W("act", gsb_free[s])
                tsg = ev_sq.inc(ACT.activation(out=gsb[s][:], in_=pb[:, 0:512], func=AF.Sigmoid,
                                               scale=rsx[:, e:e + 1]))
                bank_free[bk] = tsg
                W("sp", tsg)
                gsb_free[s] = EV('gst%d' % s).inc(SP.dma_start(out=gts[tt, :, k * 512:(k + 1) * 512], in_=gsb[s][:]), dma=True)
            if gi == hook_at and mid_hook is not None:
                mid_hook()
            if last and gi == 3:
                emit_rstd(e, tred_last)
            if last and gi == 6:
                A_normalize(e)
        hT_free[b] = [tg_last, tok_hTk.get(e)]
        tok_lastgrp[e] = tg_last
        if not last:
            emit_rstd(e, tred_last)

    def A_gates_deferred(e):
        src = hTk[DEFER.index(e)]
        tt = e - 2
        W("act", tok_stat[e])
        for k in range(4):
            gname = "g%d" % k
            a, bb = col_groups[gname]
            n = st["n"]; st["n"] += 1
            bk = n % NGB
            pb = bank(PS_G + bk)
            W("pe", bank_free[bk])
            W("pe", tok_hTk[e])
            W("pe", tokW[gname])
            for c in range(8):
                ins = PE.matmul(pb[:, 0:512], lhsT=src[:, c * 128:(c + 1) * 128], rhs=Win[:, c, a:bb],
                                start=(c == 0), stop=(c == 7))
            tg_ = ev_g.inc(ins)
            gm = st["gm"]; st["gm"] += 1
            s_ = gm % 2
            W("act", tg_)
            W("act", gsb_free[s_])
            tsg = ev_sq.inc(ACT.activation(out=gsb[s_][:], in_=pb[:, 0:512], func=AF.Sigmoid,
                                           scale=rsx[:, e:e + 1]))
            bank_free[bk] = tsg
            W("sp", tsg)
            gsb_free[s_] = EV('gst%d' % s_).inc(SP.dma_start(out=gts[tt, :, k * 512:(k + 1) * 512], in_=gsb[s_][:]), dma=True)

    def A_normalize(e):
        par = POS[e] % 2
        own = is_own(e)
        W("dve", tok_rstd[e])
        W("dve", qn_free[par])
        names = (["qA"] if own else []) + ["kA"] + (["qB"] if own else []) + (["kvB"] if 1 <= e <= 18 else [])
        tq = None
        for gname in names:
            q0 = NORM_OFF[gname]
            c0 = NORM_C0[gname]
            if gname == "qB":
                o_v = qn[par][:, 1024:1536].rearrange("p (r g d) -> p g r d", r=4, g=2, d=64)
                i_v = raw[par][:, 1024:1536].rearrange("p (g r d) -> p g r d", g=2, r=4, d=64)
                r_v = rstd[par][:, 16:24].rearrange("p (g r) -> p g r", g=2).unsqueeze(3).to_broadcast([128, 2, 4, 64])
            else:
                nh, wn = (2, 128) if gname == "kvB" else (8, 512)
                o_v = qn[par][:, q0:q0 + wn].rearrange("p (h d) -> p h d", d=64)
                i_v = raw[par][:, q0:q0 + wn].rearrange("p (h d) -> p h d", d=64)
                r_v = rstd[par][:, c0:c0 + nh].unsqueeze(2).to_broadcast([128, nh, 64])
            tq = ev_qn.inc(DVE.tensor_tensor(out=o_v, in0=i_v, in1=r_v, op=ALU.mult))
        tok_qn_last[e] = tq
        raw_free[par] = tq

    def A_ttr(e):
        par = POS[e] % 2
        own = is_own(e)
        tt = e - 2
        pT2 = bankbf(PS_T2, 2)
        W("pe", tok_qn_last[e])
        W("pe", st["psT2_free"])
        srcs = []
        if own:
            srcs += [(p, p * 128) for p in range(4)]
            srcs += [(4 + r, 1024 + r * 128) for r in range(4)]
        srcs += [(8 + p, 512 + p * 128) for p in range(4)]
        if 1 <= e <= 18:
            srcs += [(12, 1536)]
        for slot, c0 in srcs:
            ins = PE.transpose(out=pT2[:, slot * 128:(slot + 1) * 128], in_=qn[par][:, c0:c0 + 128], identity=ident[:])
        tt_ = ev_ttr.inc(ins)
        qn_free[par] = tt_
        frees = []
        if own:
            W("dve", tt_)
            W("dve", late["qs"])
            DVE.tensor_scalar(out=QAT[:, :, tt * 128:(tt + 1) * 128],
                              in0=pT2[:, 0:512].rearrange("p (a t) -> p a t", a=4),
                              scalar1=qsA[:, 0:1], scalar2=None, op0=ALU.mult)
            td = ev_evd.inc(DVE.tensor_scalar(out=QBT[:, :, tt * 128:(tt + 1) * 128],
                                              in0=pT2[:, 512:1024].rearrange("p (a t) -> p a t", a=4),
                                              scalar1=qsB[:, 0:1], scalar2=None, op0=ALU.mult))
            frees.append(td)
        W("act", tt_)
        ta = ev_eva.inc(ACT.activation(out=KAT[:, :, e * 128:(e + 1) * 128],
                                       in_=pT2[:, 1024:1536].rearrange("p (a t) -> p a t", a=4), func=AF.Copy))
        if 1 <= e <= 18:
            ta = ev_eva.inc(ACT.activation(out=KBT[:, (e - 1) * 128:e * 128], in_=pT2[:, 1536:1664], func=AF.Copy))
        frees.append(ta)
        st["psT2_free"] = frees

    A_load(ORDER[0])
    A_load(ORDER[1])
    A_stat(ORDER[0])
    issue_w(["qA", "qB", "g0", "g1", "g2", "g3"])
    A_scale(ORDER[0])
    A_stat(ORDER[1])
    A_scale(ORDER[1])
    A_load(ORDER[2])
    A_tx_old(ORDER[0])

    def mid_hook(i):
        if i >= 1:
            A_normalize(ORDER[i - 1])
        if i + 2 < NE and ORDER[i + 2] in OLD:
            A_scale(ORDER[i + 2])
        elif i + 1 < NE and ORDER[i + 1] not in OLD:
            A_scale(ORDER[i + 1])
    for i in range(NE + 1):
        if i == 1:
            late_setup()
        if i + 1 < NE and ORDER[i + 1] in OLD:
            A_tx_old(ORDER[i + 1])
        if i + 2 < NE:
            A_stat(ORDER[i + 2])
        if i < NE:
            A_groups(ORDER[i], (lambda j=i: mid_hook(j)))
        if i == 9:
            A_gates_deferred(DEFER[0])
        if i == 10:
            A_gates_deferred(DEFER[1])
        if i >= 1:
            A_ttr(ORDER[i - 1])
        if i + 3 < NE:
            A_load(ORDER[i + 3])

    p1_done = [st["psT2_free"], gsb_free[0], gsb_free[1], tok_lastgrp[ORDER[NE - 1]], tok_lastgrp[ORDER[NE - 2]]]

    W("pool", [tok_lastgrp[ORDER[NE - 1]], tok_lastgrp[ORDER[NE - 2]]])
    tok_tab1 = {}
    for k_ in range(5):
        tok_tab1[("A", k_)] = EV("tabA%d" % k_).inc(
            POOL.dma_start(out=tabAsb[:, k_, :], in_=tabA[:, k_ * 1024:(k_ + 1) * 1024]), dma=True)
    for k_ in range(3):
        tok_tab1[("B", k_)] = EV("tabB%d" % k_).inc(
            POOL.dma_start(out=tabBsb[:, k_, :], in_=tabB[:, k_ * 1024:(k_ + 1) * 1024]), dma=True)
    tok_tabA = tok_tab1[("A", 4)]
    tok_tabB = EV("tabB").inc(POOL.dma_start(out=tabBsb[:, 3:5, :].rearrange("p a b -> p (a b)"),
                                             in_=tabB[:, 3 * 1024:5 * 1024]), dma=True)
    W("pool", [tok_tab1[("B", 2)], tok_tabB])
    tok_tabE = EV("tabE").inc(POOL.dma_start(out=tabAsb[:, 5:15, :].rearrange("p a b -> p (a b)"),
                                             in_=tabA[:, 5 * 1024:15 * 1024]), dma=True)
    ev_wb = EV("wb")
    W("pool", tok_tabE)
    W("pool", p1_done)
    ev_wb.inc(POOL.dma_start(out=Wba[:], in_=w_ba.rearrange("(c p) n -> p c n", p=128)), dma=True)
    ev_wb.inc(POOL.dma_start(out=Wbb[:], in_=w_bb.rearrange("(c p) n -> p c n", p=128)), dma=True)
    tok_wb = ev_wb.inc(POOL.dma_start(out=Wout[:], in_=w_out.rearrange("(c p) n -> p c n", p=128)), dma=True)
    W("act", late["sinkld"])
    tok_sinkexp = EV("sinkexp").inc(ACT.activation(out=sinkexp[:], in_=sinkexp[:], func=AF.Exp))
    ev_te = EV("tabexp")
    tab_ready = {}

    def table_tok(kind, idx):
        key = (kind, idx)
        if key not in tab_ready:
            if kind == "A":
                W("act", tok_tab1[("A", idx)] if idx < 5 else tok_tabE)
                v = tabAsb[:, idx, :]
            else:
                W("act", tok_tab1[("B", idx)] if idx < 3 else tok_tabB)
                v = tabBsb[:, idx, :]
            tab_ready[key] = ev_te.inc(ACT.activation(out=v, in_=v, func=AF.Exp))
        return tab_ready[key]
    tabE = {"tok": None, "tokB": None}

    def exp_B_table():
        W("act", tok_tabB)
        tabE["tokB"] = ev_te.inc(ACT.activation(out=tabBsb[:], in_=tabBsb[:], func=AF.Exp))

    def exp_edge_tables():
        W("act", tok_tabE)
        for k in range(1, 3):
            v = tabAsb[:, 5 * k:5 * (k + 1), :]
            tabE["tok"] = ev_te.inc(ACT.activation(out=v, in_=v, func=AF.Exp))

    ev_S = EV("S")
    ev_add = EV("add")
    ev_exp = EV("exp")
    ev_pv = EV("pv")
    ev_na = EV("na")
    PS_S = [0, 2]
    PS_OA = 4
    PS_OB = 6
    psS_free = [None, None]
    PR_free = [None, None, None]
    PT_free = [None, None, None]
    O_free = {"A": None, "B": None}
    tok_norm = {}

    slots = []
    for t_ in list(range(2, 14)) + [0, 1, 14, 15]:
        for j in range(5):
            slots.append(("A", t_, j))
        for j in range(3):
            slots.append(("B", t_, j))

    def Oview(bk):
        return ps[:, 512 * bk:512 * (bk + 2)].rearrange("p (b c) -> p b c", b=2)[:, :, 0:260].rearrange(
            "p b (h d) -> p b h d", d=65)

    def emit_S(n):
        kind, t_, j = slots[n]
        pS = bank(PS_S[n % 2], 2)
        W("pe", psS_free[n % 2])
        if n == 0:
            W("pe", p1_done)
        for h in range(8):
            if kind == "A":
                e_ = t_ + j
                p_, hp = h // 2, (h % 2) * 64
                lhsT = KAT[hp:hp + 64, p_, e_ * 128:(e_ + 1) * 128]
                rhs = QAT[hp:hp + 64, p_, t_ * 128:(t_ + 1) * 128]
            else:
                e_ = t_ + 1 + j
                g_, r_ = h // 4, h % 4
                lhsT = KBT[g_ * 64:(g_ + 1) * 64, (e_ - 1) * 128:e_ * 128]
                rhs = QBT[g_ * 64:(g_ + 1) * 64, r_, t_ * 128:(t_ + 1) * 128]
            cb = _cbA(h) if kind == "A" else h
            ins = PE.matmul(pS[:, cb * 128:(cb + 1) * 128], lhsT=lhsT, rhs=rhs, start=True, stop=True)
        tS = ev_S.inc(ins)
        ttab = table_tok(kind, tabA_index(t_, j) if kind == "A" else tabB_index(t_, j))
        W("act", tS)
        W("act", PR_free[n % 3])
        tE = ev_exp.inc(ACT.activation(out=PR[n % 3][:], in_=pS, func=AF.Exp))
        psS_free[n % 2] = tE
        W("dve", tE)
        W("dve", PT_free[n % 3])
        if kind == "A":
            tb = tabAsb[:, tabA_index(t_, j), :]
            W("dve", ttab)
        else:
            tb = tabBsb[:, tabB_index(t_, j), :]
            W("dve", ttab)
        tA = ev_add.inc(DVE.tensor_tensor(out=PT[n % 3][:], in0=PR[n % 3][:], in1=tb, op=ALU.mult))
        PR_free[n % 3] = tA
        return tA

    def emit_PV(n, tE):
        kind, t_, j = slots[n]
        nslot = 5 if kind == "A" else 3
        bk = PS_OA if kind == "A" else PS_OB
        W("pe", tE)
        if j == 0:
            W("pe", O_free[kind])
        for h in range(8):
            if kind == "A":
                rhs = VA[:, t_ + j, h, :]
            else:
                rhs = VB[:, t_ + j, h // 4, :]
            o_ap = ps[:, 512 * (bk + h // 4) + (h % 4) * 65: 512 * (bk + h // 4) + (h % 4) * 65 + 65]
            cb = _cbA(h) if kind == "A" else h
            ins = PE.matmul(o_ap, lhsT=PT[n % 3][:, cb * 128:(cb + 1) * 128], rhs=rhs,
                            start=(j == 0 and h % 4 == 0), stop=(j == nslot - 1 and h % 4 == 3))
        tP = ev_pv.inc(ins)
        PT_free[n % 3] = tP
        if j == nslot - 1:
            ov = Oview(bk)
            W("dve", tP)
            if kind == "A":
                rd = rdenA
                t1 = ev_na.inc(DVE.reciprocal(out=rd[:].rearrange("p (b h o) -> p b h o", b=2, h=4, o=1),
                                              in_=ov[:, :, :, 64:65]))
            else:
                rd = rdenB
                W("dve", tok_sinkexp)
                t0 = ev_na.inc(DVE.tensor_tensor(out=rd[:].rearrange("p (b h o) -> p b h o", b=2, h=4, o=1),
                                                 in0=ov[:, :, :, 64:65],
                                                 in1=sinkexp[:].rearrange("p (b h o) -> p b h o", b=2, h=4, o=1),
                                                 op=ALU.add))
                W("dve", t0)
                t1 = ev_na.inc(DVE.reciprocal(out=rd[:], in_=rd[:]))
            W("dve", t1)
            c0 = 0 if kind == "A" else 512
            t2 = ev_na.inc(DVE.tensor_tensor(
                out=otok[:, t_, c0:c0 + 512].rearrange("p (b h d) -> p b h d", b=2, h=4),
                in0=ov[:, :, :, 0:64],
                in1=rd[:].rearrange("p (b h) -> p b h", b=2).unsqueeze(3).to_broadcast([128, 2, 4, 64]),
                op=ALU.mult))
            O_free[kind] = t2
            tok_norm[(kind, t_)] = t2

    toks = {}
    for n in range(len(slots)):
        toks[n] = emit_S(n)
        if n >= 2:
            emit_PV(n - 2, toks[n - 2])
    emit_PV(len(slots) - 2, toks[len(slots) - 2])
    emit_PV(len(slots) - 1, toks[len(slots) - 1])
    p2a_done = [tok_norm[("A", NT - 1)], tok_norm[("B", NT - 1)], PT_free[0], PT_free[1], PT_free[2]]

    ev_wg = EV("wg")
    w_gate_v = w_gate.rearrange("(c p) n -> p c n", p=128)
    w_up_v = w_up.rearrange("(c p) n -> p c n", p=128)
    wtok = {}

    def issue_wg_prefetch(k):
        W("pool", p2a_done)
        if k < 8:
            wtok["wg"] = ev_wg.inc(POOL.dma_start(out=Wg[:, k, :], in_=w_gate_v[:, k, :]), dma=True)
        elif k < 12:
            c0 = 2 * (k - 8)
            wtok["wu1"] = EV('wu1').inc(POOL.dma_start(out=Wu1[:, c0:c0 + 2, :],
                                                       in_=w_up_v[:, c0:c0 + 2, 0:WU_SPLIT]), dma=True)
    W("sp", p1_done)
    W("sp", tok_a4[ORDER[NE - 1]])
    tok_gvec2 = ev_gv.inc(SP.dma_start(out=gvec[:], in_=gffn.partition_broadcast(128)), dma=True)

    ev_gl = EV("gl")
    ev_xl = EV("xl2")
    ev_otr = EV("otr")
    ev_ev2 = EV("ev2")
    ev_y = EV("ymm")
    ev_cmb = EV("cmb")
    ev_ytr = EV("ytr")
    ev_x2mm = EV("x2mm")
    ev_res = EV("res")
    ev_st2 = EV("st2")
    ev_sq2 = EV("sq2")
    ev_h2 = EV("h2")
    ev_h2tr = EV("h2tr")
    PS_TB = [0, 1]
    PS_Y = [[2, 3], [4, 5]]
    PS_X = 6
    psT_free = [None, None]
    st2 = {"k": 0}
    gsh_free = [None, None]
    xt2_free = [None, None]
    ytok_free = [None, None]
    h2b_free = [None, None]
    oT_free = [None, None]
    yT_free = [None, None]
    usb_free = [None, None]
    psY_free = [None, None]
    psX_free = [None]
    tok_oT = {}
    tok_cmb = {}
    tok_yT = {}
    tok_h2 = {}
    tok_h2T = {}
    tok_x2st = {}
    tok_res = {}
    tok_gl = {}
    tok_xl = {}

    def B_loads(t_):
        b = t_ % 2
        W("sp", xt2_free[b])
        if t_ < 2:
            W("sp", p2a_done)
        tok_xl[t_] = EV('xl2_%d' % b).inc(SP.dma_start(out=xt2[b][:], in_=xe[(t_ + 2) * 128:(t_ + 3) * 128, :]), dma=True)

    def B_gload(t_, c):
        W("sp", gsh_free[c])
        if t_ == 0:
            W("sp", p2a_done)
        tok_gl[(t_, c)] = EV('gl%d' % c).inc(SP.dma_start(
            out=gsh[c][:], in_=gts[t_].rearrange("p (g n) -> p g n", g=2)[:, :, c * 512:(c + 1) * 512]), dma=True)

    def tr8(src, dst_free_tok, extra_wait):
        k = st2["k"]; st2["k"] += 1
        pT = bankbf(PS_TB[k % 2])
        W("pe", psT_free[k % 2])
        W("pe", extra_wait)
        for c in range(8):
            ins = PE.transpose(out=pT[:, c * 128:(c + 1) * 128], in_=src[:, c * 128:(c + 1) * 128], identity=ident[:])
        return k % 2, pT, ins

    def B_otr(t_):
        b = t_ % 2
        kk, pT, ins = tr8(otok[:, t_, :], None, p2a_done if t_ == 0 else None)
        tt_ = ev_otr.inc(ins)
        W("act", tt_)
        W("act", oT_free[b])
        ta_ = ev_ev2.inc(ACT.activation(out=oTb[b][:, 0:512], in_=pT[:, 0:512], func=AF.Copy))
        tb_ = ev_ev2.inc(ACT.activation(out=oTb[b][:, 512:1024], in_=pT[:, 512:1024], func=AF.Copy))
        tok_oT[t_] = (ta_, tb_)
        psT_free[kk] = tb_

    def B_ymm(t_):
        b = t_ % 2
        W("pe", tok_oT[t_][0])
        W("pe", tok_wb)
        for c in range(2):
            W("pe", psY_free[c])
            pa = bank(PS_Y[c][0])
            pbk = bank(PS_Y[c][1])
            for k in range(4):
                PE.matmul(pa, lhsT=oTb[b][:, k * 128:(k + 1) * 128], rhs=Wba[:, k, c * 512:(c + 1) * 512],
                          start=(k == 0), stop=(k == 3))
            W("pe", tok_oT[t_][1])
            for k in range(4):
                ins = PE.matmul(pbk, lhsT=oTb[b][:, (4 + k) * 128:(5 + k) * 128], rhs=Wbb[:, k, c * 512:(c + 1) * 512],
                                start=(k == 0), stop=(k == 3))
            ty = ev_y.inc(ins)
            if c == 1:
                oT_free[b] = ty
            W("dve", ty)
            W("dve", tok_gl[(t_, c)])
            W("dve", usb_free[c])
            t1 = ev_cmb.inc(DVE.tensor_tensor(out=usb[c][:], in0=pa, in1=gsh[c][:, 0, :], op=ALU.mult))
            t2 = ev_cmb.inc(DVE.tensor_tensor(out=pbk, in0=pbk, in1=gsh[c][:, 1, :], op=ALU.mult))
            gsh_free[c] = t2
            W("dve", t2)
            W("dve", ytok_free[b])
            t3 = ev_cmb.inc(DVE.tensor_tensor(out=ytok[b][:, c * 512:(c + 1) * 512], in0=pbk, in1=usb[c][:], op=ALU.add))
            usb_free[c] = t3
            psY_free[c] = t3
            tok_cmb[t_] = t3
            if t_ + 1 < NT:
                B_gload(t_ + 1, c)

    def B_ytr(t_):
        b = t_ % 2
        kk, pT, ins = tr8(ytok[b], None, tok_cmb[t_])
        tt_ = ev_ytr.inc(ins)
        ytok_free[b] = tt_
        W("act", tt_)
        W("act", yT_free[b])
        ta_ = ev_ev2.inc(ACT.activation(out=yTb[b][:, 0:512], in_=pT[:, 0:512], func=AF.Copy))
        tb_ = ev_ev2.inc(ACT.activation(out=yTb[b][:, 512:1024], in_=pT[:, 512:1024], func=AF.Copy))
        tok_yT[t_] = (ta_, tb_)
        psT_free[kk] = tb_

    def B_x2mm(t_):
        b = t_ % 2
        W("pe", tok_yT[t_][0])
        W("pe", psX_free[0])
        pX = bank(PS_X, 2)
        for c in range(2):
            for k in range(8):
                if k == 4:
                    W("pe", tok_yT[t_][1])
                ins = PE.matmul(pX[:, c * 512:(c + 1) * 512], lhsT=yTb[b][:, k * 128:(k + 1) * 128],
                                rhs=Wout[:, k, c * 512:(c + 1) * 512], start=(k == 0), stop=(k == 7))
        tx_ = ev_x2mm.inc(ins)
        yT_free[b] = tx_
        W("dve", tx_)
        W("dve", tok_xl[t_])
        tr_ = ev_res.inc(DVE.tensor_tensor(out=xt2[b][:], in0=pX, in1=xt2[b][:], op=ALU.add))
        psX_free[0] = tr_
        W("sp", tr_)
        tok_x2st[t_] = EV('st2_%d' % b).inc(SP.dma_start(out=out[t_ * 128:(t_ + 1) * 128, :], in_=xt2[b][:]), dma=True)
        tok_res[t_] = tr_

    def B_norm(t_):
        b = t_ % 2
        tr_ = tok_res[t_]
        W("act", tr_)
        ts_ = ev_sq2.inc(ACT.activation(out=junk2[:], in_=xt2[b][:], func=AF.Square, accum_out=ssq2[:, t_:t_ + 1]))
        W("pool", ts_)
        tp = ev_p.inc(POOL.tensor_scalar(out=rs2[:, t_:t_ + 1], in0=ssq2[:, t_:t_ + 1], scalar1=1.0 / D, scalar2=EPS,
                                         op0=ALU.mult, op1=ALU.add))
        W("pool", tp)
        tp = ev_p.inc(POOL.tensor_tensor(out=rs2[:, t_:t_ + 1], in0=rs2[:, t_:t_ + 1], in1=mhalf[:, 0:1], op=ALU.pow))
        issue_wg_prefetch(t_)
        W("dve", tp)
        W("dve", h2b_free[b])
        W("dve", tok_gvec2)
        tok_h2[t_] = ev_h2.inc(DVE.scalar_tensor_tensor(out=h2b[b][:], in0=xt2[b][:], scalar=rs2[:, t_:t_ + 1],
                                                        in1=gvec[:], op0=ALU.mult, op1=ALU.mult))
        xt2_free[b] = [tok_x2st[t_], tok_h2[t_]]
        if t_ + 2 < NT:
            B_loads(t_ + 2)

    def B_h2tr(t_):
        b = t_ % 2
        kk, pT, ins = tr8(h2b[b], None, tok_h2[t_])
        tt_ = ev_h2tr.inc(ins)
        h2b_free[b] = tt_
        W("act", tt_)
        tok_h2T[t_] = ev_ev2.inc(ACT.activation(out=h2T[:, :, t_ * 128:(t_ + 1) * 128],
                                                in_=pT.rearrange("p (a t) -> p a t", a=8), func=AF.Copy))
        psT_free[kk] = tok_h2T[t_]

    B_loads(0)
    B_loads(1)
    B_gload(0, 0)
    B_gload(0, 1)
    B_otr(0)
    B_ymm(0)
    for i in range(NT):
        if i + 1 < NT:
            B_otr(i + 1)
        if i >= 1:
            B_norm(i - 1)
        B_ytr(i)
        if i + 1 < NT:
            B_ymm(i + 1)
        if i == NT - 1:
            B_x2mm(i)
            B_h2tr(i - 1)
        else:
            if i >= 1:
                B_h2tr(i - 1)
            B_x2mm(i)
    tok_wg = wtok["wg"]
    tok_wu1 = wtok["wu1"]
    ev_gu = EV("gu")
    ev_silu = EV("silu")
    ev_u = EV("umul")
    PS_GU = [[2, 3], [4, 5]]
    psGU_free = [psY_free[0], psY_free[1]]
    sgb_free = [[psY_free[0], psY_free[1]], [psY_free[0], psY_free[1]]]
    p3 = {"uT_free": None, "tok_wu2": None}
    EARLY_GU = 4

    def emit_GU(u, j, tok_uT):
        s_ = j % 2
        pG = bank(PS_GU[s_][0])
        pU = bank(PS_GU[s_][1])
        W("pe", psGU_free[s_])
        W("pe", tok_wg)
        for c in range(8):
            PE.matmul(pG, lhsT=Wg[:, c, j * 128:(j + 1) * 128], rhs=h2T[:, c, u * 512:(u + 1) * 512],
                      start=(c == 0), stop=(c == 7))
        W("pe", tok_wu1 if j < 11 else p3["tok_wu2"])
        Wuj, jj = (Wu1, j) if j < 11 else (Wu2, j - 11)
        for c in range(8):
            ins = PE.matmul(pU, lhsT=Wuj[:, c, jj * 128:(jj + 1) * 128], rhs=h2T[:, c, u * 512:(u + 1) * 512],
                            start=(c == 0), stop=(c == 7))
        tgu = ev_gu.inc(ins)
        W("act", tgu)
        W("act", sgb_free[s_])
        tsl = ev_silu.inc(ACT.activation(out=sgb[s_][:], in_=pG, func=AF.Silu))
        W("dve", tsl)
        if j == 0:
            W("dve", p3["uT_free"])
        tu = ev_u.inc(DVE.tensor_tensor(out=uT[:, j, :], in0=pU, in1=sgb[s_][:], op=ALU.mult))
        sgb_free[s_] = tu
        psGU_free[s_] = tu
        tok_uT[j] = tu

    tok_uT0 = {}
    B_norm(NT - 1)
    for j in range(EARLY_GU):
        emit_GU(0, j, tok_uT0)
    B_h2tr(NT - 1)
    p2b_done = [tok_h2T[NT - 1], tok_h2T[NT - 2], tok_x2st[NT - 1], tok_x2st[NT - 2], yT_free[0], yT_free[1]]

    W("pool", p2b_done)
    tok_wu2 = EV('wu2').inc(POOL.dma_start(out=Wu2[:], in_=w_up_v[:, :, WU_SPLIT:FF]), dma=True)
    ev_wd = EV("wd")
    w_down_v = w_down.rearrange("(c p) n -> p c n", p=128)
    tok_wd = {}
    for k0 in range(0, NFC, 6):
        k1 = min(NFC, k0 + 6)
        tkn = EV('wd%d' % k0).inc(POOL.dma_start(out=Wd[:, k0:k1, :], in_=w_down_v[:, k0:k1, :]), dma=True)
        for k in range(k0, k1):
            tok_wd[k] = tkn

    ev_d = EV("dmm")
    ev_fin = EV("fin")
    PS_D = [0, 6]
    psD_free = [p2b_done, p2b_done]
    x2t_free = [None, None]
    tok_x2l = {}
    tok_ost = {}
    p3["tok_wu2"] = tok_wu2

    def C_x2load(tile):
        b = tile % 2
        W("sp", x2t_free[b])
        W("sp", p2b_done)
        W("sp", tok_x2st[tile])
        tok_x2l[tile] = EV('x2l%d' % b).inc(SP.dma_start(out=x2t[b][:], in_=out[tile * 128:(tile + 1) * 128, :]), dma=True)

    C_x2load(0)
    C_x2load(1)
    for u in range(4):
        tok_uT = tok_uT0 if u == 0 else {}
        for j in range(EARLY_GU if u == 0 else 0, NFC):
            emit_GU(u, j, tok_uT)
        for i in range(4):
            tile = 4 * u + i
            b = tile % 2
            pD = bank(PS_D[i % 2], 2)
            W("pe", psD_free[i % 2])
            for c in range(2):
                for j in range(NFC):
                    W("pe", tok_uT[j])
                    W("pe", tok_wd[j])
                    ins = PE.matmul(pD[:, c * 512:(c + 1) * 512], lhsT=uT[:, j, i * 128:(i + 1) * 128],
                                    rhs=Wd[:, j, c * 512:(c + 1) * 512], start=(j == 0), stop=(j == NFC - 1))
            td = ev_d.inc(ins)
            if i == 3:
                p3["uT_free"] = td
            W("dve", td)
            W("dve", tok_x2l[tile])
            tf = ev_fin.inc(DVE.tensor_tensor(out=x2t[b][:], in0=pD, in1=x2t[b][:], op=ALU.add))
            psD_free[i % 2] = tf
            W("sp", tf)
            tok_ost[tile] = EV('ost%d' % b).inc(SP.dma_start(out=out[tile * 128:(tile + 1) * 128, :], in_=x2t[b][:]), dma=True)
            x2t_free[b] = tok_ost[tile]
            if tile + 2 < NT:
                C_x2load(tile + 2)
    W("sp", tok_ost[NT - 1])
    W("sp", tok_ost[NT - 2])
    return nc


_CACHE = {}


def _host_inputs(x, norm_mix, w_in, q_norm_a, k_norm_a, rpb_a, q_norm_b, k_norm_b, sink_b, t5_table,
                 w_branch_a, w_branch_b, w_out, norm_ffn, w_gate, w_up, w_down):
    f = lambda a: np.ascontiguousarray(np.asarray(a, dtype=np.float32))
    x = f(x)
    shared = {
        "w_in": f(w_in[0]), "gmix": f(norm_mix[0]).reshape(1, D),
        "gmixT": np.ascontiguousarray(f(norm_mix[0]).reshape(8, 128).T), "gffn": f(norm_ffn[0]).reshape(1, D),
        "qna": f(q_norm_a[0]).reshape(64, 1), "kna": f(k_norm_a[0]).reshape(64, 1),
        "qnb": f(q_norm_b[0]).reshape(64, 1), "knb": f(k_norm_b[0]).reshape(64, 1),
        "sink": f(sink_b[0]).reshape(1, 8),
        "w_ba": f(w_branch_a[0]), "w_bb": f(w_branch_b[0]), "w_out": f(w_out[0]),
        "w_gate": f(w_gate[0]), "w_up": f(w_up[0]), "w_down": f(w_down[0]),
    }
    rpb = f(rpb_a[0])
    t5 = f(t5_table)
    tabAs = [_build_tabA(rpb, s) for s in range(4)]
    tabBs = [_build_tabB(t5, s) for s in range(4)]
    in_maps = []
    for c in range(NCORES):
        b, s = c // 4, c % 4
        xe = np.zeros((NE * 128, D), dtype=np.float32)
        for e in range(NE):
            r0 = _ext_rows(s, e)
            if r0 is None:
                continue
            xe[e * 128:(e + 1) * 128] = x[b, r0 * 64:r0 * 64 + 128]
        m = dict(shared)
        m["xe"] = xe
        m["xeT"] = np.ascontiguousarray(xe.reshape(NE, 128, 8, 128).transpose(0, 3, 2, 1).reshape(NE * 128, D))
        m["tabA"] = tabAs[s]
        m["tabB"] = tabBs[s]
        in_maps.append(m)
    return in_maps


def kernel(**inputs):
    if "nc" not in _CACHE:
        _CACHE["nc"] = build_program()
    nc = _CACHE["nc"]
    in_maps = _host_inputs(**inputs)
    res = run_bass_kernel_spmd(nc, in_maps, core_ids=list(range(NCORES)))
    outp = np.empty((2, T, D), dtype=np.float32)
    for c in range(NCORES):
        b, s = c // 4, c % 4
        outp[b, s * TOK:(s + 1) * TOK] = res.results[c]["out"]
    return outp
```
